# Optimizing a Trainium2 kernel written in Bass

```python
import math
import jax
import jax.numpy as jnp
from jax import lax
import numpy as np


D_MODEL = 1024
BATCH = 8
SEQ = 2048
DEPTH = 1
DEC_BATCH = 128
DEC_SEQ = 1
PAST_LEN = 8192
PAGE_SIZE = 128

N_HEADS = 16
KV_HEADS = 4
HEAD_DIM = D_MODEL // N_HEADS
Q_PER_KV = N_HEADS // KV_HEADS
WINDOW = 128
ATTN_BLOCK = WINDOW
ROPE_THETA = 10000.0
Q_DIM = N_HEADS * HEAD_DIM
KV_DIM = KV_HEADS * HEAD_DIM
SSM_EXPAND = 2
D_INNER = SSM_EXPAND * D_MODEL
SSM_HEAD_DIM = 64
SSM_HEADS = D_INNER // SSM_HEAD_DIM
SSM_GROUPS = 4
HEADS_PER_GROUP = SSM_HEADS // SSM_GROUPS
D_STATE = 128
CONV_W = 4
CONV_DIM = D_INNER + 2 * SSM_GROUPS * D_STATE
SSD_CHUNK = 128
D_FF = 4 * D_MODEL
EPS = 1e-6
IN_SPLITS = (Q_DIM, Q_DIM + KV_DIM, Q_DIM + 2 * KV_DIM, Q_DIM + 2 * KV_DIM + D_INNER,
             Q_DIM + 2 * KV_DIM + D_INNER + CONV_DIM, Q_DIM + 2 * KV_DIM + D_INNER + CONV_DIM + SSM_HEADS)
IN_DIM = Q_DIM + 2 * KV_DIM + D_INNER + CONV_DIM + SSM_HEADS + 2 * D_MODEL

kernel_name = 'hybrid_swa_sink_ssd_decoder_step'


def rms_norm(x, g):
    xf = x.astype(jnp.float32)
    y = xf * lax.rsqrt(jnp.mean(xf * xf, axis=-1, keepdims=True) + EPS)
    return (y * g.astype(jnp.float32)).astype(x.dtype)


def rope(x, pos):
    half = HEAD_DIM // 2
    inv = ROPE_THETA ** (-jnp.arange(half, dtype=jnp.float32) / half)
    ang = pos.astype(jnp.float32)[:, None] * inv[None, :]
    cos = jnp.cos(ang)[None, :, None, :]
    sin = jnp.sin(ang)[None, :, None, :]
    xf = x.astype(jnp.float32)
    x1, x2 = xf[..., :half], xf[..., half:]
    return jnp.concatenate([x1 * cos - x2 * sin, x2 * cos + x1 * sin], axis=-1).astype(x.dtype)


def sink_attention(q, k, v, mask, sinks):
    s = jnp.einsum('...qkgd,...skd->...kgqs', q.astype(jnp.float32), k.astype(jnp.float32)) * (HEAD_DIM ** -0.5)
    s = jnp.where(mask[..., None, None, :, :], s, -jnp.inf)
    sink = jnp.broadcast_to(sinks.astype(jnp.float32).reshape(KV_HEADS, Q_PER_KV, 1, 1), s.shape[:-1] + (1,))
    p = jax.nn.softmax(jnp.concatenate([s, sink], axis=-1), axis=-1)[..., :-1]
    o = jnp.einsum('...kgqs,...skd->...qkgd', p, v.astype(jnp.float32))
    return o.astype(q.dtype)


def window_attention_prompt(q, k, v, sinks):
    b, t = q.shape[:2]
    nb = t // ATTN_BLOCK
    qb = q.reshape(b, nb, ATTN_BLOCK, KV_HEADS, Q_PER_KV, HEAD_DIM)
    pad = jnp.zeros((b, ATTN_BLOCK, KV_HEADS, HEAD_DIM), k.dtype)

    def band(a):
        prev = jnp.concatenate([pad, a], axis=1)[:, :t].reshape(b, nb, ATTN_BLOCK, KV_HEADS, HEAD_DIM)
        cur = a.reshape(b, nb, ATTN_BLOCK, KV_HEADS, HEAD_DIM)
        return jnp.concatenate([prev, cur], axis=2)

    kb, vb = band(k), band(v)
    blk = jnp.arange(nb)[:, None, None]
    qpos = blk * ATTN_BLOCK + jnp.arange(ATTN_BLOCK)[None, :, None]
    kpos = (blk - 1) * ATTN_BLOCK + jnp.arange(2 * ATTN_BLOCK)[None, None, :]
    mask = (kpos <= qpos) & (kpos > qpos - WINDOW) & (kpos >= 0)
    o = sink_attention(qb, kb, vb, mask, sinks)
    return o.reshape(b, t, Q_DIM)


def window_attention_decode(q, k, v, win_k, win_v, pos, sinks):
    b, t = q.shape[:2]
    w = win_k.shape[1]
    kk = jnp.concatenate([win_k, k], axis=1)
    vv = jnp.concatenate([win_v, v], axis=1)
    kpos = jnp.concatenate([pos[:1] - w + jnp.arange(w), pos])
    mask = (kpos[None, :] <= pos[:, None]) & (kpos[None, :] > pos[:, None] - WINDOW)
    qg = q.reshape(b, t, KV_HEADS, Q_PER_KV, HEAD_DIM)
    o = sink_attention(qg, kk, vv, mask, sinks)
    return o.reshape(b, t, Q_DIM), kk[:, -w:], vv[:, -w:]


def causal_conv(xbc, prev, w, bias):
    xp = jnp.concatenate([prev.astype(xbc.dtype), xbc], axis=1)
    y = lax.conv_general_dilated(xp, w[:, None, :].astype(xbc.dtype), window_strides=(1,), padding='VALID',
                                 dimension_numbers=('NWC', 'WIO', 'NWC'), feature_group_count=CONV_DIM)
    return jax.nn.silu(y + bias), xp[:, -(CONV_W - 1):]


def ssd_scan(x, dt, A, Bm, Cm, h0):
    b, T = x.shape[:2]
    L = SSD_CHUNK if T % SSD_CHUNK == 0 else T
    nc = T // L
    G, Hg, P, N = SSM_GROUPS, HEADS_PER_GROUP, SSM_HEAD_DIM, D_STATE
    x = x.reshape(b, nc, L, G, Hg, P)
    dt = dt.reshape(b, nc, L, G, Hg)
    Bm = Bm.reshape(b, nc, L, G, N)
    Cm = Cm.reshape(b, nc, L, G, N)
    cum = jnp.cumsum(dt * A.reshape(G, Hg), axis=2)
    diff = cum[:, :, :, None] - cum[:, :, None, :]
    causal = jnp.tril(jnp.ones((L, L), dtype=bool))[:, :, None, None]
    decay = jnp.exp(jnp.where(causal, diff, -jnp.inf))
    cb = jnp.einsum('bclgn,bcsgn->bclsg', Cm, Bm)
    w_intra = cb[..., None] * decay * dt[:, :, None]
    y = jnp.einsum('bclsgh,bcsghp->bclghp', w_intra, x)
    to_end = jnp.exp(cum[:, :, -1:] - cum) * dt
    st = jnp.einsum('bcsgn,bcsgh,bcsghp->bcghpn', Bm, to_end, x)
    chunk_decay = jnp.exp(cum[:, :, -1])

    def step(h, inp):
        s_c, d_c = inp
        return d_c[..., None, None] * h + s_c, h

    h_fin, h_enter = lax.scan(step, h0.reshape(b, G, Hg, P, N),
                              (jnp.moveaxis(st, 1, 0), jnp.moveaxis(chunk_decay, 1, 0)))
    h_enter = jnp.moveaxis(h_enter, 0, 1)
    y = y + jnp.einsum('bclgn,bclgh,bcghpn->bclghp', Cm, jnp.exp(cum), h_enter)
    return y.reshape(b, T, SSM_HEADS, P), h_fin.reshape(b, SSM_HEADS, P, N)


def hybrid_layer(x, pos, conv_prev, ssm_h0, win_k, win_v, w_buf,
                 norm_mix, w_in, q_norm, k_norm, attn_sinks, conv_w, conv_b, dt_bias, a_log, d_skip,
                 ssm_norm, w_attn_o, w_ssm_o, w_out, norm_mlp, w_up, w_down):
    b, t, _ = x.shape
    h = rms_norm(x, norm_mix)
    q, k, v, z, xbc, dt_raw, gates = jnp.split(h @ w_in, IN_SPLITS, axis=-1)
    q = rope(rms_norm(q.reshape(b, t, N_HEADS, HEAD_DIM), q_norm), pos)
    k = rope(rms_norm(k.reshape(b, t, KV_HEADS, HEAD_DIM), k_norm), pos)
    v = v.reshape(b, t, KV_HEADS, HEAD_DIM)
    if win_k is None:
        o_attn = window_attention_prompt(q, k, v, attn_sinks)
        new_k, new_v = k[:, -w_buf:], v[:, -w_buf:]
    else:
        o_attn, new_k, new_v = window_attention_decode(q, k, v, win_k, win_v, pos, attn_sinks)
    xc, conv_state = causal_conv(xbc, conv_prev, conv_w, conv_b)
    xs, Bm, Cm = jnp.split(xc, [D_INNER, D_INNER + SSM_GROUPS * D_STATE], axis=-1)
    dt = jax.nn.softplus(dt_raw.astype(jnp.float32) + dt_bias.astype(jnp.float32))
    A = -jnp.exp(a_log.astype(jnp.float32))
    xs_h = xs.reshape(b, t, SSM_HEADS, SSM_HEAD_DIM).astype(jnp.float32)
    y, h_fin = ssd_scan(xs_h, dt, A,
                        Bm.reshape(b, t, SSM_GROUPS, D_STATE).astype(jnp.float32),
                        Cm.reshape(b, t, SSM_GROUPS, D_STATE).astype(jnp.float32),
                        ssm_h0.astype(jnp.float32))
    y = (y + d_skip.astype(jnp.float32)[:, None] * xs_h).reshape(b, t, D_INNER)
    o_ssm = rms_norm(y * jax.nn.silu(z.astype(jnp.float32)), ssm_norm).astype(x.dtype)
    g_attn, g_ssm = jnp.split(jax.nn.sigmoid(gates), 2, axis=-1)
    mixed = g_attn * (o_attn @ w_attn_o) + g_ssm * (o_ssm @ w_ssm_o)
    x = x + mixed @ w_out
    u = rms_norm(x, norm_mlp) @ w_up
    x = x + jnp.square(jax.nn.relu(u)) @ w_down
    return x, new_k, new_v, conv_state, h_fin.astype(x.dtype)


def setup_inputs(seed: int = 0) -> dict:
    key = jax.random.key(seed)
    ks = jax.random.split(key, 32)
    w_buf = min(WINDOW, PAST_LEN)

    def nrm(k, shape, scale):
        return jax.random.normal(k, shape, jnp.float32) * scale

    def gain(k, n):
        return 1.0 + nrm(k, (DEPTH, n), 0.02)

    dt0 = jnp.exp(jax.random.uniform(ks[20], (DEPTH, SSM_HEADS)) * (math.log(0.1) - math.log(0.001)) + math.log(0.001))
    return {
        'x_prompt': nrm(ks[0], (BATCH, SEQ, D_MODEL), 1.0),
        'x_sample': nrm(ks[1], (DEC_BATCH, DEC_SEQ, D_MODEL), 1.0),
        'cache_k': nrm(ks[2], (DEPTH, DEC_BATCH, w_buf, KV_HEADS, HEAD_DIM), 1.0),
        'cache_v': nrm(ks[3], (DEPTH, DEC_BATCH, w_buf, KV_HEADS, HEAD_DIM), 1.0),
        'state_conv': nrm(ks[4], (DEPTH, DEC_BATCH, CONV_W - 1, CONV_DIM), 1.0),
        'state_ssm': nrm(ks[5], (DEPTH, DEC_BATCH, SSM_HEADS, SSM_HEAD_DIM, D_STATE), 0.1),
        'norm_mix': gain(ks[6], D_MODEL),
        'w_in': nrm(ks[7], (DEPTH, D_MODEL, IN_DIM), D_MODEL ** -0.5),
        'q_norm': gain(ks[8], HEAD_DIM),
        'k_norm': gain(ks[9], HEAD_DIM),
        'attn_sinks': nrm(ks[10], (DEPTH, N_HEADS), 0.5),
        'conv_w': nrm(ks[11], (DEPTH, CONV_W, CONV_DIM), 0.5),
        'conv_b': nrm(ks[12], (DEPTH, CONV_DIM), 0.02),
        'dt_bias': dt0 + jnp.log(-jnp.expm1(-dt0)),
        'a_log': jnp.log(jax.random.uniform(ks[13], (DEPTH, SSM_HEADS), minval=1.0, maxval=16.0)),
        'd_skip': 1.0 + nrm(ks[14], (DEPTH, SSM_HEADS), 0.02),
        'ssm_norm': gain(ks[15], D_INNER),
        'w_attn_o': nrm(ks[16], (DEPTH, Q_DIM, D_MODEL), Q_DIM ** -0.5),
        'w_ssm_o': nrm(ks[17], (DEPTH, D_INNER, D_MODEL), D_INNER ** -0.5),
        'w_out': nrm(ks[18], (DEPTH, D_MODEL, D_MODEL), D_MODEL ** -0.5),
        'norm_mlp': gain(ks[19], D_MODEL),
        'w_up': nrm(ks[21], (DEPTH, D_MODEL, D_FF), D_MODEL ** -0.5),
        'w_down': nrm(ks[22], (DEPTH, D_FF, D_MODEL), D_FF ** -0.5),
    }


def reference(x_prompt, x_sample, cache_k, cache_v, state_conv, state_ssm, norm_mix, w_in, q_norm, k_norm,
              attn_sinks, conv_w, conv_b, dt_bias, a_log, d_skip, ssm_norm, w_attn_o, w_ssm_o, w_out,
              norm_mlp, w_up, w_down):
    b_p, seq = x_prompt.shape[:2]
    dec_seq = x_sample.shape[1]
    w_buf = cache_k.shape[2]
    pos_p = jnp.arange(seq)
    pos_s = PAST_LEN + jnp.arange(dec_seq)
    yp, ys = x_prompt, x_sample
    kp_l, vp_l, cp_l, hp_l, ks_l, vs_l, cs_l, hs_l = [], [], [], [], [], [], [], []
    for l in range(DEPTH):
        lw = (norm_mix[l], w_in[l], q_norm[l], k_norm[l], attn_sinks[l], conv_w[l], conv_b[l], dt_bias[l],
              a_log[l], d_skip[l], ssm_norm[l], w_attn_o[l], w_ssm_o[l], w_out[l], norm_mlp[l], w_up[l], w_down[l])
        conv0 = jnp.zeros((b_p, CONV_W - 1, CONV_DIM), x_prompt.dtype)
        ssm0 = jnp.zeros((b_p, SSM_HEADS, SSM_HEAD_DIM, D_STATE), x_prompt.dtype)
        yp, kp, vp, cp, hp = hybrid_layer(yp, pos_p, conv0, ssm0, None, None, w_buf, *lw)
        ys, k_s, v_s, c_s, h_s = hybrid_layer(ys, pos_s, state_conv[l], state_ssm[l], cache_k[l], cache_v[l],
                                              w_buf, *lw)
        kp_l.append(kp); vp_l.append(vp); cp_l.append(cp); hp_l.append(hp)
        ks_l.append(k_s); vs_l.append(v_s); cs_l.append(c_s); hs_l.append(h_s)
    return (yp, ys, jnp.stack(kp_l), jnp.stack(vp_l), jnp.stack(cp_l), jnp.stack(hp_l),
            jnp.stack(ks_l), jnp.stack(vs_l), jnp.stack(cs_l), jnp.stack(hs_l))
```

```python
import contextlib
import numpy as np
import concourse.bass as bass
import concourse.mybir as mybir
from concourse.bass_utils import run_bass_kernel_spmd

F32 = mybir.dt.float32
BF16 = mybir.dt.bfloat16
F32R = mybir.dt.float32r
AF = mybir.ActivationFunctionType
ALU = mybir.AluOpType
AX = mybir.AxisListType

NCORES = 8
D = 1024
T = 2048
NSMP = 16
NS = 2
TT = NS * 128
NTILE = T // TT
PAST = 8192
EPS = 1e-6
NEG = -1.0e5
IN_DIM = 8736
C_Q, C_K, C_V, C_Z, C_X, C_DT, C_GA, C_GS = 0, 1024, 1280, 1536, 3584, 6656, 6688, 7712
NSLOT = 4

COMPUTE = ("pe", "act", "dve", "pool")
NDSEM = 8


class StopBuild(Exception):
    pass


class Op:
    __slots__ = ("eng", "fn", "dma", "cdeps", "ddeps", "signal", "dsem", "dval", "prev_same", "cidx", "phase")

    def __init__(self, eng, fn, dma):
        self.eng = eng
        self.fn = fn
        self.dma = dma
        self.cdeps = {}
        self.ddeps = []
        self.signal = False
        self.dsem = None
        self.dval = 0
        self.prev_same = None
        self.cidx = -1


class Rec:
    __slots__ = ("lo", "hi", "Wc", "Rc", "Wd", "Rd")

    def __init__(self, lo, hi):
        self.lo = lo
        self.hi = hi
        self.Wc = {}
        self.Rc = {}
        self.Wd = []
        self.Rd = []


class Sched:
    def __init__(self):
        self.queues = {e: [] for e in ("pe", "act", "dve", "pool", "sp")}
        self.ccount = {e: 0 for e in COMPUTE}
        self.cops = {e: [] for e in COMPUTE}
        self.spaces = {}
        self.dma_n = {q: 0 for q in ("sp", "act", "pool")}
        self.dma_last = {}

    def _recs(self, tok):
        sp, lo, hi = tok
        lst = self.spaces.setdefault(sp, [])
        exact = None
        over = []
        for r in lst:
            if r.lo < hi and lo < r.hi:
                over.append(r)
                if r.lo == lo and r.hi == hi:
                    exact = r
        if exact is None:
            exact = Rec(lo, hi)
            lst.append(exact)
            over.append(exact)
        return exact, over

    max_ops = None
    n_ops = 0
    phase = "setup"

    def add(self, eng, fn, reads=(), writes=(), dma=False):
        if self.max_ops is not None and self.n_ops >= self.max_ops:
            raise StopBuild()
        self.n_ops += 1
        op = Op(eng, fn, dma)
        op.phase = self.phase
        cd = op.cdeps
        sp_excl = True
        dd = []
        rrecs = []
        for tok in reads:
            exact, over = self._recs(tok)
            rrecs.append(exact)
            for r in over:
                for e, i in r.Wc.items():
                    if cd.get(e, -1) < i:
                        cd[e] = i
                dd.extend(r.Wd)
                if sp_excl and tok[0] == "ps":
                    for e, i in r.Rc.items():
                        if e != eng and cd.get(e, -1) < i:
                            cd[e] = i
        wrecs = []
        for tok in writes:
            exact, over = self._recs(tok)
            wrecs.append(exact)
            for r in over:
                for e, i in r.Wc.items():
                    if cd.get(e, -1) < i:
                        cd[e] = i
                for e, i in r.Rc.items():
                    if cd.get(e, -1) < i:
                        cd[e] = i
                dd.extend(r.Wd)
                dd.extend(r.Rd)
        seen = set()
        for d in dd:
            if id(d) not in seen:
                seen.add(id(d))
                op.ddeps.append(d)
        if dma:
            k = self.dma_n[eng] % NDSEM
            self.dma_n[eng] += 1
            op.dsem = (eng, k)
            prev = self.dma_last.get((eng, k))
            op.prev_same = prev
            op.dval = (prev.dval if prev is not None else 0) + 16
            self.dma_last[(eng, k)] = op
            for r in rrecs:
                r.Rd.append(op)
            for r in wrecs:
                r.Wd = [op]
                r.Rd = []
        else:
            op.cidx = self.ccount[eng]
            self.ccount[eng] += 1
            self.cops[eng].append(op)
            for r in rrecs:
                r.Rc[eng] = op.cidx
            for r in wrecs:
                r.Wc[eng] = op.cidx
                r.Wd = []
                r.Rd = []
        self.queues[eng].append(op)
        return op

    def finalize(self):
        self.plan = {}
        for q, ops in self.queues.items():
            waited_c = {e: -1 for e in COMPUTE}
            waited_d = set()
            for op in ops:
                waits_c = []
                waits_d = []
                for e, i in op.cdeps.items():
                    if e == q and not op.dma:
                        if q == "pe":
                            continue
                    if waited_c[e] >= i:
                        continue
                    waited_c[e] = i
                    waits_c.append((e, i))
                for d in op.ddeps:
                    if id(d) in waited_d:
                        continue
                    waited_d.add(id(d))
                    waits_d.append(d)
                if op.dma and op.prev_same is not None and id(op.prev_same) not in waited_d:
                    waited_d.add(id(op.prev_same))
                    waits_d.append(op.prev_same)
                self.plan[id(op)] = (waits_c, waits_d)
                for e, i in waits_c:
                    self.cops[e][i].signal = True
        self.sigcount = {}
        for e in COMPUTE:
            c = 0
            arr = []
            for op in self.cops[e]:
                if op.signal:
                    c += 1
                arr.append(c)
            self.sigcount[e] = arr

    def emit_queue(self, q, e, sems_c, sems_d, final_wait=False):
        for op in self.queues[q]:
            waits_c, waits_d = self.plan[id(op)]
            for (pe_, i) in waits_c:
                e.wait_ge(sems_c[pe_], self.sigcount[pe_][i])
            for d in waits_d:
                e.wait_ge(sems_d[d.dsem], d.dval)
            ins = op.fn(e)
            if op.dma:
                ins.then_inc(sems_d[op.dsem], 16)
            elif op.signal:
                ins.then_inc(sems_c[q], 1)
        if final_wait:
            for key, d in self.dma_last.items():
                e.wait_ge(sems_d[key], d.dval)


def run_sched(nc, S):
    S.finalize()
    with contextlib.ExitStack() as st:
        sems_c = {e: st.enter_context(nc.semaphore("c_" + e)) for e in COMPUTE}
        sems_d = {}
        for q in ("sp", "act", "pool"):
            for k in range(NDSEM):
                sems_d[(q, k)] = st.enter_context(nc.semaphore("d_%s%d" % (q, k)))
        block = st.enter_context(nc.Block())

        @block.sync
        def _(e):
            S.emit_queue("sp", e, sems_c, sems_d, final_wait=True)

        @block.scalar
        def _(e):
            S.emit_queue("act", e, sems_c, sems_d)

        @block.vector
        def _(e):
            S.emit_queue("dve", e, sems_c, sems_d)

        @block.gpsimd
        def _(e):
            S.emit_queue("pool", e, sems_c, sems_d)

        @block.tensor
        def _(e):
            S.emit_queue("pe", e, sems_c, sems_d)


class Buf:
    def __init__(self, ap, space, lo, hi):
        self.ap = ap
        self.space = space
        self.lo = lo
        self.hi = hi

    def tok(self, a=None, b=None, n=None):
        if a is None:
            return (self.space, self.lo, self.hi)
        w = (self.hi - self.lo) // n
        return (self.space, self.lo + a * w, self.lo + b * w)


def host_consts():
    c = {}
    c["ident"] = np.eye(128, dtype=np.float32)
    k = np.arange(128)
    c["tri"] = (k[:, None] <= k[None, :]).astype(np.float32)
    c["stri"] = (k[None, :] < k[:, None]).astype(np.float32)
    cur = np.where(k[None, :] <= k[:, None], 0.0, NEG).astype(np.float32)
    prv = np.where(k[None, :] > k[:, None], 0.0, NEG).astype(np.float32)
    neg = np.full((128, 128), NEG, np.float32)
    c["masks"] = np.stack([np.concatenate([cur, prv], 1), np.concatenate([prv, cur], 1),
                           np.concatenate([cur, neg], 1)], 0).transpose(1, 0, 2).copy()
    half = 32
    inv = (10000.0 ** (-np.arange(half, dtype=np.float32) / half)).astype(np.float32)
    pos = np.concatenate([np.arange(T, dtype=np.float32).reshape(16, 128).T,
                          np.full((128, 1), float(PAST), np.float32)], 1)
    ang = pos[:, :, None] * inv[None, None, :]
    cos = np.cos(ang).astype(np.float32)
    sin = np.sin(ang).astype(np.float32)
    c["cc"] = np.concatenate([cos, cos], -1).astype(np.float32)
    c["sn"] = np.concatenate([-sin, sin], -1).astype(np.float32)
    selT = np.zeros((128, 16, 16), np.float32)
    for b in range(16):
        selT[:, b, b] = 1.0
    c["selT"] = selT
    return c


CONST_SHAPES = {"ident": [128, 128], "tri": [128, 128], "stri": [128, 128], "masks": [128, 3, 256],
                "cc": [128, 17, 64], "sn": [128, 17, 64], "selT": [128, 16, 16]}

IN_SHAPES = {
    "x": [T, D], "xs": [NSMP, D], "ck": [NSMP, 128, 256], "cv": [NSMP, 128, 256],
    "sconv": [NSMP, 3, 3072], "sssm": [NSMP, 2048, 128],
    "norm_mix": [1, D], "w_in": [D, IN_DIM], "q_norm": [1, 64], "k_norm": [1, 64], "attn_sinks": [1, 16],
    "conv_w": [4, 3072], "conv_b": [1, 3072], "dt_bias": [1, 32], "a_log": [1, 32], "d_skip": [1, 32],
    "ssm_norm": [1, 2048], "w_attn_o": [D, D], "w_ssm_o": [2048, D], "w_out": [D, D], "norm_mlp": [1, D],
    "w_up": [D, 4096], "w_down": [4096, D],
}
OUT_SHAPES = {
    "yp": [T, D], "ys": [NSMP, D], "kp": [128, 256], "vp": [128, 256], "cp": [3, 3072], "hp": [2048, 128],
    "ks": [NSMP, 128, 256], "vs": [NSMP, 128, 256], "cs": [NSMP, 3, 3072], "hs": [NSMP, 2048, 128],
}


DEBUG = []


def build_program(do_sample=True, ntile=NTILE, debug=False, stop_after=None, max_ops=None):
    nc = bass.Bass("TRN2", target_bir_lowering=False)
    I = {n: nc.dram_tensor(n, s, F32, kind="ExternalInput").ap() for n, s in IN_SHAPES.items()}
    CI = {n: nc.dram_tensor("c_" + n, s, F32, kind="ExternalInput").ap() for n, s in CONST_SHAPES.items()}
    O = {n: nc.dram_tensor(n, s, F32, kind="ExternalOutput").ap() for n, s in OUT_SHAPES.items()}
    S = Sched()
    S.max_ops = max_ops

    with contextlib.ExitStack() as st:
        def sbt(name, shape, dt):
            return st.enter_context(nc.sbuf_tensor(name, shape, dt))

        def pbuf(name, shape, dt):
            return Buf(sbt(name, shape, dt)[:], name, 0, 4096)

        def dbg(name, buf, ap=None):
            if not debug:
                return
            ap = ap if ap is not None else buf.ap
            d = nc.dram_tensor("dbg_" + name, list(ap.shape), ap.dtype, kind="ExternalOutput").ap()
            DEBUG.append("dbg_" + name)
            S.add("sp", lambda e: e.dma_start(out=d, in_=ap), reads=[buf.tok()], dma=True)

        ps = st.enter_context(nc.psum_tensor("ps", [128, 8, 512], F32))
        ps_state = {"next": 0, "reserved": set()}

        def alloc_ps(n=1):
            while True:
                b = ps_state["next"]
                if b % n != 0:
                    b += n - (b % n)
                if b + n > 8:
                    b = 0
                ps_state["next"] = (b + n) % 8
                if all((b + i) not in ps_state["reserved"] for i in range(n)):
                    return b

        def pstok(b, n=1):
            return ("ps", b, b + n)

        def psf(b, n=1):
            if n == 1:
                return ps[:, b, :]
            return ps[:, b:b + n, :].rearrange("p b c -> p (b c)")

        def psb(b, n=1):
            return psf(b, n).bitcast(BF16)

        ident_f = pbuf("ident_f", [128, 128], F32)
        ident_b = pbuf("ident_b", [128, 128], BF16)
        tri_f = pbuf("tri_f", [128, 128], F32)
        tri_r = pbuf("tri_r", [128, 128], F32R)
        stri_f = pbuf("stri_f", [128, 128], F32)
        stri_r = pbuf("stri_r", [128, 128], F32R)
        ones_r = pbuf("ones_r", [128, 128], F32R)
        masks = pbuf("masks", [128, 3, 256], F32)
        masks_b = pbuf("masks_b", [128, 3, 256], BF16)
        cc = pbuf("cc", [128, 17, 64], F32)
        sn = pbuf("sn", [128, 17, 64], F32)
        selT = pbuf("selT", [128, 16, 16], F32)
        cst_all = pbuf("cst_all", [128, 152], F32)
        stgA = pbuf("stgA", [128, 128], F32)
        stgB = pbuf("stgB", [128, 128], F32)

        def cview(lo, hi, shape3=None):
            ap = cst_all.ap[:, lo:hi]
            if shape3 is not None:
                ap = ap.rearrange("p (a b) -> p a b", b=shape3)
            return Buf(ap, "cst_all", 0, 4096)

        gT1 = cview(0, 8)
        gT2 = cview(8, 16)
        gsT = cview(16, 32)
        convb = cview(32, 56)
        convw = cview(56, 152, 24)
        gq = pbuf("gq", [128, 64], F32)
        gk = pbuf("gk", [128, 64], F32)
        esink = pbuf("esink", [128, 16], F32)
        dtb = pbuf("dtb", [128, 32], F32)
        A_bc = pbuf("A_bc", [128, 32], F32)
        dsk = pbuf("dsk", [128, 32], F32)

        def dma_in(buf, src, q="sp"):
            S.add(q, lambda e: e.dma_start(out=buf.ap, in_=src), writes=[buf.tok()], dma=True)

        def dma_in_slow(buf, src):
            S.add("sp", lambda e: e.dma_start(out=buf.ap, in_=src, allow_slow_non_contiguous=True),
                  writes=[buf.tok()], dma=True)

        dma_in(ident_f, CI["ident"])
        dma_in(tri_f, CI["tri"])
        dma_in(stri_f, CI["stri"])
        dma_in(masks, CI["masks"])
        dma_in(cc, CI["cc"])
        dma_in(sn, CI["sn"])
        dma_in(selT, CI["selT"])
        S.add("sp", lambda e: e.dma_start(out=stgA.ap[0:8, :], in_=I["norm_mix"].rearrange("o (k p) -> (o k) p", p=128)),
              writes=[stgA.tok()], dma=True)
        S.add("sp", lambda e: e.dma_start(out=stgA.ap[8:16, :], in_=I["norm_mlp"].rearrange("o (k p) -> (o k) p", p=128)),
              writes=[stgA.tok()], dma=True)
        S.add("sp", lambda e: e.dma_start(out=stgA.ap[16:32, :], in_=I["ssm_norm"].rearrange("o (k p) -> (o k) p", p=128)),
              writes=[stgA.tok()], dma=True)
        S.add("sp", lambda e: e.dma_start(out=stgA.ap[32:56, :], in_=I["conv_b"].rearrange("o (k p) -> (o k) p", p=128)),
              writes=[stgA.tok()], dma=True)
        S.add("sp", lambda e: e.dma_start(out=stgB.ap[0:96, :], in_=I["conv_w"].rearrange("i (c p) -> (i c) p", p=128)),
              writes=[stgB.tok()], dma=True)
        S.add("pe", lambda e: e.transpose(out=ps[:, 0, 0:56], in_=stgA.ap[0:56, :], identity=ident_f.ap[0:56, 0:56]),
              [stgA.tok(), ident_f.tok()], [("ps", 0, 1)])
        S.add("pe", lambda e: e.transpose(out=ps[:, 0, 56:152], in_=stgB.ap[0:96, :], identity=ident_f.ap[0:96, 0:96]),
              [stgB.tok(), ident_f.tok()], [("ps", 0, 1)])
        S.add("dve", lambda e: e.tensor_copy(out=cst_all.ap, in_=ps[:, 0, 0:152]), [("ps", 0, 1)], [cst_all.tok()])
        dma_in(gq, I["q_norm"][0:1, :].to_broadcast([128, 64]))
        dma_in(gk, I["k_norm"][0:1, :].to_broadcast([128, 64]))
        dma_in(esink, I["attn_sinks"][0:1, :].to_broadcast([128, 16]))
        dma_in(dtb, I["dt_bias"][0:1, :].to_broadcast([128, 32]))
        dma_in(A_bc, I["a_log"][0:1, :].to_broadcast([128, 32]))
        dma_in(dsk, I["d_skip"][0:1, :].to_broadcast([128, 32]))

        S.add("dve", lambda e: e.tensor_copy(out=ident_b.ap, in_=ident_f.ap), [ident_f.tok()], [ident_b.tok()])
        S.add("dve", lambda e: e.tensor_copy(out=masks_b.ap, in_=masks.ap), [masks.tok()], [masks_b.tok()])
        S.add("dve", lambda e: e.tensor_copy(out=tri_r.ap, in_=tri_f.ap), [tri_f.tok()], [tri_r.tok()])
        S.add("dve", lambda e: e.tensor_copy(out=stri_r.ap, in_=stri_f.ap), [stri_f.tok()], [stri_r.tok()])
        S.add("dve", lambda e: e.tensor_scalar(out=ones_r.ap, in0=tri_f.ap, scalar1=0.0, scalar2=1.0,
                                               op0=ALU.mult, op1=ALU.add), [tri_f.tok()], [ones_r.tok()])
        S.add("act", lambda e: e.activation(out=esink.ap, in_=esink.ap, func=AF.Exp), [esink.tok()], [esink.tok()])
        S.add("act", lambda e: e.activation(out=A_bc.ap, in_=A_bc.ap, func=AF.Exp), [A_bc.tok()], [A_bc.tok()])
        S.add("dve", lambda e: e.tensor_scalar(out=A_bc.ap, in0=A_bc.ap, scalar1=-1.0, scalar2=None, op0=ALU.mult),
              [A_bc.tok()], [A_bc.tok()])

        xt = pbuf("xt", [128, NS, D], F32)
        hT = pbuf("hT", [128, 8, TT], BF16)
        mixT = pbuf("mixT", [128, 8, TT], BF16)
        onT = pbuf("onT", [128, 16, TT], BF16)
        oT = pbuf("oT", [128, 8, TT], BF16)
        hst = pbuf("hst", [128, 2048], F32)
        hTb = pbuf("hTb", [128, 2048], BF16)
        kT = pbuf("kT", [128, 4, 2, 2, 128], BF16)
        vaug = pbuf("vaug", [128, 2, 4, 80], BF16)
        hal = Buf(sbt("hal", [128, 3, 24], F32)[:], "hal", 0, 24 * 64)
        small = pbuf("small", [128, 256], F32)
        dA_p = [pbuf("dA%d" % i, [128, 32], F32R) for i in range(NS)]
        Rg_p = [pbuf("Rg%d" % i, [128, 8, 128], F32R) for i in range(2)]

        S.add("dve", lambda e: e.memset(hst.ap, 0.0), [], [hst.tok()])
        S.add("dve", lambda e: e.memset(hTb.ap, 0.0), [], [hTb.tok()])
        S.add("dve", lambda e: e.memset(kT.ap, 0.0), [], [kT.tok()])
        S.add("dve", lambda e: e.memset(vaug.ap, 1.0), [], [vaug.tok()])
        S.add("dve", lambda e: e.memset(hal.ap, 0.0), [], [hal.tok()])

        _sm = {"o": 0}

        def smalloc(n):
            o = _sm["o"]
            _sm["o"] += n
            assert _sm["o"] <= 256
            return Buf(small.ap[:, o:o + n], "small", o * 16, (o + n) * 16)

        wslots = [pbuf("w%d" % i, [128, 8, 512], BF16) for i in range(NSLOT)]
        wstate = {"issued": 0, "rel": 0, "pos": 0}

        def wsrc(name, r0, c0, ncol):
            return (I[name][r0:r0 + 1024, c0:c0 + ncol].rearrange("(k p) c -> p k c", p=128), ncol)

        tile_seq = []
        tile_seq += [wsrc("w_in", 0, C_Q, 512), wsrc("w_in", 0, C_Q + 512, 512), wsrc("w_in", 0, C_K, 512)]
        tile_seq += [wsrc("w_attn_o", 0, 0, 512), wsrc("w_in", 0, C_GA, 512),
                     wsrc("w_attn_o", 0, 512, 512), wsrc("w_in", 0, C_GA + 512, 512)]
        tile_seq += [wsrc("w_in", 0, C_Z + 512 * i, 512) for i in range(4)]
        tile_seq += [wsrc("w_in", 0, C_DT, 32)]
        tile_seq += [wsrc("w_in", 0, C_X + 512 * i, 512) for i in range(6)]
        for ct in range(2):
            tile_seq += [wsrc("w_ssm_o", 0, 512 * ct, 512), wsrc("w_ssm_o", 1024, 512 * ct, 512),
                         wsrc("w_in", 0, C_GS + 512 * ct, 512)]
        tile_seq += [wsrc("w_out", 0, 0, 512), wsrc("w_out", 0, 512, 512)]
        tile_seq += [wsrc("w_up", 0, 512 * i, 512) for i in range(8)]
        for ct in range(2):
            tile_seq += [wsrc("w_down", 1024 * kg, 512 * ct, 512) for kg in range(4)]
        NWT = len(tile_seq)
        n_pass = ntile + (1 if do_sample else 0)
        total_w = NWT * n_pass

        wscr = nc.dram_tensor("wscr", [NWT, 128, 4096], BF16, kind="Internal").ap()

        def w_issue():
            i = wstate["issued"]
            if i >= total_w:
                return
            j = i % NWT
            src, ncol = tile_seq[j]
            slot = wslots[i % NSLOT]
            scr = wscr[j].rearrange("p (k c) -> p k c", c=512)[:, :, 0:ncol]
            full = (ncol == 512)
            scr2 = wscr[j].bitcast(F32)
            slot2 = slot.ap.rearrange("p k c -> p (k c)").bitcast(F32)
            if i < NWT:
                S.add("pool", lambda e: e.dma_start(out=slot.ap[:, :, 0:ncol], in_=src), writes=[slot.tok()], dma=True)
                if n_pass > 1:
                    if full:
                        S.add("sp", lambda e: e.dma_start(out=scr2, in_=slot2), reads=[slot.tok()],
                              writes=[("wscr", j, j + 1)], dma=True)
                    else:
                        S.add("sp", lambda e: e.dma_start(out=scr, in_=slot.ap[:, :, 0:ncol]), reads=[slot.tok()],
                              writes=[("wscr", j, j + 1)], dma=True)
            else:
                if full:
                    S.add("pool", lambda e: e.dma_start(out=slot2, in_=scr2), reads=[("wscr", j, j + 1)],
                          writes=[slot.tok()], dma=True)
                else:
                    S.add("pool", lambda e: e.dma_start(out=slot.ap[:, :, 0:ncol], in_=scr), reads=[("wscr", j, j + 1)],
                          writes=[slot.tok()], dma=True)
            wstate["issued"] += 1

        def w_next():
            i = wstate["pos"]
            wstate["pos"] += 1
            while wstate["issued"] < min(total_w, wstate["rel"] + NSLOT):
                w_issue()
            assert wstate["issued"] > i or i >= total_w, "weight ring: holding too many tiles"
            return wslots[i % NSLOT]

        def w_done(n=1):
            wstate["rel"] += n
            while wstate["issued"] < min(total_w, wstate["rel"] + NSLOT):
                w_issue()

        ARW = 24576
        arena_t = sbt("arena", [128, ARW], F32)
        ar = {"o": 0}

        def ar_reset():
            ar["o"] = 0

        def carve(shape, dt, parts=128):
            n = int(np.prod(shape[1:]))
            words = n if dt in (F32, F32R) else (n + 1) // 2
            words = (words + 15) // 16 * 16
            lo = ar["o"]
            ar["o"] += words
            assert ar["o"] <= ARW, ("arena overflow", ar["o"])
            ap = arena_t[:, lo:lo + words]
            if dt == BF16:
                ap = ap.bitcast(BF16)[:, 0:n]
            elif dt == F32R:
                ap = ap.bitcast(F32R)[:, 0:n]
            else:
                ap = ap[:, 0:n]
            if len(shape) == 3:
                ap = ap.rearrange("p (a b) -> p a b", b=shape[2])
            elif len(shape) == 4:
                ap = ap.rearrange("p (a b c) -> p a b c", b=shape[2], c=shape[3])
            return Buf(ap, "arena", lo, lo + words)

        def mm(out, lhsT, rhs, start, stop, r, w):
            S.add("pe", lambda e: e.matmul(out, lhsT=lhsT, rhs=rhs, start=start, stop=stop), r, w)

        def tr(out, in_, ident, r, w):
            S.add("pe", lambda e: e.transpose(out=out, in_=in_, identity=ident), r, w)

        def tt(eng, out, in0, in1, op, r, w):
            S.add(eng, lambda e: e.tensor_tensor(out=out, in0=in0, in1=in1, op=op), r, w)

        def ts(eng, out, in0, s1, s2, op0, op1, r, w):
            if op1 is None:
                S.add(eng, lambda e: e.tensor_scalar(out=out, in0=in0, scalar1=s1, scalar2=None, op0=op0), r, w)
            else:
                S.add(eng, lambda e: e.tensor_scalar(out=out, in0=in0, scalar1=s1, scalar2=s2, op0=op0, op1=op1), r, w)

        def stt(eng, out, in0, scalar, in1, op0, op1, r, w):
            S.add(eng, lambda e: e.scalar_tensor_tensor(out=out, in0=in0, scalar=scalar, in1=in1, op0=op0, op1=op1),
                  r, w)

        def act(out, in_, func, r, w, bias=None, scale=None, accum_out=None):
            kw = {}
            if bias is not None:
                kw["bias"] = bias
            if scale is not None:
                kw["scale"] = scale
            if accum_out is not None:
                kw["accum_out"] = accum_out
            S.add("act", lambda e: e.activation(out=out, in_=in_, func=func, **kw), r, w)

        def cp(eng, out, in_, r, w):
            if eng == "act":
                S.add("act", lambda e: e.activation(out=out, in_=in_, func=AF.Copy), r, w)
            else:
                S.add(eng, lambda e: e.tensor_copy(out=out, in_=in_), r, w)

        def rsqrt_mean(stat, R, n, inv_n):
            act(stat.ap[:R, 0:n], stat.ap[:R, 0:n], AF.Sqrt, [stat.tok()], [stat.tok()], bias=EPS, scale=inv_n)
            S.add("dve", lambda e: e.reciprocal(out=stat.ap[:R, 0:n], in_=stat.ap[:R, 0:n]), [stat.tok()], [stat.tok()])

        ss1 = smalloc(NS)
        ss20 = smalloc(20)
        ss2g = smalloc(4)
        ss2 = smalloc(1)
        den4 = smalloc(4)

        def norm_to_hT(s, R, gT, xn, junk):
            xs_ = xt.ap[:R, s, :]
            act(junk.ap[:R, 0:D], xs_, AF.Square, [xt.tok(s, s + 1, NS)], [junk.tok(), ss1.tok()],
                accum_out=ss1.ap[:R, s:s + 1])
            act(ss1.ap[:R, s:s + 1], ss1.ap[:R, s:s + 1], AF.Sqrt, [ss1.tok()], [ss1.tok()], bias=EPS, scale=1.0 / D)
            S.add("dve", lambda e: e.reciprocal(out=ss1.ap[:R, s:s + 1], in_=ss1.ap[:R, s:s + 1]),
                  [ss1.tok()], [ss1.tok()])
            ts("dve", xn.ap[:R, :], xs_, ss1.ap[:R, s:s + 1], None, ALU.mult, None,
               [xt.tok(s, s + 1, NS), ss1.tok()], [xn.tok()])
            b = alloc_ps(1)
            pv = psb(b).rearrange("p (k t) -> p k t", t=128)
            for k in range(8):
                tr(pv[:, k, :R], xn.ap[:R, k * 128:(k + 1) * 128], ident_b.ap[:R, :R],
                   [xn.tok(), ident_b.tok()], [pstok(b)])
            tt("dve", hT.ap[:, :, s * 128:s * 128 + R], pv[:, :, :R],
               gT.ap.unsqueeze(2).to_broadcast([128, 8, R]), ALU.mult,
               [pstok(b), gT.tok()], [hT.tok(s, s + 1, NS)])

        def proj_b(bank, actT, tokr, s, R, wb, ncol, kc=8, k0=0, ktot=None, acc_first=True, acc_last=True, wk0=0):
            for k in range(kc):
                mm(ps[:R, bank, 0:ncol], actT.ap[:, k0 + k, s * 128:s * 128 + R], wb.ap[:, wk0 + k, 0:ncol],
                   acc_first and k == 0, acc_last and k == kc - 1,
                   [tokr, wb.tok()], [pstok(bank)])

        def proj_a(bank, actT, tokr, NTOK, wb, j, kc=8, k0=0, first=True, last=True):
            for k in range(kc):
                mm(ps[:, bank, 0:NTOK], wb.ap[:, k, j * 128:(j + 1) * 128], actT.ap[:, k0 + k, 0:NTOK],
                   first and k == 0, last and k == kc - 1,
                   [tokr, wb.tok()], [pstok(bank)])

        def do_tile(t, is_sample):
            R = NSMP if is_sample else 128
            nsb = 1 if is_sample else NS
            NTOK = R * nsb
            S.phase = "A" + str(t) + ("s" if is_sample else "")
            ar_reset()
            xn = carve([128, D], BF16)
            junk = carve([128, 1280], F32)
            for s in range(nsb):
                if is_sample:
                    src = I["xs"]
                else:
                    src = I["x"][t * TT + s * 128: t * TT + (s + 1) * 128, :]
                S.add("sp", lambda e, s=s, src=src: e.dma_start(out=xt.ap[:R, s, :], in_=src),
                      writes=[xt.tok(s, s + 1, NS)], dma=True)
                norm_to_hT(s, R, gT1, xn, junk)

            if stop_after == "A":
                raise StopBuild()
            S.phase = "B" + str(t) + ("s" if is_sample else "")
            xg = carve([128, 1280], F32)
            tq = carve([128, 1280], F32)
            uq = carve([128, 1280], F32)
            q_rs = [carve([128, D], BF16) for _ in range(nsb)]
            qf = carve([128, D], F32) if is_sample else None
            kfs = [carve([128, 256], F32) for _ in range(nsb)]
            vfs = [carve([128, 256], F32) for _ in range(nsb)]
            kdup = carve([128, 4, 2, 64], BF16)
            qT = carve([128, 8, 128], BF16)
            sm = [carve([128, 4, 256], F32) for _ in range(2)]
            pp = [carve([128, 4, 256], BF16) for _ in range(2)]
            pT = [carve([128, 8, 128], BF16) for _ in range(2)]
            ob = carve([128, D], BF16)
            rden = carve([128, 16], F32)

            wq0 = w_next()
            wq1 = w_next()
            wkv = w_next()
            for s in range(nsb):
                blk = 16 if is_sample else t * NS + s
                q_r, kf, vf = q_rs[s], kfs[s], vfs[s]
                bq = alloc_ps(2)
                bkv = alloc_ps(1)
                htok = hT.tok(s, s + 1, NS)
                proj_b(bq, hT, htok, s, R, wq0, 512)
                proj_b(bq + 1, hT, htok, s, R, wq1, 512)
                proj_b(bkv, hT, htok, s, R, wkv, 512)
                qps = psf(bq, 2)[:R, :]
                kps = ps[:R, bkv, 0:256]
                vps = ps[:R, bkv, 256:512]
                act(junk.ap[:R, 0:1024], qps, AF.Square, [pstok(bq, 2)], [junk.tok()])
                act(junk.ap[:R, 1024:1280], kps, AF.Square, [pstok(bkv)], [junk.tok()])
                S.add("dve", lambda e: e.tensor_reduce(out=ss20.ap[:R, :],
                                                       in_=junk.ap[:R, :].rearrange("p (h d) -> p h d", d=64),
                                                       axis=AX.X, op=ALU.add), [junk.tok()], [ss20.tok()])
                rsqrt_mean(ss20, R, 20, 1.0 / 64)
                tt("dve", xg.ap[:R, 0:1024].rearrange("p (h d) -> p h d", d=64),
                   qps.rearrange("p (h d) -> p h d", d=64),
                   gq.ap[:R, :].unsqueeze(1).to_broadcast([R, 16, 64]), ALU.mult,
                   [pstok(bq, 2), gq.tok()], [xg.tok()])
                tt("dve", xg.ap[:R, 1024:1280].rearrange("p (h d) -> p h d", d=64),
                   kps.rearrange("p (h d) -> p h d", d=64),
                   gk.ap[:R, :].unsqueeze(1).to_broadcast([R, 4, 64]), ALU.mult,
                   [pstok(bkv), gk.tok()], [xg.tok()])
                cp("act", vf.ap[:R, :], vps, [pstok(bkv)], [vf.tok()])
                xg3 = xg.ap[:R, :].rearrange("p (h d) -> p h d", d=64)
                tq3 = tq.ap[:R, :].rearrange("p (h d) -> p h d", d=64)
                uq3 = uq.ap[:R, :].rearrange("p (h d) -> p h d", d=64)
                tt("dve", tq3, xg3, cc.ap[:R, blk, :].unsqueeze(1).to_broadcast([R, 20, 64]), ALU.mult,
                   [xg.tok(), cc.tok()], [tq.tok()])
                tt("dve", uq3[:, :, 0:32], xg3[:, :, 32:64],
                   sn.ap[:R, blk, 0:32].unsqueeze(1).to_broadcast([R, 20, 32]), ALU.mult,
                   [xg.tok(), sn.tok()], [uq.tok()])
                tt("dve", uq3[:, :, 32:64], xg3[:, :, 0:32],
                   sn.ap[:R, blk, 32:64].unsqueeze(1).to_broadcast([R, 20, 32]), ALU.mult,
                   [xg.tok(), sn.tok()], [uq.tok()])
                tt("dve", tq.ap[:R, :], tq.ap[:R, :], uq.ap[:R, :], ALU.add, [tq.tok(), uq.tok()], [tq.tok()])
                qdst = qf if is_sample else q_r
                tt("dve", qdst.ap[:R, :].rearrange("p (h d) -> p h d", d=64), tq3[:, 0:16, :],
                   ss20.ap[:R, 0:16].unsqueeze(2).to_broadcast([R, 16, 64]), ALU.mult,
                   [tq.tok(), ss20.tok()], [qdst.tok()])
                tt("dve", kf.ap[:R, :].rearrange("p (h d) -> p h d", d=64), tq3[:, 16:20, :],
                   ss20.ap[:R, 16:20].unsqueeze(2).to_broadcast([R, 4, 64]), ALU.mult,
                   [tq.tok(), ss20.tok()], [kf.tok()])

            for s in range(nsb):
                blk = 16 if is_sample else t * NS + s
                q_r, kf, vf = q_rs[s], kfs[s], vfs[s]
                if not is_sample and t == 0:
                    dbg("q%d" % s, q_r)
                    dbg("k%d" % s, kf)
                    dbg("v%d" % s, vf)
                if not is_sample:
                    slot = blk % 2
                    mi = 2 if blk == 0 else (0 if slot == 0 else 1)
                    cp("dve", vaug.ap[:, slot, :, 0:64], vf.ap[:, :].rearrange("p (g d) -> p g d", d=64),
                       [vf.tok()], [vaug.tok(slot, slot + 1, 2)])
                    cp("dve", kdup.ap, kf.ap[:, :].rearrange("p (g d) -> p g d", d=64).unsqueeze(2)
                       .to_broadcast([128, 4, 2, 64]), [kf.tok()], [kdup.tok()])
                    bk = alloc_ps(1)
                    pk = psb(bk).rearrange("p (k t) -> p k t", t=128)
                    for g in range(4):
                        tr(pk[:, g, :], kdup.ap[:, g, :, :].rearrange("p a d -> p (a d)"), ident_b.ap,
                           [kdup.tok(), ident_b.tok()], [pstok(bk)])
                    cp("act", kT.ap[0:64, :, 0, slot, :], pk[0:64, 0:4, :], [pstok(bk)], [kT.tok()])
                    cp("act", kT.ap[64:128, :, 1, slot, :], pk[64:128, 0:4, :], [pstok(bk)], [kT.tok()])
                    bqt = alloc_ps(1)
                    pq = psb(bqt).rearrange("p (k t) -> p k t", t=128)
                    for k in range(8):
                        tr(pq[:, k, :], q_r.ap[:, k * 128:(k + 1) * 128], ident_b.ap,
                           [q_r.tok(), ident_b.tok()], [pstok(bqt)])
                    cp("act", qT.ap, pq, [pstok(bqt)], [qT.tok()])
                    if blk == T // 128 - 1:
                        S.add("sp", lambda e: e.dma_start(out=O["kp"], in_=kf.ap), reads=[kf.tok()], dma=True)
                        S.add("sp", lambda e: e.dma_start(out=O["vp"], in_=vf.ap), reads=[vf.tok()], dma=True)
                    def att_stage1(g):
                        smg, ppg = sm[g % 2], pp[g % 2]
                        bs = alloc_ps(2)
                        for hh in range(4):
                            h = 4 * g + hh
                            qt_ = h // 2
                            mm(ps[:, bs + hh // 2, (hh % 2) * 256:(hh % 2) * 256 + 256],
                               qT.ap[:, qt_, :],
                               kT.ap[:, g, h % 2, :, :].rearrange("p a b -> p (a b)"), True, False,
                               [qT.tok(), kT.tok()], [pstok(bs, 2)])
                            mm(ps[:, bs + hh // 2, (hh % 2) * 256:(hh % 2) * 256 + 256],
                               ident_b.ap, masks_b.ap[:, mi, :], False, True,
                               [ident_b.tok(), masks_b.tok()], [pstok(bs, 2)])
                        sv = psf(bs, 2).rearrange("p (h k) -> p h k", k=256)
                        act(ppg.ap, sv, AF.Exp, [pstok(bs, 2)], [ppg.tok()], scale=0.125)

                    def att_stage2(g):
                        ppg, pTg = pp[g % 2], pT[g % 2]
                        bt_ = alloc_ps(1)
                        pt = psb(bt_).rearrange("p (k t) -> p k t", t=128)
                        for hh in range(4):
                            for c in range(2):
                                tr(pt[:, hh * 2 + c, :], ppg.ap[:, hh, c * 128:(c + 1) * 128], ident_b.ap,
                                   [ppg.tok(), ident_b.tok()], [pstok(bt_)])
                        cp("act", pTg.ap, pt, [pstok(bt_)], [pTg.tok()])
                        bo = alloc_ps(1)
                        ov = ps[:, bo, :].rearrange("p (h e) -> p h e", e=128)
                        for hh in range(4):
                            for c in range(2):
                                mm(ov[:, hh, 0:65], pTg.ap[:, hh * 2 + c, :], vaug.ap[:, c, g, 0:65], c == 0, c == 1,
                                   [pTg.tok(), vaug.tok()], [pstok(bo)])
                        tt("dve", den4.ap, ov[:, :, 64], esink.ap[:, 4 * g:4 * g + 4], ALU.add,
                           [pstok(bo), esink.tok()], [den4.tok()])
                        S.add("dve", lambda e, g=g: e.reciprocal(out=rden.ap[:, 4 * g:4 * g + 4], in_=den4.ap),
                              [den4.tok()], [rden.tok()])
                        tt("dve", ob.ap[:, 256 * g:256 * (g + 1)].rearrange("p (h d) -> p h d", d=64),
                           ov[:, :, 0:64], rden.ap[:, 4 * g:4 * g + 4].unsqueeze(2).to_broadcast([128, 4, 64]),
                           ALU.mult, [pstok(bo), rden.tok()], [ob.tok()])

                    for g in range(5):
                        if g < 4:
                            att_stage1(g)
                        if g >= 1:
                            att_stage2(g - 1)
                else:
                    attention_decode(R, qf, kf, vf, ob, junk, xg, tq)
                if not is_sample and t == 0:
                    dbg("ob%d" % s, ob)
                bot = alloc_ps(1)
                po = psb(bot).rearrange("p (k t) -> p k t", t=128)
                for k in range(8):
                    tr(po[:, k, :R], ob.ap[:R, k * 128:(k + 1) * 128], ident_b.ap[:R, :R],
                       [ob.tok(), ident_b.tok()], [pstok(bot)])
                cp("act", oT.ap[:, :, s * 128:s * 128 + R], po[:, :, :R], [pstok(bot)], [oT.tok(s, s + 1, NS)])

            if stop_after == "B":
                raise StopBuild()
            w_done(3)
            S.phase = "D" + str(t) + ("s" if is_sample else "")
            sg = [carve([128, TT], F32) for _ in range(2)]
            for ct in range(2):
                wao = w_next()
                wga = w_next()
                for j in range(4):
                    cb = ct * 4 + j
                    boa = alloc_ps(1)
                    bga = alloc_ps(1)
                    proj_a(boa, oT, oT.tok(), NTOK, wao, j)
                    proj_a(bga, hT, hT.tok(), NTOK, wga, j)
                    sgb = sg[cb % 2]
                    act(sgb.ap[:, :NTOK], ps[:, bga, 0:NTOK], AF.Sigmoid, [pstok(bga)], [sgb.tok()])
                    tt("dve", mixT.ap[:, cb, :NTOK], ps[:, boa, 0:NTOK], sgb.ap[:, :NTOK], ALU.mult,
                       [pstok(boa), sgb.tok()], [mixT.tok(cb, cb + 1, 8)])
                w_done(2)

            if stop_after == "D":
                raise StopBuild()
            S.phase = "E" + str(t) + ("s" if is_sample else "")
            ar_reset()
            ssm_branch(t, is_sample, R, nsb, NTOK)

            if stop_after == "E":
                raise StopBuild()
            S.phase = "F" + str(t) + ("s" if is_sample else "")
            ar_reset()
            sgf = [carve([128, TT], F32) for _ in range(2)]
            tg = [carve([128, TT], F32) for _ in range(2)]
            for ct in range(2):
                wso0 = w_next()
                wso1 = w_next()
                wgs = w_next()
                for j in range(4):
                    cb = ct * 4 + j
                    bos = alloc_ps(1)
                    bgs = alloc_ps(1)
                    proj_a(bos, onT, onT.tok(), NTOK, wso0, j, k0=0, first=True, last=False)
                    proj_a(bos, onT, onT.tok(), NTOK, wso1, j, k0=8, first=False, last=True)
                    proj_a(bgs, hT, hT.tok(), NTOK, wgs, j)
                    sgb, tgb = sgf[cb % 2], tg[cb % 2]
                    act(sgb.ap[:, :NTOK], ps[:, bgs, 0:NTOK], AF.Sigmoid, [pstok(bgs)], [sgb.tok()])
                    tt("dve", tgb.ap[:, :NTOK], ps[:, bos, 0:NTOK], sgb.ap[:, :NTOK], ALU.mult,
                       [pstok(bos), sgb.tok()], [tgb.tok()])
                    tt("dve", mixT.ap[:, cb, :NTOK], mixT.ap[:, cb, :NTOK], tgb.ap[:, :NTOK], ALU.add,
                       [mixT.tok(cb, cb + 1, 8), tgb.tok()], [mixT.tok(cb, cb + 1, 8)])
                w_done(3)

            S.phase = "G" + str(t) + ("s" if is_sample else "")
            if stop_after == "F":
                raise StopBuild()
            xn2 = carve([128, D], BF16)
            junkg = carve([128, D], F32)
            wo0 = w_next()
            wo1 = w_next()
            bo2s = []
            for s in range(nsb):
                bo2 = alloc_ps(2)
                ps_state["reserved"].update([bo2, bo2 + 1])
                bo2s.append(bo2)
                proj_b(bo2, mixT, mixT.tok(), s, R, wo0, 512)
                proj_b(bo2 + 1, mixT, mixT.tok(), s, R, wo1, 512)
            for s in range(nsb):
                bo2 = bo2s[s]
                ps_state["reserved"].discard(bo2)
                ps_state["reserved"].discard(bo2 + 1)
                tt("dve", xt.ap[:R, s, :], xt.ap[:R, s, :], psf(bo2, 2)[:R, :], ALU.add,
                   [xt.tok(s, s + 1, NS), pstok(bo2, 2)], [xt.tok(s, s + 1, NS)])
                norm_to_hT(s, R, gT2, xn2, junkg)
            w_done(2)

            if stop_after == "G":
                raise StopBuild()
            S.phase = "H" + str(t) + ("s" if is_sample else "")
            aT = carve([128, 32, TT], BF16)
            rl = [carve([128, TT], F32) for _ in range(2)]
            yo = carve([128, NS, D], F32)
            for i in range(8):
                wu = w_next()
                for j in range(4):
                    cb = i * 4 + j
                    bu = alloc_ps(1)
                    proj_a(bu, hT, hT.tok(), NTOK, wu, j)
                    rlb = rl[cb % 2]
                    act(rlb.ap[:, :NTOK], ps[:, bu, 0:NTOK], AF.Relu, [pstok(bu)], [rlb.tok()])
                    tt("dve", aT.ap[:, cb, :NTOK], rlb.ap[:, :NTOK], rlb.ap[:, :NTOK], ALU.mult,
                       [rlb.tok()], [aT.tok(cb, cb + 1, 32)])
                w_done(1)

            if stop_after == "H":
                raise StopBuild()
            S.phase = "I" + str(t) + ("s" if is_sample else "")
            for ct in range(2):
                banks = []
                for s in range(nsb):
                    b = alloc_ps(1)
                    ps_state["reserved"].add(b)
                    banks.append(b)
                for kg in range(4):
                    wd = w_next()
                    for s in range(nsb):
                        proj_b(banks[s], aT, aT.tok(), s, R, wd, 512, kc=8, k0=kg * 8,
                               acc_first=(kg == 0), acc_last=(kg == 3))
                    w_done(1)
                for s in range(nsb):
                    tt("dve", yo.ap[:R, s, ct * 512:(ct + 1) * 512], xt.ap[:R, s, ct * 512:(ct + 1) * 512],
                       ps[:R, banks[s], :], ALU.add,
                       [xt.tok(s, s + 1, NS), pstok(banks[s])], [yo.tok(s, s + 1, NS)])
                    ps_state["reserved"].discard(banks[s])
            for s in range(nsb):
                if is_sample:
                    dst = O["ys"]
                else:
                    dst = O["yp"][t * TT + s * 128: t * TT + (s + 1) * 128, :]
                S.add("sp", lambda e, s=s, dst=dst: e.dma_start(out=dst, in_=yo.ap[:R, s, :]),
                      reads=[yo.tok(s, s + 1, NS)], dma=True)

        def ssm_branch(t, is_sample, R, nsb, NTOK):
            sz = carve([128, NS, 2048], BF16)
            dtv = carve([128, NS, 32], F32)
            xb = carve([128, 32], F32)
            ab = carve([128, 32], F32)
            for i in range(4):
                wz = w_next()
                for s in range(nsb):
                    bz = alloc_ps(1)
                    proj_b(bz, hT, hT.tok(s, s + 1, NS), s, R, wz, 512)
                    act(sz.ap[:R, s, i * 512:(i + 1) * 512], ps[:R, bz, :], AF.Silu, [pstok(bz)],
                        [sz.tok(s, s + 1, NS)])
                w_done(1)
            wdt = w_next()
            for s in range(nsb):
                bd = alloc_ps(1)
                proj_b(bd, hT, hT.tok(s, s + 1, NS), s, R, wdt, 32)
                tt("dve", xb.ap[:R, :], ps[:R, bd, 0:32], dtb.ap[:R, :], ALU.add, [pstok(bd), dtb.tok()], [xb.tok()])
                act(ab.ap[:R, :], xb.ap[:R, :], AF.Abs, [xb.tok()], [ab.tok()])
                act(ab.ap[:R, :], ab.ap[:R, :], AF.Exp, [ab.tok()], [ab.tok()], scale=-1.0)
                act(ab.ap[:R, :], ab.ap[:R, :], AF.Ln, [ab.tok()], [ab.tok()], bias=1.0)
                stt("dve", dtv.ap[:R, s, :], xb.ap[:R, :], 0.0, ab.ap[:R, :], ALU.max, ALU.add,
                    [xb.tok(), ab.tok()], [dtv.tok(s, s + 1, NS)])
            w_done(1)
            if is_sample:
                ssm_sample(R, sz, dtv)
                return
            xs_tok = carve([128, NS, 2048], BF16)
            BT = carve([128, 4, TT], BF16)
            CT = carve([128, 4, TT], BF16)
            xr = [carve([128, TT + 16], F32) for _ in range(3)]
            acc = [carve([128, TT], F32) for _ in range(3)]
            xc8 = carve([128, 8, TT], BF16)
            def conv_tail(c, accb):
                if c < 16:
                    act(xc8.ap[:, c % 8, :], accb.ap, AF.Silu, [accb.tok()], [xc8.tok(c % 8, c % 8 + 1, 8)])
                    if c % 8 == 7:
                        for s in range(NS):
                            bxt = alloc_ps(1)
                            pxt = psb(bxt).rearrange("p (k t) -> p k t", t=128)
                            for cc_ in range(8):
                                tr(pxt[:, cc_, :], xc8.ap[:, cc_, s * 128:(s + 1) * 128], ident_b.ap,
                                   [xc8.tok(), ident_b.tok()], [pstok(bxt)])
                            cp("act", xs_tok.ap[:, s, (c // 8) * 1024:(c // 8 + 1) * 1024],
                               psb(bxt)[:, 0:1024], [pstok(bxt)], [xs_tok.tok(s, s + 1, NS)])
                elif c < 20:
                    act(BT.ap[:, c - 16, :], accb.ap, AF.Silu, [accb.tok()], [BT.tok()])
                else:
                    act(CT.ap[:, c - 20, :], accb.ap, AF.Silu, [accb.tok()], [CT.tok()])

            pend = None
            for i in range(6):
                wx = w_next()
                for j in range(4):
                    c = 4 * i + j
                    bx = alloc_ps(1)
                    proj_a(bx, hT, hT.tok(), TT, wx, j)
                    xrb, accb = xr[c % 3], acc[c % 3]
                    cp("act", xrb.ap[:, 3:3 + TT], ps[:, bx, 0:TT], [pstok(bx)], [xrb.tok()])
                    act(accb.ap, ps[:, bx, 0:TT], AF.Identity, [pstok(bx), convw.tok(), convb.tok()], [accb.tok()],
                        scale=convw.ap[:, 3, c:c + 1], bias=convb.ap[:, c:c + 1])
                    if pend is not None:
                        conv_tail(*pend)
                    cp("dve", xrb.ap[:, 0:3], hal.ap[:, :, c], [hal.tok(c, c + 1, 24)], [xrb.tok()])
                    for i_t in (2, 1, 0):
                        stt("dve", accb.ap, xrb.ap[:, i_t:i_t + TT], convw.ap[:, i_t, c:c + 1], accb.ap,
                            ALU.mult, ALU.add, [xrb.tok(), accb.tok(), convw.tok()], [accb.tok()])
                    cp("dve", hal.ap[:, :, c], xrb.ap[:, TT:TT + 3], [xrb.tok()], [hal.tok(c, c + 1, 24)])
                    pend = (c, accb)
                w_done(1)
            conv_tail(*pend)
            S.phase = "E2_" + str(t)
            dAs = dA_p
            ecds = [carve([128, 64], F32) for _ in range(NS)]
            Btoks = [carve([128, 4, 128], BF16) for _ in range(NS)]
            cbms = [carve([128, 4, 128], BF16) for _ in range(NS)]
            Rg = Rg_p
            Eg = [carve([128, 8, 128], BF16) for _ in range(2)]
            WTg = [carve([128, 8, 128], BF16) for _ in range(2)]
            dtx = [carve([128, 8, 64], BF16) for _ in range(2)]
            tex = [carve([128, 8, 64], BF16) for _ in range(2)]
            t1 = [carve([128, 512], F32) for _ in range(2)]
            t2 = [carve([128, 512], F32) for _ in range(2)]
            yz = carve([128, 2048], F32)
            onb = carve([128, 2048], BF16)
            junk2 = carve([128, 512], F32)
            for s in range(NS):
                sl = slice(s * 128, (s + 1) * 128)
                dA, ecd, Btok, cbm = dAs[s], ecds[s], Btoks[s], cbms[s]
                tt("dve", dA.ap, dtv.ap[:, s, :], A_bc.ap, ALU.mult, [dtv.tok(s, s + 1, NS), A_bc.tok()], [dA.tok()])
                bc_ = alloc_ps(1)
                mm(ps[:, bc_, 0:32], tri_r.ap, dA.ap, True, True, [tri_r.tok(), dA.tok()], [pstok(bc_)])
                mm(ps[:, bc_, 32:64], ones_r.ap, dA.ap, True, True, [ones_r.tok(), dA.tok()], [pstok(bc_)])
                act(ecd.ap, ps[:, bc_, 0:64], AF.Exp, [pstok(bc_)], [ecd.tok()])
                bb = alloc_ps(1)
                pbt = psb(bb).rearrange("p (k t) -> p k t", t=128)
                for g in range(4):
                    tr(pbt[:, g, :], BT.ap[:, g, sl], ident_b.ap, [BT.tok(), ident_b.tok()], [pstok(bb)])
                cp("act", Btok.ap, pbt[:, 0:4, :], [pstok(bb)], [Btok.tok()])
                bcb = alloc_ps(1)
                pcb = ps[:, bcb, :].rearrange("p (g l) -> p g l", l=128)
                for g in range(4):
                    mm(pcb[:, g, :], BT.ap[:, g, sl], CT.ap[:, g, sl], True, True, [BT.tok(), CT.tok()], [pstok(bcb)])
                tt("dve", cbm.ap, pcb, tri_f.ap.unsqueeze(1).to_broadcast([128, 4, 128]), ALU.mult,
                   [pstok(bcb), tri_f.tok()], [cbm.tok()])

            for s in range(NS):
                sl = slice(s * 128, (s + 1) * 128)
                dA, ecd, Btok, cbm = dAs[s], ecds[s], Btoks[s], cbms[s]
                def ssd_stage1(g, s=s, sl=sl, dA=dA):
                    Rb, Eb = Rg[g % 2], Eg[g % 2]
                    hs = slice(8 * g, 8 * g + 8)
                    tt("dve", Rb.ap, tri_f.ap.unsqueeze(1).to_broadcast([128, 8, 128]),
                       dA.ap[:, hs].bitcast(F32).unsqueeze(2).to_broadcast([128, 8, 128]), ALU.mult,
                       [tri_f.tok(), dA.tok()], [Rb.tok()])
                    bd_ = alloc_ps(2)
                    for hf in range(2):
                        mm(ps[:, bd_ + hf, :], stri_r.ap, Rb.ap[:, 4 * hf:4 * hf + 4, :].rearrange("p a b -> p (a b)"),
                           True, True, [stri_r.tok(), Rb.tok()], [pstok(bd_, 2)])
                    act(Eb.ap.rearrange("p a b -> p (a b)"), psf(bd_, 2), AF.Exp, [pstok(bd_, 2)], [Eb.tok()])

                def ssd_stage2(g, s=s, sl=sl, ecd=ecd, Btok=Btok, cbm=cbm):
                    Rb, Eb, Wb, dxb, txb, t1b, t2b = Rg[g % 2], Eg[g % 2], WTg[g % 2], dtx[g % 2], tex[g % 2], t1[g % 2], t2[g % 2]
                    hs = slice(8 * g, 8 * g + 8)
                    gs_ = slice(512 * g, 512 * (g + 1))
                    tt("dve", Wb.ap, Eb.ap, cbm.ap[:, g, :].unsqueeze(1).to_broadcast([128, 8, 128]), ALU.mult,
                       [Eb.tok(), cbm.tok()], [Wb.tok()])
                    tt("dve", dxb.ap, xs_tok.ap[:, s, gs_].rearrange("p (h d) -> p h d", d=64),
                       dtv.ap[:, s, hs].unsqueeze(2).to_broadcast([128, 8, 64]), ALU.mult,
                       [xs_tok.tok(s, s + 1, NS), dtv.tok(s, s + 1, NS)], [dxb.tok()])
                    tt("dve", txb.ap, dxb.ap, Eb.ap[:, :, 127:128].to_broadcast([128, 8, 64]), ALU.mult,
                       [dxb.tok(), Eb.tok()], [txb.tok()])
                    by = alloc_ps(1)
                    for hh in range(8):
                        mm(ps[:, by, hh * 64:(hh + 1) * 64], Wb.ap[:, hh, :], dxb.ap[:, hh, :], True, True,
                           [Wb.tok(), dxb.tok()], [pstok(by)])
                    byi = alloc_ps(1)
                    mm(ps[:, byi, :], CT.ap[:, g, sl], hTb.ap[:, gs_], True, True,
                       [CT.tok(), hTb.tok(g, g + 1, 4)], [pstok(byi)])
                    tt("dve", t1b.ap.rearrange("p (h d) -> p h d", d=64),
                       ps[:, byi, :].rearrange("p (h d) -> p h d", d=64),
                       ecd.ap[:, hs].unsqueeze(2).to_broadcast([128, 8, 64]), ALU.mult,
                       [pstok(byi), ecd.tok()], [t1b.tok()])
                    tt("dve", t1b.ap, ps[:, by, :], t1b.ap, ALU.add, [pstok(by), t1b.tok()], [t1b.tok()])
                    tt("dve", t2b.ap.rearrange("p (h d) -> p h d", d=64),
                       xs_tok.ap[:, s, gs_].rearrange("p (h d) -> p h d", d=64),
                       dsk.ap[:, hs].unsqueeze(2).to_broadcast([128, 8, 64]), ALU.mult,
                       [xs_tok.tok(s, s + 1, NS), dsk.tok()], [t2b.tok()])
                    tt("dve", t1b.ap, t1b.ap, t2b.ap, ALU.add, [t1b.tok(), t2b.tok()], [t1b.tok()])
                    tt("dve", yz.ap[:, gs_], t1b.ap, sz.ap[:, s, gs_], ALU.mult,
                       [t1b.tok(), sz.tok(s, s + 1, NS)], [yz.tok(g, g + 1, 4)])
                    act(junk2.ap, yz.ap[:, gs_], AF.Square, [yz.tok(g, g + 1, 4)], [junk2.tok(), ss2g.tok()],
                        accum_out=ss2g.ap[:, g:g + 1])
                    bst = alloc_ps(1)
                    mm(ps[:, bst, :], Btok.ap[:, g, :], txb.ap.rearrange("p a b -> p (a b)"), True, True,
                       [Btok.tok(), txb.tok()], [pstok(bst)])
                    tt("dve", hst.ap[:, gs_].rearrange("p (h d) -> p h d", d=64),
                       hst.ap[:, gs_].rearrange("p (h d) -> p h d", d=64),
                       ecd.ap[:, 32 + 8 * g:32 + 8 * g + 8].unsqueeze(2).to_broadcast([128, 8, 64]), ALU.mult,
                       [hst.tok(g, g + 1, 4), ecd.tok()], [hst.tok(g, g + 1, 4)])
                    tt("dve", hst.ap[:, gs_], hst.ap[:, gs_], ps[:, bst, :], ALU.add,
                       [hst.tok(g, g + 1, 4), pstok(bst)], [hst.tok(g, g + 1, 4)])
                    cp("act", hTb.ap[:, gs_], hst.ap[:, gs_], [hst.tok(g, g + 1, 4)], [hTb.tok(g, g + 1, 4)])

                for g in range(5):
                    if g < 4:
                        ssd_stage1(g)
                    if g >= 1:
                        ssd_stage2(g - 1)
                gated_norm_tail(s, 128, yz, onb)
            if t == NTILE - 1:
                hout = carve([128, 16, 128], F32)
                for q4 in range(4):
                    bh = alloc_ps(4)
                    for k in range(4):
                        tk = q4 * 4 + k
                        tr(ps[:, bh + k, 0:128], hst.ap[:, tk * 128:(tk + 1) * 128], ident_f.ap,
                           [hst.tok(), ident_f.tok()], [pstok(bh, 4)])
                    cp("dve", hout.ap[:, q4 * 4:(q4 + 1) * 4, :], ps[:, bh:bh + 4, 0:128], [pstok(bh, 4)], [hout.tok()])
                S.add("sp", lambda e: e.dma_start(out=O["hp"].rearrange("(t r) n -> r t n", r=128), in_=hout.ap),
                      reads=[hout.tok()], dma=True)
                cst = carve([128, 128], F32)
                bcs = alloc_ps(1)
                tr(ps[0:72, bcs, 0:128], hal.ap.rearrange("p i c -> p (i c)"), ident_f.ap,
                   [hal.tok(), ident_f.tok()], [pstok(bcs)])
                cp("dve", cst.ap[0:72, :], ps[0:72, bcs, 0:128], [pstok(bcs)], [cst.tok()])
                S.add("sp", lambda e: e.dma_start(out=O["cp"].rearrange("i (c p) -> (i c) p", p=128),
                                                  in_=cst.ap[0:72, :]), reads=[cst.tok()], dma=True)

        def gated_norm_tail(s, R, yz, onb):
            S.add("dve", lambda e: e.tensor_reduce(out=ss2.ap[:R, :], in_=ss2g.ap[:R, :], axis=AX.X, op=ALU.add),
                  [ss2g.tok()], [ss2.tok()])
            rsqrt_mean(ss2, R, 1, 1.0 / 2048)
            act(onb.ap[:R, :], yz.ap[:R, :], AF.Copy, [yz.tok(), ss2.tok()], [onb.tok()], scale=ss2.ap[:R, 0:1])
            bn = alloc_ps(2)
            pn = psb(bn, 2).rearrange("p (k t) -> p k t", t=128)
            for k in range(16):
                tr(pn[:, k, :R], onb.ap[:R, k * 128:(k + 1) * 128], ident_b.ap[:R, :R],
                   [onb.tok(), ident_b.tok()], [pstok(bn, 2)])
            tt("dve", onT.ap[:, :, s * 128:s * 128 + R], pn[:, :, :R],
               gsT.ap.unsqueeze(2).to_broadcast([128, 16, R]), ALU.mult,
               [pstok(bn, 2), gsT.tok()], [onT.tok(s, s + 1, NS)])

        def attention_decode(R, qf, kf, vf, ob, junk, xg, tq):
            Kc = [carve([128, 256], F32) for _ in range(2)]
            Vc = [carve([128, 256], F32) for _ in range(2)]
            sel = [carve([128, 128], F32) for _ in range(2)]
            prod = carve([128, 1024], F32)
            pv = [carve([128, 1024], F32) for _ in range(2)]
            sTb = carve([128, 16], F32)
            pTb = [carve([128, 16], F32) for _ in range(2)]
            snew = carve([128, 16], F32)
            pnew = carve([128, 16], F32)
            den16 = carve([128, 16], F32)
            q4 = qf.ap[:R, :].rearrange("p (g h d) -> p g h d", g=4, h=4)
            kb4 = kf.ap[:R, :].rearrange("p (g d) -> p g d", d=64).unsqueeze(2).to_broadcast([R, 4, 4, 64])
            vb4 = vf.ap[:R, :].rearrange("p (g d) -> p g d", d=64).unsqueeze(2).to_broadcast([R, 4, 4, 64])
            tt("dve", xg.ap[:R, 0:1024].rearrange("p (g h d) -> p g h d", g=4, h=4), q4, kb4, ALU.mult,
               [qf.tok(), kf.tok()], [xg.tok()])
            S.add("dve", lambda e: e.tensor_reduce(out=snew.ap[:R, :],
                                                   in_=xg.ap[:R, 0:1024].rearrange("p (h d) -> p h d", d=64),
                                                   axis=AX.X, op=ALU.add), [xg.tok()], [snew.tok()])
            act(pnew.ap[:R, :], snew.ap[:R, :], AF.Exp, [snew.tok()], [pnew.tok()], scale=0.125)
            boacc = alloc_ps(2)
            ps_state["reserved"].update([boacc, boacc + 1])
            bdacc = alloc_ps(1)
            ps_state["reserved"].add(bdacc)
            S.add("sp", lambda e: e.dma_start(out=O["ks"][:, 0:127, :], in_=I["ck"][:, 1:128, :]), dma=True)
            S.add("sp", lambda e: e.dma_start(out=O["vs"][:, 0:127, :], in_=I["cv"][:, 1:128, :]), dma=True)
            S.add("sp", lambda e: e.dma_start(out=O["ks"][:, 127, :], in_=kf.ap[:R, :]), reads=[kf.tok()], dma=True)
            S.add("sp", lambda e: e.dma_start(out=O["vs"][:, 127, :], in_=vf.ap[:R, :]), reads=[vf.tok()], dma=True)
            for b in range(NSMP):
                Kb, Vb, selb, pvb, pTbb = Kc[b % 2], Vc[b % 2], sel[b % 2], pv[b % 2], pTb[b % 2]
                S.add("sp", lambda e, b=b, Kb=Kb: e.dma_start(out=Kb.ap[0:127, :], in_=I["ck"][b, 1:128, :]),
                      writes=[Kb.tok()], dma=True)
                S.add("sp", lambda e, b=b, Vb=Vb: e.dma_start(out=Vb.ap[0:127, :], in_=I["cv"][b, 1:128, :]),
                      writes=[Vb.tok()], dma=True)
                cp("dve", selb.ap[:R, :], ident_f.ap[:R, b:b + 1].to_broadcast([R, 128]), [ident_f.tok()], [selb.tok()])
                bqb = alloc_ps(2)
                for hf in range(2):
                    mm(ps[:, bqb + hf, :], selb.ap[:R, :], qf.ap[:R, hf * 512:(hf + 1) * 512], True, True,
                       [selb.tok(), qf.tok()], [pstok(bqb, 2)])
                tt("dve", prod.ap[0:127, :].rearrange("p (g h d) -> p g h d", g=4, h=4),
                   psf(bqb, 2)[0:127, :].rearrange("p (g h d) -> p g h d", g=4, h=4),
                   Kb.ap[0:127, :].rearrange("p (g d) -> p g d", d=64).unsqueeze(2).to_broadcast([127, 4, 4, 64]),
                   ALU.mult, [pstok(bqb, 2), Kb.tok()], [prod.tok()])
                S.add("dve", lambda e: e.tensor_reduce(out=sTb.ap[0:127, :],
                                                       in_=prod.ap[0:127, :].rearrange("p (h d) -> p h d", d=64),
                                                       axis=AX.X, op=ALU.add), [prod.tok()], [sTb.tok()])
                act(pTbb.ap[0:127, :], sTb.ap[0:127, :], AF.Exp, [sTb.tok()], [pTbb.tok()], scale=0.125)
                tt("dve", pvb.ap[0:127, :].rearrange("p (g h d) -> p g h d", g=4, h=4),
                   Vb.ap[0:127, :].rearrange("p (g d) -> p g d", d=64).unsqueeze(2).to_broadcast([127, 4, 4, 64]),
                   pTbb.ap[0:127, :].rearrange("p (g h) -> p g h", g=4).unsqueeze(3).to_broadcast([127, 4, 4, 64]),
                   ALU.mult, [Vb.tok(), pTbb.tok()], [pvb.tok()])
                for hf in range(2):
                    mm(ps[:R, boacc + hf, :], selT.ap[0:127, b, :], pvb.ap[0:127, hf * 512:(hf + 1) * 512],
                       b == 0, b == NSMP - 1, [selT.tok(), pvb.tok()], [pstok(boacc, 2)])
                mm(ps[:R, bdacc, 0:16], selT.ap[0:127, b, :], pTbb.ap[0:127, :], b == 0, b == NSMP - 1,
                   [selT.tok(), pTbb.tok()], [pstok(bdacc)])
            tt("dve", xg.ap[:R, 0:1024].rearrange("p (g h d) -> p g h d", g=4, h=4), vb4,
               pnew.ap[:R, :].rearrange("p (g h) -> p g h", g=4).unsqueeze(3).to_broadcast([R, 4, 4, 64]), ALU.mult,
               [vf.tok(), pnew.tok()], [xg.tok()])
            tt("dve", xg.ap[:R, 0:1024], xg.ap[:R, 0:1024], psf(boacc, 2)[:R, :], ALU.add,
               [xg.tok(), pstok(boacc, 2)], [xg.tok()])
            tt("dve", den16.ap[:R, :], ps[:R, bdacc, 0:16], pnew.ap[:R, :], ALU.add, [pstok(bdacc), pnew.tok()],
               [den16.tok()])
            tt("dve", den16.ap[:R, :], den16.ap[:R, :], esink.ap[:R, :], ALU.add, [den16.tok(), esink.tok()],
               [den16.tok()])
            S.add("dve", lambda e: e.reciprocal(out=den16.ap[:R, :], in_=den16.ap[:R, :]), [den16.tok()], [den16.tok()])
            tt("dve", ob.ap[:R, :].rearrange("p (h d) -> p h d", d=64),
               xg.ap[:R, 0:1024].rearrange("p (h d) -> p h d", d=64),
               den16.ap[:R, :].unsqueeze(2).to_broadcast([R, 16, 64]), ALU.mult, [xg.tok(), den16.tok()], [ob.tok()])
            for b_ in (boacc, boacc + 1, bdacc):
                ps_state["reserved"].discard(b_)

        def ssm_sample(R, sz, dtv):
            xbc_s = carve([128, 3072], F32)
            xc_s = carve([128, 3072], F32)
            mark = ar["o"]
            cwt = [carve([128, 4, 512], F32) for _ in range(2)]
            stc = [carve([128, 3, 512], F32) for _ in range(2)]
            cbt = [carve([128, 512], F32) for _ in range(2)]
            accs = [carve([128, 512], F32) for _ in range(2)]
            tmpc = [carve([128, 512], F32) for _ in range(2)]
            S.add("sp", lambda e: e.dma_start(out=O["cs"][:, 0:2, :], in_=I["sconv"][:, 1:3, :]), dma=True)
            for i in range(6):
                wx = w_next()
                cw, sc, cb_, ac, tm = cwt[i % 2], stc[i % 2], cbt[i % 2], accs[i % 2], tmpc[i % 2]
                cs_ = slice(i * 512, (i + 1) * 512)
                S.add("sp", lambda e, cw=cw, cs_=cs_: e.dma_start(
                    out=cw.ap[:R, :, :], in_=I["conv_w"][:, cs_].unsqueeze(0).to_broadcast([R, 4, 512])),
                    writes=[cw.tok()], dma=True)
                S.add("sp", lambda e, sc=sc, cs_=cs_: e.dma_start(out=sc.ap[:R, :, :], in_=I["sconv"][:, :, cs_]),
                      writes=[sc.tok()], dma=True)
                S.add("sp", lambda e, cb_=cb_, cs_=cs_: e.dma_start(
                    out=cb_.ap[:R, :], in_=I["conv_b"][0:1, cs_].to_broadcast([R, 512])), writes=[cb_.tok()], dma=True)
                bx = alloc_ps(1)
                proj_b(bx, hT, hT.tok(0, 1, NS), 0, R, wx, 512)
                w_done(1)
                cp("act", xbc_s.ap[:R, cs_], ps[:R, bx, :], [pstok(bx)], [xbc_s.tok()])
                tt("dve", ac.ap[:R, :], ps[:R, bx, :], cw.ap[:R, 3, :], ALU.mult, [pstok(bx), cw.tok()], [ac.tok()])
                tt("dve", ac.ap[:R, :], ac.ap[:R, :], cb_.ap[:R, :], ALU.add, [ac.tok(), cb_.tok()], [ac.tok()])
                for j in range(3):
                    tt("dve", tm.ap[:R, :], sc.ap[:R, j, :], cw.ap[:R, j, :], ALU.mult, [sc.tok(), cw.tok()], [tm.tok()])
                    tt("dve", ac.ap[:R, :], ac.ap[:R, :], tm.ap[:R, :], ALU.add, [ac.tok(), tm.tok()], [ac.tok()])
                act(xc_s.ap[:R, cs_], ac.ap[:R, :], AF.Silu, [ac.tok()], [xc_s.tok()])
            S.add("sp", lambda e: e.dma_start(out=O["cs"][:, 2, :], in_=xbc_s.ap[:R, :]), reads=[xbc_s.tok()], dma=True)
            ar["o"] = mark
            dtx_s = carve([128, 2048], F32)
            decx_s = carve([128, 2048], F32)
            dtxT = carve([128, 16, 16], F32)
            decT = carve([128, 16, 16], F32)
            yT_all = carve([128, 16, 16], F32)
            dA_s = carve([128, 32], F32)
            sel = [carve([128, 128], F32) for _ in range(2)]
            h0t = [carve([128, 16, 128], F32) for _ in range(2)]
            tmp = carve([128, 16, 128], F32)
            tt("dve", dA_s.ap[:R, :], dtv.ap[:R, 0, :], A_bc.ap[:R, :], ALU.mult, [dtv.tok(), A_bc.tok()], [dA_s.tok()])
            act(dA_s.ap[:R, :], dA_s.ap[:R, :], AF.Exp, [dA_s.tok()], [dA_s.tok()])
            tt("dve", dtx_s.ap[:R, :].rearrange("p (h d) -> p h d", d=64),
               xc_s.ap[:R, 0:2048].rearrange("p (h d) -> p h d", d=64),
               dtv.ap[:R, 0, :].unsqueeze(2).to_broadcast([R, 32, 64]), ALU.mult, [xc_s.tok(), dtv.tok()], [dtx_s.tok()])
            cp("dve", decx_s.ap[:R, :].rearrange("p (h d) -> p h d", d=64),
               dA_s.ap[:R, :].unsqueeze(2).to_broadcast([R, 32, 64]), [dA_s.tok()], [decx_s.tok()])
            for src_, dstT in ((dtx_s, dtxT), (decx_s, decT)):
                bt_ = alloc_ps(1)
                pvw = ps[:, bt_, 0:256].rearrange("p (t b) -> p t b", b=16)
                for tk in range(16):
                    tr(pvw[:, tk, :], src_.ap[:R, tk * 128:(tk + 1) * 128], ident_f.ap[:R, :R],
                       [src_.tok(), ident_f.tok()], [pstok(bt_)])
                cp("dve", dstT.ap, pvw, [pstok(bt_)], [dstT.tok()])
            for b in range(NSMP):
                hb, selb = h0t[b % 2], sel[b % 2]
                S.add("sp", lambda e, b=b, hb=hb: e.dma_start(
                    out=hb.ap, in_=I["sssm"][b].rearrange("(t r) n -> r t n", r=128)), writes=[hb.tok()], dma=True)
                cp("dve", selb.ap[:R, :], ident_f.ap[:R, b:b + 1].to_broadcast([R, 128]), [ident_f.tok()], [selb.tok()])
                bB = alloc_ps(1)
                bC = alloc_ps(1)
                mm(ps[:, bB, :], selb.ap[:R, :], xc_s.ap[:R, 2048:2560], True, True, [selb.tok(), xc_s.tok()], [pstok(bB)])
                mm(ps[:, bC, :], selb.ap[:R, :], xc_s.ap[:R, 2560:3072], True, True, [selb.tok(), xc_s.tok()], [pstok(bC)])
                tmp4 = tmp.ap.rearrange("p (g a) n -> p g a n", g=4)
                h4 = hb.ap.rearrange("p (g a) n -> p g a n", g=4)
                tt("dve", tmp4, ps[:, bB, :].rearrange("p (g n) -> p g n", g=4).unsqueeze(2).to_broadcast([128, 4, 4, 128]),
                   dtxT.ap[:, :, b:b + 1].rearrange("p (g a) o -> p g a o", g=4).to_broadcast([128, 4, 4, 128]),
                   ALU.mult, [pstok(bB), dtxT.tok()], [tmp.tok()])
                tt("dve", hb.ap, hb.ap, decT.ap[:, :, b:b + 1].to_broadcast([128, 16, 128]), ALU.mult,
                   [hb.tok(), decT.tok()], [hb.tok()])
                tt("dve", hb.ap, hb.ap, tmp.ap, ALU.add, [hb.tok(), tmp.tok()], [hb.tok()])
                S.add("sp", lambda e, b=b, hb=hb: e.dma_start(
                    out=O["hs"][b].rearrange("(t r) n -> r t n", r=128), in_=hb.ap), reads=[hb.tok()], dma=True)
                tt("dve", tmp4, h4, ps[:, bC, :].rearrange("p (g n) -> p g n", g=4).unsqueeze(2).to_broadcast([128, 4, 4, 128]),
                   ALU.mult, [hb.tok(), pstok(bC)], [tmp.tok()])
                S.add("dve", lambda e, b=b: e.tensor_reduce(out=yT_all.ap[:, :, b], in_=tmp.ap, axis=AX.X, op=ALU.add),
                      [tmp.tok()], [yT_all.tok()])
            by4 = alloc_ps(4)
            for tk in range(16):
                tr(ps[:R, by4 + tk // 4, (tk % 4) * 128:(tk % 4 + 1) * 128], yT_all.ap[:, tk, :], ident_f.ap,
                   [yT_all.tok(), ident_f.tok()], [pstok(by4, 4)])
            ar["o"] = mark
            yz = carve([128, 2048], F32)
            onb = carve([128, 2048], BF16)
            junk3 = carve([128, 2048], BF16)
            tt("dve", yz.ap[:R, :].rearrange("p (h d) -> p h d", d=64),
               xc_s.ap[:R, 0:2048].rearrange("p (h d) -> p h d", d=64),
               dsk.ap[:R, :].unsqueeze(2).to_broadcast([R, 32, 64]), ALU.mult, [xc_s.tok(), dsk.tok()], [yz.tok()])
            tt("dve", yz.ap[:R, :], yz.ap[:R, :], psf(by4, 4)[:R, :], ALU.add, [yz.tok(), pstok(by4, 4)], [yz.tok()])
            tt("dve", yz.ap[:R, :], yz.ap[:R, :], sz.ap[:R, 0, :], ALU.mult, [yz.tok(), sz.tok()], [yz.tok()])
            S.add("dve", lambda e: e.memset(ss2g.ap[:R, :], 0.0), [], [ss2g.tok()])
            act(junk3.ap[:R, :], yz.ap[:R, :], AF.Square, [yz.tok()], [junk3.tok(), ss2g.tok()],
                accum_out=ss2g.ap[:R, 0:1])
            gated_norm_tail(0, R, yz, onb)

        try:
            for t in range(ntile):
                do_tile(t, False)
            if do_sample:
                do_tile(0, True)
        except StopBuild:
            pass
        S.max_ops = None

        run_sched(nc, S)
    return nc


def make_in_maps(inputs):
    consts = host_consts()
    shared = {}
    for n in ("norm_mix", "q_norm", "k_norm", "attn_sinks", "conv_b", "dt_bias", "a_log", "d_skip", "ssm_norm",
              "norm_mlp"):
        shared[n] = np.ascontiguousarray(np.asarray(inputs[n], np.float32).reshape(IN_SHAPES[n]))
    for n in ("w_in", "conv_w", "w_attn_o", "w_ssm_o", "w_out", "w_up", "w_down"):
        shared[n] = np.ascontiguousarray(np.asarray(inputs[n], np.float32)[0])
    for n, v in consts.items():
        shared["c_" + n] = np.ascontiguousarray(v)
    maps = []
    for c in range(NCORES):
        m = dict(shared)
        sl = slice(c * NSMP, (c + 1) * NSMP)
        m["x"] = np.ascontiguousarray(np.asarray(inputs["x_prompt"], np.float32)[c])
        m["xs"] = np.ascontiguousarray(np.asarray(inputs["x_sample"], np.float32)[sl, 0, :])
        m["ck"] = np.ascontiguousarray(np.asarray(inputs["cache_k"], np.float32)[0, sl].reshape(NSMP, 128, 256))
        m["cv"] = np.ascontiguousarray(np.asarray(inputs["cache_v"], np.float32)[0, sl].reshape(NSMP, 128, 256))
        m["sconv"] = np.ascontiguousarray(np.asarray(inputs["state_conv"], np.float32)[0, sl])
        m["sssm"] = np.ascontiguousarray(np.asarray(inputs["state_ssm"], np.float32)[0, sl].reshape(NSMP, 2048, 128))
        maps.append(m)
    return maps


def assemble(results):
    g = lambda n: [np.asarray(r[n], np.float32) for r in results]
    yp = np.stack(g("yp"), 0)
    ys = np.concatenate(g("ys"), 0).reshape(NCORES * NSMP, 1, D)
    kp = np.stack(g("kp"), 0).reshape(1, NCORES, 128, 4, 64)
    vp = np.stack(g("vp"), 0).reshape(1, NCORES, 128, 4, 64)
    cpo = np.stack(g("cp"), 0).reshape(1, NCORES, 3, 3072)
    hp = np.stack(g("hp"), 0).reshape(1, NCORES, 32, 64, 128)
    ks = np.concatenate(g("ks"), 0).reshape(1, NCORES * NSMP, 128, 4, 64)
    vs = np.concatenate(g("vs"), 0).reshape(1, NCORES * NSMP, 128, 4, 64)
    cs = np.concatenate(g("cs"), 0).reshape(1, NCORES * NSMP, 3, 3072)
    hs = np.concatenate(g("hs"), 0).reshape(1, NCORES * NSMP, 32, 64, 128)
    return (yp, ys, kp, vp, cpo, hp, ks, vs, cs, hs)


_PROG = {}


def kernel(**inputs):
    if "nc" not in _PROG:
        _PROG["nc"] = build_program()
    nc = _PROG["nc"]
    maps = make_in_maps(inputs)
    res = run_bass_kernel_spmd(nc, maps, core_ids=list(range(NCORES)))
    return assemble(res.results)
```

```python
import contextlib
import numpy as np
import concourse.bass as bass
import concourse.mybir as mybir
from concourse.bass_utils import run_bass_kernel_spmd

F32 = mybir.dt.float32
BF16 = mybir.dt.bfloat16
F32R = mybir.dt.float32r
AF = mybir.ActivationFunctionType
ALU = mybir.AluOpType
AX = mybir.AxisListType

NCORES = 8
D = 1024
T = 2048
NSMP = 16
NS = 2
TT = NS * 128
NTILE = T // TT
PAST = 8192
EPS = 1e-6
NEG = -1.0e5
IN_DIM = 8736
C_Q, C_K, C_V, C_Z, C_X, C_DT, C_GA, C_GS = 0, 1024, 1280, 1536, 3584, 6656, 6688, 7712
NSLOT = 4

COMPUTE = ("pe", "act", "dve", "pool")
NDSEM = 8


class StopBuild(Exception):
    pass


class Op:
    __slots__ = ("eng", "fn", "dma", "cdeps", "ddeps", "signal", "dsem", "dval", "prev_same", "cidx", "phase")

    def __init__(self, eng, fn, dma):
        self.eng = eng
        self.fn = fn
        self.dma = dma
        self.cdeps = {}
        self.ddeps = []
        self.signal = False
        self.dsem = None
        self.dval = 0
        self.prev_same = None
        self.cidx = -1


class Rec:
    __slots__ = ("lo", "hi", "Wc", "Rc", "Wd", "Rd")

    def __init__(self, lo, hi):
        self.lo = lo
        self.hi = hi
        self.Wc = {}
        self.Rc = {}
        self.Wd = []
        self.Rd = []


class Sched:
    def __init__(self):
        self.queues = {e: [] for e in ("pe", "act", "dve", "pool", "sp")}
        self.ccount = {e: 0 for e in COMPUTE}
        self.cops = {e: [] for e in COMPUTE}
        self.spaces = {}
        self.dma_n = {q: 0 for q in ("sp", "act", "pool")}
        self.dma_last = {}

    def _recs(self, tok):
        sp, lo, hi = tok
        lst = self.spaces.setdefault(sp, [])
        exact = None
        over = []
        for r in lst:
            if r.lo < hi and lo < r.hi:
                over.append(r)
                if r.lo == lo and r.hi == hi:
                    exact = r
        if exact is None:
            exact = Rec(lo, hi)
            lst.append(exact)
            over.append(exact)
        return exact, over

    max_ops = None
    n_ops = 0
    phase = "setup"

    def add(self, eng, fn, reads=(), writes=(), dma=False):
        if self.max_ops is not None and self.n_ops >= self.max_ops:
            raise StopBuild()
        self.n_ops += 1
        op = Op(eng, fn, dma)
        op.phase = self.phase
        cd = op.cdeps
        sp_excl = True
        dd = []
        rrecs = []
        for tok in reads:
            exact, over = self._recs(tok)
            rrecs.append(exact)
            for r in over:
                for e, i in r.Wc.items():
                    if cd.get(e, -1) < i:
                        cd[e] = i
                dd.extend(r.Wd)
                if sp_excl and tok[0] == "ps":
                    for e, i in r.Rc.items():
                        if e != eng and cd.get(e, -1) < i:
                            cd[e] = i
        wrecs = []
        for tok in writes:
            exact, over = self._recs(tok)
            wrecs.append(exact)
            for r in over:
                for e, i in r.Wc.items():
                    if cd.get(e, -1) < i:
                        cd[e] = i
                for e, i in r.Rc.items():
                    if cd.get(e, -1) < i:
                        cd[e] = i
                dd.extend(r.Wd)
                dd.extend(r.Rd)
        seen = set()
        for d in dd:
            if id(d) not in seen:
                seen.add(id(d))
                op.ddeps.append(d)
        if dma:
            k = self.dma_n[eng] % NDSEM
            self.dma_n[eng] += 1
            op.dsem = (eng, k)
            prev = self.dma_last.get((eng, k))
            op.prev_same = prev
            op.dval = (prev.dval if prev is not None else 0) + 16
            self.dma_last[(eng, k)] = op
            for r in rrecs:
                r.Rd.append(op)
            for r in wrecs:
                r.Wd = [op]
                r.Rd = []
        else:
            op.cidx = self.ccount[eng]
            self.ccount[eng] += 1
            self.cops[eng].append(op)
            for r in rrecs:
                r.Rc[eng] = op.cidx
            for r in wrecs:
                r.Wc[eng] = op.cidx
                r.Wd = []
                r.Rd = []
        self.queues[eng].append(op)
        return op

    def finalize(self):
        self.plan = {}
        for q, ops in self.queues.items():
            waited_c = {e: -1 for e in COMPUTE}
            waited_d = set()
            for op in ops:
                waits_c = []
                waits_d = []
                for e, i in op.cdeps.items():
                    if e == q and not op.dma:
                        if q == "pe":
                            continue
                    if waited_c[e] >= i:
                        continue
                    waited_c[e] = i
                    waits_c.append((e, i))
                for d in op.ddeps:
                    if id(d) in waited_d:
                        continue
                    waited_d.add(id(d))
                    waits_d.append(d)
                if op.dma and op.prev_same is not None and id(op.prev_same) not in waited_d:
                    waited_d.add(id(op.prev_same))
                    waits_d.append(op.prev_same)
                self.plan[id(op)] = (waits_c, waits_d)
                for e, i in waits_c:
                    self.cops[e][i].signal = True
        self.sigcount = {}
        for e in COMPUTE:
            c = 0
            arr = []
            for op in self.cops[e]:
                if op.signal:
                    c += 1
                arr.append(c)
            self.sigcount[e] = arr

    def emit_queue(self, q, e, sems_c, sems_d, final_wait=False):
        for op in self.queues[q]:
            waits_c, waits_d = self.plan[id(op)]
            for (pe_, i) in waits_c:
                e.wait_ge(sems_c[pe_], self.sigcount[pe_][i])
            for d in waits_d:
                e.wait_ge(sems_d[d.dsem], d.dval)
            ins = op.fn(e)
            if op.dma:
                ins.then_inc(sems_d[op.dsem], 16)
            elif op.signal:
                ins.then_inc(sems_c[q], 1)
        if final_wait:
            for key, d in self.dma_last.items():
                e.wait_ge(sems_d[key], d.dval)


def run_sched(nc, S):
    S.finalize()
    with contextlib.ExitStack() as st:
        sems_c = {e: st.enter_context(nc.semaphore("c_" + e)) for e in COMPUTE}
        sems_d = {}
        for q in ("sp", "act", "pool"):
            for k in range(NDSEM):
                sems_d[(q, k)] = st.enter_context(nc.semaphore("d_%s%d" % (q, k)))
        block = st.enter_context(nc.Block())

        @block.sync
        def _(e):
            S.emit_queue("sp", e, sems_c, sems_d, final_wait=True)

        @block.scalar
        def _(e):
            S.emit_queue("act", e, sems_c, sems_d)

        @block.vector
        def _(e):
            S.emit_queue("dve", e, sems_c, sems_d)

        @block.gpsimd
        def _(e):
            S.emit_queue("pool", e, sems_c, sems_d)

        @block.tensor
        def _(e):
            S.emit_queue("pe", e, sems_c, sems_d)


class Buf:
    def __init__(self, ap, space, lo, hi):
        self.ap = ap
        self.space = space
        self.lo = lo
        self.hi = hi

    def tok(self, a=None, b=None, n=None):
        if a is None:
            return (self.space, self.lo, self.hi)
        w = (self.hi - self.lo) // n
        return (self.space, self.lo + a * w, self.lo + b * w)


def host_consts():
    c = {}
    c["ident"] = np.eye(128, dtype=np.float32)
    k = np.arange(128)
    c["tri"] = (k[:, None] <= k[None, :]).astype(np.float32)
    c["stri"] = (k[None, :] < k[:, None]).astype(np.float32)
    cur = np.where(k[None, :] <= k[:, None], 0.0, NEG).astype(np.float32)
    prv = np.where(k[None, :] > k[:, None], 0.0, NEG).astype(np.float32)
    neg = np.full((128, 128), NEG, np.float32)
    c["masks"] = np.stack([np.concatenate([cur, prv], 1), np.concatenate([prv, cur], 1),
                           np.concatenate([cur, neg], 1)], 0).transpose(1, 0, 2).copy()
    half = 32
    inv = (10000.0 ** (-np.arange(half, dtype=np.float32) / half)).astype(np.float32)
    pos = np.concatenate([np.arange(T, dtype=np.float32).reshape(16, 128).T,
                          np.full((128, 1), float(PAST), np.float32)], 1)
    ang = pos[:, :, None] * inv[None, None, :]
    cos = np.cos(ang).astype(np.float32)
    sin = np.sin(ang).astype(np.float32)
    c["cc"] = np.concatenate([cos, cos], -1).astype(np.float32)
    c["sn"] = np.concatenate([-sin, sin], -1).astype(np.float32)
    selT = np.zeros((128, 16, 16), np.float32)
    for b in range(16):
        selT[:, b, b] = 1.0
    c["selT"] = selT
    return c


CONST_SHAPES = {"ident": [128, 128], "tri": [128, 128], "stri": [128, 128], "masks": [128, 3, 256],
                "cc": [128, 17, 64], "sn": [128, 17, 64], "selT": [128, 16, 16]}

IN_SHAPES = {
    "x": [T, D], "xs": [NSMP, D], "ck": [NSMP, 128, 256], "cv": [NSMP, 128, 256],
    "sconv": [NSMP, 3, 3072], "sssm": [NSMP, 2048, 128],
    "norm_mix": [1, D], "w_in": [D, IN_DIM], "q_norm": [1, 64], "k_norm": [1, 64], "attn_sinks": [1, 16],
    "conv_w": [4, 3072], "conv_b": [1, 3072], "dt_bias": [1, 32], "a_log": [1, 32], "d_skip": [1, 32],
    "ssm_norm": [1, 2048], "w_attn_o": [D, D], "w_ssm_o": [2048, D], "w_out": [D, D], "norm_mlp": [1, D],
    "w_up": [D, 4096], "w_down": [4096, D],
}
OUT_SHAPES = {
    "yp": [T, D], "ys": [NSMP, D], "kp": [128, 256], "vp": [128, 256], "cp": [3, 3072], "hp": [2048, 128],
    "ks": [NSMP, 128, 256], "vs": [NSMP, 128, 256], "cs": [NSMP, 3, 3072], "hs": [NSMP, 2048, 128],
}


DEBUG = []


def build_program(do_sample=True, ntile=NTILE, debug=False, stop_after=None, max_ops=None):
    nc = bass.Bass("TRN2", target_bir_lowering=False)
    I = {n: nc.dram_tensor(n, s, F32, kind="ExternalInput").ap() for n, s in IN_SHAPES.items()}
    CI = {n: nc.dram_tensor("c_" + n, s, F32, kind="ExternalInput").ap() for n, s in CONST_SHAPES.items()}
    O = {n: nc.dram_tensor(n, s, F32, kind="ExternalOutput").ap() for n, s in OUT_SHAPES.items()}
    S = Sched()
    S.max_ops = max_ops

    with contextlib.ExitStack() as st:
        def sbt(name, shape, dt):
            return st.enter_context(nc.sbuf_tensor(name, shape, dt))

        def pbuf(name, shape, dt):
            return Buf(sbt(name, shape, dt)[:], name, 0, 4096)

        def dbg(name, buf, ap=None):
            if not debug:
                return
            ap = ap if ap is not None else buf.ap
            d = nc.dram_tensor("dbg_" + name, list(ap.shape), ap.dtype, kind="ExternalOutput").ap()
            DEBUG.append("dbg_" + name)
            S.add("sp", lambda e: e.dma_start(out=d, in_=ap), reads=[buf.tok()], dma=True)

        ps = st.enter_context(nc.psum_tensor("ps", [128, 8, 512], F32))
        ps_state = {"next": 0, "reserved": set()}

        def alloc_ps(n=1):
            while True:
                b = ps_state["next"]
                if b % n != 0:
                    b += n - (b % n)
                if b + n > 8:
                    b = 0
                ps_state["next"] = (b + n) % 8
                if all((b + i) not in ps_state["reserved"] for i in range(n)):
                    return b

        def pstok(b, n=1):
            return ("ps", b, b + n)

        def psf(b, n=1):
            if n == 1:
                return ps[:, b, :]
            return ps[:, b:b + n, :].rearrange("p b c -> p (b c)")

        def psb(b, n=1):
            return psf(b, n).bitcast(BF16)

        ident_f = pbuf("ident_f", [128, 128], F32)
        ident_b = pbuf("ident_b", [128, 128], BF16)
        tri_f = pbuf("tri_f", [128, 128], F32)
        tri_r = pbuf("tri_r", [128, 128], F32R)
        stri_f = pbuf("stri_f", [128, 128], F32)
        stri_r = pbuf("stri_r", [128, 128], F32R)
        ones_r = pbuf("ones_r", [128, 128], F32R)
        masks = pbuf("masks", [128, 3, 256], F32)
        masks_b = pbuf("masks_b", [128, 3, 256], BF16)
        cc = pbuf("cc", [128, 17, 64], F32)
        sn = pbuf("sn", [128, 17, 64], F32)
        selT = pbuf("selT", [128, 16, 16], F32)
        cst_all = pbuf("cst_all", [128, 152], F32)
        stgA = pbuf("stgA", [128, 128], F32)
        stgB = pbuf("stgB", [128, 128], F32)

        def cview(lo, hi, shape3=None):
            ap = cst_all.ap[:, lo:hi]
            if shape3 is not None:
                ap = ap.rearrange("p (a b) -> p a b", b=shape3)
            return Buf(ap, "cst_all", 0, 4096)

        gT1 = cview(0, 8)
        gT2 = cview(8, 16)
        gsT = cview(16, 32)
        convb = cview(32, 56)
        convw = cview(56, 152, 24)
        gq = pbuf("gq", [128, 64], F32)
        gk = pbuf("gk", [128, 64], F32)
        esink = pbuf("esink", [128, 16], F32)
        dtb = pbuf("dtb", [128, 32], F32)
        A_bc = pbuf("A_bc", [128, 32], F32)
        dsk = pbuf("dsk", [128, 32], F32)

        def dma_in(buf, src, q="sp"):
            S.add(q, lambda e: e.dma_start(out=buf.ap, in_=src), writes=[buf.tok()], dma=True)

        def dma_in_slow(buf, src):
            S.add("sp", lambda e: e.dma_start(out=buf.ap, in_=src, allow_slow_non_contiguous=True),
                  writes=[buf.tok()], dma=True)

        dma_in(ident_f, CI["ident"])
        dma_in(tri_f, CI["tri"])
        dma_in(stri_f, CI["stri"])
        dma_in(masks, CI["masks"])
        dma_in(cc, CI["cc"])
        dma_in(sn, CI["sn"])
        dma_in(selT, CI["selT"])
        S.add("sp", lambda e: e.dma_start(out=stgA.ap[0:8, :], in_=I["norm_mix"].rearrange("o (k p) -> (o k) p", p=128)),
              writes=[stgA.tok()], dma=True)
        S.add("sp", lambda e: e.dma_start(out=stgA.ap[8:16, :], in_=I["norm_mlp"].rearrange("o (k p) -> (o k) p", p=128)),
              writes=[stgA.tok()], dma=True)
        S.add("sp", lambda e: e.dma_start(out=stgA.ap[16:32, :], in_=I["ssm_norm"].rearrange("o (k p) -> (o k) p", p=128)),
              writes=[stgA.tok()], dma=True)
        S.add("sp", lambda e: e.dma_start(out=stgA.ap[32:56, :], in_=I["conv_b"].rearrange("o (k p) -> (o k) p", p=128)),
              writes=[stgA.tok()], dma=True)
        S.add("sp", lambda e: e.dma_start(out=stgB.ap[0:96, :], in_=I["conv_w"].rearrange("i (c p) -> (i c) p", p=128)),
              writes=[stgB.tok()], dma=True)
        S.add("pe", lambda e: e.transpose(out=ps[:, 0, 0:56], in_=stgA.ap[0:56, :], identity=ident_f.ap[0:56, 0:56]),
              [stgA.tok(), ident_f.tok()], [("ps", 0, 1)])
        S.add("pe", lambda e: e.transpose(out=ps[:, 0, 56:152], in_=stgB.ap[0:96, :], identity=ident_f.ap[0:96, 0:96]),
              [stgB.tok(), ident_f.tok()], [("ps", 0, 1)])
        S.add("dve", lambda e: e.tensor_copy(out=cst_all.ap, in_=ps[:, 0, 0:152]), [("ps", 0, 1)], [cst_all.tok()])
        dma_in(gq, I["q_norm"][0:1, :].to_broadcast([128, 64]))
        dma_in(gk, I["k_norm"][0:1, :].to_broadcast([128, 64]))
        dma_in(esink, I["attn_sinks"][0:1, :].to_broadcast([128, 16]))
        dma_in(dtb, I["dt_bias"][0:1, :].to_broadcast([128, 32]))
        dma_in(A_bc, I["a_log"][0:1, :].to_broadcast([128, 32]))
        dma_in(dsk, I["d_skip"][0:1, :].to_broadcast([128, 32]))

        S.add("dve", lambda e: e.tensor_copy(out=ident_b.ap, in_=ident_f.ap), [ident_f.tok()], [ident_b.tok()])
        S.add("dve", lambda e: e.tensor_copy(out=masks_b.ap, in_=masks.ap), [masks.tok()], [masks_b.tok()])
        S.add("dve", lambda e: e.tensor_copy(out=tri_r.ap, in_=tri_f.ap), [tri_f.tok()], [tri_r.tok()])
        S.add("dve", lambda e: e.tensor_copy(out=stri_r.ap, in_=stri_f.ap), [stri_f.tok()], [stri_r.tok()])
        S.add("dve", lambda e: e.tensor_scalar(out=ones_r.ap, in0=tri_f.ap, scalar1=0.0, scalar2=1.0,
                                               op0=ALU.mult, op1=ALU.add), [tri_f.tok()], [ones_r.tok()])
        S.add("act", lambda e: e.activation(out=esink.ap, in_=esink.ap, func=AF.Exp), [esink.tok()], [esink.tok()])
        S.add("act", lambda e: e.activation(out=A_bc.ap, in_=A_bc.ap, func=AF.Exp), [A_bc.tok()], [A_bc.tok()])
        S.add("dve", lambda e: e.tensor_scalar(out=A_bc.ap, in0=A_bc.ap, scalar1=-1.0, scalar2=None, op0=ALU.mult),
              [A_bc.tok()], [A_bc.tok()])

        xt = pbuf("xt", [128, NS, D], F32)
        hT = pbuf("hT", [128, 8, TT], BF16)
        mixT = pbuf("mixT", [128, 8, TT], BF16)
        onT = pbuf("onT", [128, 16, TT], BF16)
        oT = pbuf("oT", [128, 8, TT], BF16)
        hst = pbuf("hst", [128, 2048], F32)
        hTb = pbuf("hTb", [128, 2048], BF16)
        kT = pbuf("kT", [128, 4, 2, 2, 128], BF16)
        vaug = pbuf("vaug", [128, 2, 4, 80], BF16)
        hal = Buf(sbt("hal", [128, 3, 24], F32)[:], "hal", 0, 24 * 64)
        small = pbuf("small", [128, 256], F32)
        dA_p = [pbuf("dA%d" % i, [128, 32], F32R) for i in range(NS)]
        Rg_p = [pbuf("Rg%d" % i, [128, 8, 128], F32R) for i in range(2)]

        S.add("dve", lambda e: e.memset(hst.ap, 0.0), [], [hst.tok()])
        S.add("dve", lambda e: e.memset(hTb.ap, 0.0), [], [hTb.tok()])
        S.add("dve", lambda e: e.memset(kT.ap, 0.0), [], [kT.tok()])
        S.add("dve", lambda e: e.memset(vaug.ap, 1.0), [], [vaug.tok()])
        S.add("dve", lambda e: e.memset(hal.ap, 0.0), [], [hal.tok()])

        _sm = {"o": 0}

        def smalloc(n):
            o = _sm["o"]
            _sm["o"] += n
            assert _sm["o"] <= 256
            return Buf(small.ap[:, o:o + n], "small", o * 16, (o + n) * 16)

        wslots = [pbuf("w%d" % i, [128, 8, 512], BF16) for i in range(NSLOT)]
        wstate = {"issued": 0, "rel": 0, "pos": 0}

        def wsrc(name, r0, c0, ncol):
            return (I[name][r0:r0 + 1024, c0:c0 + ncol].rearrange("(k p) c -> p k c", p=128), ncol)

        tile_seq = []
        tile_seq += [wsrc("w_in", 0, C_Q, 512), wsrc("w_in", 0, C_Q + 512, 512), wsrc("w_in", 0, C_K, 512)]
        tile_seq += [wsrc("w_attn_o", 0, 0, 512), wsrc("w_in", 0, C_GA, 512),
                     wsrc("w_attn_o", 0, 512, 512), wsrc("w_in", 0, C_GA + 512, 512)]
        tile_seq += [wsrc("w_in", 0, C_Z + 512 * i, 512) for i in range(4)]
        tile_seq += [wsrc("w_in", 0, C_DT, 32)]
        tile_seq += [wsrc("w_in", 0, C_X + 512 * i, 512) for i in range(6)]
        for ct in range(2):
            tile_seq += [wsrc("w_ssm_o", 0, 512 * ct, 512), wsrc("w_ssm_o", 1024, 512 * ct, 512),
                         wsrc("w_in", 0, C_GS + 512 * ct, 512)]
        tile_seq += [wsrc("w_out", 0, 0, 512), wsrc("w_out", 0, 512, 512)]
        tile_seq += [wsrc("w_up", 0, 512 * i, 512) for i in range(8)]
        for ct in range(2):
            tile_seq += [wsrc("w_down", 1024 * kg, 512 * ct, 512) for kg in range(4)]
        NWT = len(tile_seq)
        n_pass = ntile + (1 if do_sample else 0)
        total_w = NWT * n_pass

        wscr = nc.dram_tensor("wscr", [NWT, 128, 4096], BF16, kind="Internal").ap()

        def w_issue():
            i = wstate["issued"]
            if i >= total_w:
                return
            j = i % NWT
            src, ncol = tile_seq[j]
            slot = wslots[i % NSLOT]
            scr = wscr[j].rearrange("p (k c) -> p k c", c=512)[:, :, 0:ncol]
            full = (ncol == 512)
            scr2 = wscr[j].bitcast(F32)
            slot2 = slot.ap.rearrange("p k c -> p (k c)").bitcast(F32)
            if i < NWT:
                S.add("pool", lambda e: e.dma_start(out=slot.ap[:, :, 0:ncol], in_=src), writes=[slot.tok()], dma=True)
                if n_pass > 1:
                    if full:
                        S.add("sp", lambda e: e.dma_start(out=scr2, in_=slot2), reads=[slot.tok()],
                              writes=[("wscr", j, j + 1)], dma=True)
                    else:
                        S.add("sp", lambda e: e.dma_start(out=scr, in_=slot.ap[:, :, 0:ncol]), reads=[slot.tok()],
                              writes=[("wscr", j, j + 1)], dma=True)
            else:
                if full:
                    S.add("pool", lambda e: e.dma_start(out=slot2, in_=scr2), reads=[("wscr", j, j + 1)],
                          writes=[slot.tok()], dma=True)
                else:
                    S.add("pool", lambda e: e.dma_start(out=slot.ap[:, :, 0:ncol], in_=scr), reads=[("wscr", j, j + 1)],
                          writes=[slot.tok()], dma=True)
            wstate["issued"] += 1

        def w_next():
            i = wstate["pos"]
            wstate["pos"] += 1
            while wstate["issued"] < min(total_w, wstate["rel"] + NSLOT):
                w_issue()
            assert wstate["issued"] > i or i >= total_w, "weight ring: holding too many tiles"
            return wslots[i % NSLOT]

        def w_done(n=1):
            wstate["rel"] += n
            while wstate["issued"] < min(total_w, wstate["rel"] + NSLOT):
                w_issue()

        ARW = 24576
        arena_t = sbt("arena", [128, ARW], F32)
        ar = {"o": 0}

        def ar_reset():
            ar["o"] = 0

        def carve(shape, dt, parts=128):
            n = int(np.prod(shape[1:]))
            words = n if dt in (F32, F32R) else (n + 1) // 2
            words = (words + 15) // 16 * 16
            lo = ar["o"]
            ar["o"] += words
            assert ar["o"] <= ARW, ("arena overflow", ar["o"])
            ap = arena_t[:, lo:lo + words]
            if dt == BF16:
                ap = ap.bitcast(BF16)[:, 0:n]
            elif dt == F32R:
                ap = ap.bitcast(F32R)[:, 0:n]
            else:
                ap = ap[:, 0:n]
            if len(shape) == 3:
                ap = ap.rearrange("p (a b) -> p a b", b=shape[2])
            elif len(shape) == 4:
                ap = ap.rearrange("p (a b c) -> p a b c", b=shape[2], c=shape[3])
            return Buf(ap, "arena", lo, lo + words)

        def mm(out, lhsT, rhs, start, stop, r, w):
            S.add("pe", lambda e: e.matmul(out, lhsT=lhsT, rhs=rhs, start=start, stop=stop), r, w)

        def tr(out, in_, ident, r, w):
            S.add("pe", lambda e: e.transpose(out=out, in_=in_, identity=ident), r, w)

        def tt(eng, out, in0, in1, op, r, w):
            S.add(eng, lambda e: e.tensor_tensor(out=out, in0=in0, in1=in1, op=op), r, w)

        def ts(eng, out, in0, s1, s2, op0, op1, r, w):
            if op1 is None:
                S.add(eng, lambda e: e.tensor_scalar(out=out, in0=in0, scalar1=s1, scalar2=None, op0=op0), r, w)
            else:
                S.add(eng, lambda e: e.tensor_scalar(out=out, in0=in0, scalar1=s1, scalar2=s2, op0=op0, op1=op1), r, w)

        def stt(eng, out, in0, scalar, in1, op0, op1, r, w):
            S.add(eng, lambda e: e.scalar_tensor_tensor(out=out, in0=in0, scalar=scalar, in1=in1, op0=op0, op1=op1),
                  r, w)

        def act(out, in_, func, r, w, bias=None, scale=None, accum_out=None):
            kw = {}
            if bias is not None:
                kw["bias"] = bias
            if scale is not None:
                kw["scale"] = scale
            if accum_out is not None:
                kw["accum_out"] = accum_out
            S.add("act", lambda e: e.activation(out=out, in_=in_, func=func, **kw), r, w)

        def cp(eng, out, in_, r, w):
            if eng == "act":
                S.add("act", lambda e: e.activation(out=out, in_=in_, func=AF.Copy), r, w)
            else:
                S.add(eng, lambda e: e.tensor_copy(out=out, in_=in_), r, w)

        def rsqrt_mean(stat, R, n, inv_n):
            act(stat.ap[:R, 0:n], stat.ap[:R, 0:n], AF.Sqrt, [stat.tok()], [stat.tok()], bias=EPS, scale=inv_n)
            S.add("dve", lambda e: e.reciprocal(out=stat.ap[:R, 0:n], in_=stat.ap[:R, 0:n]), [stat.tok()], [stat.tok()])

        ss1 = smalloc(NS)
        ss20 = smalloc(20)
        ss2g = smalloc(4)
        ss2 = smalloc(1)
        ss2g_alt = smalloc(4)
        ss2_alt = smalloc(1)
        den4 = smalloc(4)

        def norm_to_hT(s, R, gT, xn, junk):
            xs_ = xt.ap[:R, s, :]
            act(junk.ap[:R, 0:D], xs_, AF.Square, [xt.tok(s, s + 1, NS)], [junk.tok(), ss1.tok()],
                accum_out=ss1.ap[:R, s:s + 1])
            act(ss1.ap[:R, s:s + 1], ss1.ap[:R, s:s + 1], AF.Sqrt, [ss1.tok()], [ss1.tok()], bias=EPS, scale=1.0 / D)
            S.add("dve", lambda e: e.reciprocal(out=ss1.ap[:R, s:s + 1], in_=ss1.ap[:R, s:s + 1]),
                  [ss1.tok()], [ss1.tok()])
            ts("dve", xn.ap[:R, :], xs_, ss1.ap[:R, s:s + 1], None, ALU.mult, None,
               [xt.tok(s, s + 1, NS), ss1.tok()], [xn.tok()])
            b = alloc_ps(1)
            pv = psb(b).rearrange("p (k t) -> p k t", t=128)
            for k in range(8):
                tr(pv[:, k, :R], xn.ap[:R, k * 128:(k + 1) * 128], ident_b.ap[:R, :R],
                   [xn.tok(), ident_b.tok()], [pstok(b)])
            tt("dve", hT.ap[:, :, s * 128:s * 128 + R], pv[:, :, :R],
               gT.ap.unsqueeze(2).to_broadcast([128, 8, R]), ALU.mult,
               [pstok(b), gT.tok()], [hT.tok(s, s + 1, NS)])

        def proj_b(bank, actT, tokr, s, R, wb, ncol, kc=8, k0=0, ktot=None, acc_first=True, acc_last=True, wk0=0):
            for k in range(kc):
                mm(ps[:R, bank, 0:ncol], actT.ap[:, k0 + k, s * 128:s * 128 + R], wb.ap[:, wk0 + k, 0:ncol],
                   acc_first and k == 0, acc_last and k == kc - 1,
                   [tokr, wb.tok()], [pstok(bank)])

        def proj_a(bank, actT, tokr, NTOK, wb, j, kc=8, k0=0, first=True, last=True):
            for k in range(kc):
                mm(ps[:, bank, 0:NTOK], wb.ap[:, k, j * 128:(j + 1) * 128], actT.ap[:, k0 + k, 0:NTOK],
                   first and k == 0, last and k == kc - 1,
                   [tokr, wb.tok()], [pstok(bank)])

        def do_tile(t, is_sample):
            R = NSMP if is_sample else 128
            nsb = 1 if is_sample else NS
            NTOK = R * nsb
            S.phase = "A" + str(t) + ("s" if is_sample else "")
            ar_reset()
            xn = carve([128, D], BF16)
            junk = carve([128, 1280], F32)
            for s in range(nsb):
                if is_sample:
                    src = I["xs"]
                else:
                    src = I["x"][t * TT + s * 128: t * TT + (s + 1) * 128, :]
                S.add("sp", lambda e, s=s, src=src: e.dma_start(out=xt.ap[:R, s, :], in_=src),
                      writes=[xt.tok(s, s + 1, NS)], dma=True)
                norm_to_hT(s, R, gT1, xn, junk)

            if stop_after == "A":
                raise StopBuild()
            S.phase = "B" + str(t) + ("s" if is_sample else "")
            xg = carve([128, 1280], F32)
            tq = carve([128, 1280], F32)
            uq = carve([128, 1280], F32)
            q_rs = [carve([128, D], BF16) for _ in range(nsb)]
            qf = carve([128, D], F32) if is_sample else None
            kfs = [carve([128, 256], F32) for _ in range(nsb)]
            vfs = [carve([128, 256], F32) for _ in range(nsb)]
            kdup = carve([128, 4, 2, 64], BF16)
            qT = carve([128, 8, 128], BF16)
            sm = [carve([128, 4, 256], F32) for _ in range(2)]
            pp = [carve([128, 4, 256], BF16) for _ in range(2)]
            pT = [carve([128, 8, 128], BF16) for _ in range(2)]
            ob = carve([128, D], BF16)
            rden = carve([128, 16], F32)

            wq0 = w_next()
            wq1 = w_next()
            wkv = w_next()
            for s in range(nsb):
                blk = 16 if is_sample else t * NS + s
                q_r, kf, vf = q_rs[s], kfs[s], vfs[s]
                bq = alloc_ps(2)
                bkv = alloc_ps(1)
                htok = hT.tok(s, s + 1, NS)
                proj_b(bq, hT, htok, s, R, wq0, 512)
                proj_b(bq + 1, hT, htok, s, R, wq1, 512)
                proj_b(bkv, hT, htok, s, R, wkv, 512)
                qps = psf(bq, 2)[:R, :]
                kps = ps[:R, bkv, 0:256]
                vps = ps[:R, bkv, 256:512]
                act(junk.ap[:R, 0:1024], qps, AF.Square, [pstok(bq, 2)], [junk.tok()])
                act(junk.ap[:R, 1024:1280], kps, AF.Square, [pstok(bkv)], [junk.tok()])
                S.add("dve", lambda e: e.tensor_reduce(out=ss20.ap[:R, :],
                                                       in_=junk.ap[:R, :].rearrange("p (h d) -> p h d", d=64),
                                                       axis=AX.X, op=ALU.add), [junk.tok()], [ss20.tok()])
                rsqrt_mean(ss20, R, 20, 1.0 / 64)
                tt("dve", xg.ap[:R, 0:1024].rearrange("p (h d) -> p h d", d=64),
                   qps.rearrange("p (h d) -> p h d", d=64),
                   gq.ap[:R, :].unsqueeze(1).to_broadcast([R, 16, 64]), ALU.mult,
                   [pstok(bq, 2), gq.tok()], [xg.tok()])
                tt("dve", xg.ap[:R, 1024:1280].rearrange("p (h d) -> p h d", d=64),
                   kps.rearrange("p (h d) -> p h d", d=64),
                   gk.ap[:R, :].unsqueeze(1).to_broadcast([R, 4, 64]), ALU.mult,
                   [pstok(bkv), gk.tok()], [xg.tok()])
                cp("act", vf.ap[:R, :], vps, [pstok(bkv)], [vf.tok()])
                xg3 = xg.ap[:R, :].rearrange("p (h d) -> p h d", d=64)
                tq3 = tq.ap[:R, :].rearrange("p (h d) -> p h d", d=64)
                uq3 = uq.ap[:R, :].rearrange("p (h d) -> p h d", d=64)
                tt("dve", tq3, xg3, cc.ap[:R, blk, :].unsqueeze(1).to_broadcast([R, 20, 64]), ALU.mult,
                   [xg.tok(), cc.tok()], [tq.tok()])
                tt("dve", uq3[:, :, 0:32], xg3[:, :, 32:64],
                   sn.ap[:R, blk, 0:32].unsqueeze(1).to_broadcast([R, 20, 32]), ALU.mult,
                   [xg.tok(), sn.tok()], [uq.tok()])
                tt("dve", uq3[:, :, 32:64], xg3[:, :, 0:32],
                   sn.ap[:R, blk, 32:64].unsqueeze(1).to_broadcast([R, 20, 32]), ALU.mult,
                   [xg.tok(), sn.tok()], [uq.tok()])
                tt("dve", tq.ap[:R, :], tq.ap[:R, :], uq.ap[:R, :], ALU.add, [tq.tok(), uq.tok()], [tq.tok()])
                qdst = qf if is_sample else q_r
                tt("dve", qdst.ap[:R, :].rearrange("p (h d) -> p h d", d=64), tq3[:, 0:16, :],
                   ss20.ap[:R, 0:16].unsqueeze(2).to_broadcast([R, 16, 64]), ALU.mult,
                   [tq.tok(), ss20.tok()], [qdst.tok()])
                tt("dve", kf.ap[:R, :].rearrange("p (h d) -> p h d", d=64), tq3[:, 16:20, :],
                   ss20.ap[:R, 16:20].unsqueeze(2).to_broadcast([R, 4, 64]), ALU.mult,
                   [tq.tok(), ss20.tok()], [kf.tok()])

            for s in range(nsb):
                blk = 16 if is_sample else t * NS + s
                q_r, kf, vf = q_rs[s], kfs[s], vfs[s]
                if not is_sample and t == 0:
                    dbg("q%d" % s, q_r)
                    dbg("k%d" % s, kf)
                    dbg("v%d" % s, vf)
                if not is_sample:
                    slot = blk % 2
                    mi = 2 if blk == 0 else (0 if slot == 0 else 1)
                    cp("dve", vaug.ap[:, slot, :, 0:64], vf.ap[:, :].rearrange("p (g d) -> p g d", d=64),
                       [vf.tok()], [vaug.tok(slot, slot + 1, 2)])
                    cp("dve", kdup.ap, kf.ap[:, :].rearrange("p (g d) -> p g d", d=64).unsqueeze(2)
                       .to_broadcast([128, 4, 2, 64]), [kf.tok()], [kdup.tok()])
                    bk = alloc_ps(1)
                    pk = psb(bk).rearrange("p (k t) -> p k t", t=128)
                    for g in range(4):
                        tr(pk[:, g, :], kdup.ap[:, g, :, :].rearrange("p a d -> p (a d)"), ident_b.ap,
                           [kdup.tok(), ident_b.tok()], [pstok(bk)])
                    cp("act", kT.ap[0:64, :, 0, slot, :], pk[0:64, 0:4, :], [pstok(bk)], [kT.tok()])
                    cp("act", kT.ap[64:128, :, 1, slot, :], pk[64:128, 0:4, :], [pstok(bk)], [kT.tok()])
                    bqt = alloc_ps(1)
                    pq = psb(bqt).rearrange("p (k t) -> p k t", t=128)
                    for k in range(8):
                        tr(pq[:, k, :], q_r.ap[:, k * 128:(k + 1) * 128], ident_b.ap,
                           [q_r.tok(), ident_b.tok()], [pstok(bqt)])
                    cp("act", qT.ap, pq, [pstok(bqt)], [qT.tok()])
                    if blk == T // 128 - 1:
                        S.add("sp", lambda e: e.dma_start(out=O["kp"], in_=kf.ap), reads=[kf.tok()], dma=True)
                        S.add("sp", lambda e: e.dma_start(out=O["vp"], in_=vf.ap), reads=[vf.tok()], dma=True)
                    def att_stage1(g):
                        smg, ppg = sm[g % 2], pp[g % 2]
                        bs = alloc_ps(2)
                        for hh in range(4):
                            h = 4 * g + hh
                            qt_ = h // 2
                            mm(ps[:, bs + hh // 2, (hh % 2) * 256:(hh % 2) * 256 + 256],
                               qT.ap[:, qt_, :],
                               kT.ap[:, g, h % 2, :, :].rearrange("p a b -> p (a b)"), True, False,
                               [qT.tok(), kT.tok()], [pstok(bs, 2)])
                            mm(ps[:, bs + hh // 2, (hh % 2) * 256:(hh % 2) * 256 + 256],
                               ident_b.ap, masks_b.ap[:, mi, :], False, True,
                               [ident_b.tok(), masks_b.tok()], [pstok(bs, 2)])
                        sv = psf(bs, 2).rearrange("p (h k) -> p h k", k=256)
                        act(ppg.ap, sv, AF.Exp, [pstok(bs, 2)], [ppg.tok()], scale=0.125)

                    def att_stage2(g):
                        ppg, pTg = pp[g % 2], pT[g % 2]
                        bt_ = alloc_ps(1)
                        pt = psb(bt_).rearrange("p (k t) -> p k t", t=128)
                        for hh in range(4):
                            for c in range(2):
                                tr(pt[:, hh * 2 + c, :], ppg.ap[:, hh, c * 128:(c + 1) * 128], ident_b.ap,
                                   [ppg.tok(), ident_b.tok()], [pstok(bt_)])
                        cp("act", pTg.ap, pt, [pstok(bt_)], [pTg.tok()])
                        bo = alloc_ps(1)
                        ov = ps[:, bo, :].rearrange("p (h e) -> p h e", e=128)
                        for hh in range(4):
                            for c in range(2):
                                mm(ov[:, hh, 0:65], pTg.ap[:, hh * 2 + c, :], vaug.ap[:, c, g, 0:65], c == 0, c == 1,
                                   [pTg.tok(), vaug.tok()], [pstok(bo)])
                        tt("dve", den4.ap, ov[:, :, 64], esink.ap[:, 4 * g:4 * g + 4], ALU.add,
                           [pstok(bo), esink.tok()], [den4.tok()])
                        S.add("dve", lambda e, g=g: e.reciprocal(out=rden.ap[:, 4 * g:4 * g + 4], in_=den4.ap),
                              [den4.tok()], [rden.tok()])
                        tt("dve", ob.ap[:, 256 * g:256 * (g + 1)].rearrange("p (h d) -> p h d", d=64),
                           ov[:, :, 0:64], rden.ap[:, 4 * g:4 * g + 4].unsqueeze(2).to_broadcast([128, 4, 64]),
                           ALU.mult, [pstok(bo), rden.tok()], [ob.tok()])

                    for g in range(5):
                        if g < 4:
                            att_stage1(g)
                        if g >= 1:
                            att_stage2(g - 1)
                else:
                    attention_decode(R, qf, kf, vf, ob, junk, xg, tq)
                if not is_sample and t == 0:
                    dbg("ob%d" % s, ob)
                bot = alloc_ps(1)
                po = psb(bot).rearrange("p (k t) -> p k t", t=128)
                for k in range(8):
                    tr(po[:, k, :R], ob.ap[:R, k * 128:(k + 1) * 128], ident_b.ap[:R, :R],
                       [ob.tok(), ident_b.tok()], [pstok(bot)])
                cp("act", oT.ap[:, :, s * 128:s * 128 + R], po[:, :, :R], [pstok(bot)], [oT.tok(s, s + 1, NS)])

            if stop_after == "B":
                raise StopBuild()
            w_done(3)
            S.phase = "D" + str(t) + ("s" if is_sample else "")
            sg = [carve([128, TT], F32) for _ in range(2)]
            for ct in range(2):
                wao = w_next()
                wga = w_next()
                for j in range(4):
                    cb = ct * 4 + j
                    boa = alloc_ps(1)
                    bga = alloc_ps(1)
                    proj_a(boa, oT, oT.tok(), NTOK, wao, j)
                    proj_a(bga, hT, hT.tok(), NTOK, wga, j)
                    sgb = sg[cb % 2]
                    act(sgb.ap[:, :NTOK], ps[:, bga, 0:NTOK], AF.Sigmoid, [pstok(bga)], [sgb.tok()])
                    tt("dve", mixT.ap[:, cb, :NTOK], ps[:, boa, 0:NTOK], sgb.ap[:, :NTOK], ALU.mult,
                       [pstok(boa), sgb.tok()], [mixT.tok(cb, cb + 1, 8)])
                w_done(2)

            if stop_after == "D":
                raise StopBuild()
            S.phase = "E" + str(t) + ("s" if is_sample else "")
            ar_reset()
            ssm_branch(t, is_sample, R, nsb, NTOK)

            if stop_after == "E":
                raise StopBuild()
            S.phase = "F" + str(t) + ("s" if is_sample else "")
            ar_reset()
            sgf = [carve([128, TT], F32) for _ in range(2)]
            tg = [carve([128, TT], F32) for _ in range(2)]
            for ct in range(2):
                wso0 = w_next()
                wso1 = w_next()
                wgs = w_next()
                for j in range(4):
                    cb = ct * 4 + j
                    bos = alloc_ps(1)
                    bgs = alloc_ps(1)
                    proj_a(bos, onT, onT.tok(), NTOK, wso0, j, k0=0, first=True, last=False)
                    proj_a(bos, onT, onT.tok(), NTOK, wso1, j, k0=8, first=False, last=True)
                    proj_a(bgs, hT, hT.tok(), NTOK, wgs, j)
                    sgb, tgb = sgf[cb % 2], tg[cb % 2]
                    act(sgb.ap[:, :NTOK], ps[:, bgs, 0:NTOK], AF.Sigmoid, [pstok(bgs)], [sgb.tok()])
                    tt("dve", tgb.ap[:, :NTOK], ps[:, bos, 0:NTOK], sgb.ap[:, :NTOK], ALU.mult,
                       [pstok(bos), sgb.tok()], [tgb.tok()])
                    tt("dve", mixT.ap[:, cb, :NTOK], mixT.ap[:, cb, :NTOK], tgb.ap[:, :NTOK], ALU.add,
                       [mixT.tok(cb, cb + 1, 8), tgb.tok()], [mixT.tok(cb, cb + 1, 8)])
                w_done(3)

            S.phase = "G" + str(t) + ("s" if is_sample else "")
            if stop_after == "F":
                raise StopBuild()
            xn2 = carve([128, D], BF16)
            junkg = carve([128, D], F32)
            wo0 = w_next()
            wo1 = w_next()
            bo2s = []
            for s in range(nsb):
                bo2 = alloc_ps(2)
                ps_state["reserved"].update([bo2, bo2 + 1])
                bo2s.append(bo2)
                proj_b(bo2, mixT, mixT.tok(), s, R, wo0, 512)
                proj_b(bo2 + 1, mixT, mixT.tok(), s, R, wo1, 512)
            for s in range(nsb):
                bo2 = bo2s[s]
                ps_state["reserved"].discard(bo2)
                ps_state["reserved"].discard(bo2 + 1)
                tt("dve", xt.ap[:R, s, :], xt.ap[:R, s, :], psf(bo2, 2)[:R, :], ALU.add,
                   [xt.tok(s, s + 1, NS), pstok(bo2, 2)], [xt.tok(s, s + 1, NS)])
                norm_to_hT(s, R, gT2, xn2, junkg)
            w_done(2)

            if stop_after == "G":
                raise StopBuild()
            S.phase = "H" + str(t) + ("s" if is_sample else "")
            aT = carve([128, 32, TT], BF16)
            rl = [carve([128, TT], F32) for _ in range(2)]
            yo = carve([128, NS, D], F32)
            for i in range(8):
                wu = w_next()
                for j in range(4):
                    cb = i * 4 + j
                    bu = alloc_ps(1)
                    proj_a(bu, hT, hT.tok(), NTOK, wu, j)
                    rlb = rl[cb % 2]
                    act(rlb.ap[:, :NTOK], ps[:, bu, 0:NTOK], AF.Relu, [pstok(bu)], [rlb.tok()])
                    tt("dve", aT.ap[:, cb, :NTOK], rlb.ap[:, :NTOK], rlb.ap[:, :NTOK], ALU.mult,
                       [rlb.tok()], [aT.tok(cb, cb + 1, 32)])
                w_done(1)

            if stop_after == "H":
                raise StopBuild()
            S.phase = "I" + str(t) + ("s" if is_sample else "")
            for ct in range(2):
                banks = []
                for s in range(nsb):
                    b = alloc_ps(1)
                    ps_state["reserved"].add(b)
                    banks.append(b)
                for kg in range(4):
                    wd = w_next()
                    for s in range(nsb):
                        proj_b(banks[s], aT, aT.tok(), s, R, wd, 512, kc=8, k0=kg * 8,
                               acc_first=(kg == 0), acc_last=(kg == 3))
                    w_done(1)
                for s in range(nsb):
                    tt("dve", yo.ap[:R, s, ct * 512:(ct + 1) * 512], xt.ap[:R, s, ct * 512:(ct + 1) * 512],
                       ps[:R, banks[s], :], ALU.add,
                       [xt.tok(s, s + 1, NS), pstok(banks[s])], [yo.tok(s, s + 1, NS)])
                    ps_state["reserved"].discard(banks[s])
            for s in range(nsb):
                if is_sample:
                    dst = O["ys"]
                else:
                    dst = O["yp"][t * TT + s * 128: t * TT + (s + 1) * 128, :]
                S.add("sp", lambda e, s=s, dst=dst: e.dma_start(out=dst, in_=yo.ap[:R, s, :]),
                      reads=[yo.tok(s, s + 1, NS)], dma=True)

        def ssm_branch(t, is_sample, R, nsb, NTOK):
            sz = carve([128, NS, 2048], BF16)
            dtv = carve([128, NS, 32], F32)
            xb = carve([128, 32], F32)
            ab = carve([128, 32], F32)
            for i in range(4):
                wz = w_next()
                for s in range(nsb):
                    bz = alloc_ps(1)
                    proj_b(bz, hT, hT.tok(s, s + 1, NS), s, R, wz, 512)
                    act(sz.ap[:R, s, i * 512:(i + 1) * 512], ps[:R, bz, :], AF.Silu, [pstok(bz)],
                        [sz.tok(s, s + 1, NS)])
                w_done(1)
            wdt = w_next()
            for s in range(nsb):
                bd = alloc_ps(1)
                proj_b(bd, hT, hT.tok(s, s + 1, NS), s, R, wdt, 32)
                tt("dve", xb.ap[:R, :], ps[:R, bd, 0:32], dtb.ap[:R, :], ALU.add, [pstok(bd), dtb.tok()], [xb.tok()])
                act(ab.ap[:R, :], xb.ap[:R, :], AF.Abs, [xb.tok()], [ab.tok()])
                act(ab.ap[:R, :], ab.ap[:R, :], AF.Exp, [ab.tok()], [ab.tok()], scale=-1.0)
                act(ab.ap[:R, :], ab.ap[:R, :], AF.Ln, [ab.tok()], [ab.tok()], bias=1.0)
                stt("dve", dtv.ap[:R, s, :], xb.ap[:R, :], 0.0, ab.ap[:R, :], ALU.max, ALU.add,
                    [xb.tok(), ab.tok()], [dtv.tok(s, s + 1, NS)])
            w_done(1)
            if is_sample:
                ssm_sample(R, sz, dtv)
                return
            xs_tok = carve([128, NS, 2048], BF16)
            BT = carve([128, 4, TT], BF16)
            CT = carve([128, 4, TT], BF16)
            xr = [carve([128, TT + 16], F32) for _ in range(3)]
            acc = [carve([128, TT], F32) for _ in range(3)]
            xc8 = carve([128, 8, TT], BF16)
            def conv_tail(c, accb):
                if c < 16:
                    act(xc8.ap[:, c % 8, :], accb.ap, AF.Silu, [accb.tok()], [xc8.tok(c % 8, c % 8 + 1, 8)])
                    if c % 8 == 7:
                        for s in range(NS):
                            bxt = alloc_ps(1)
                            pxt = psb(bxt).rearrange("p (k t) -> p k t", t=128)
                            for cc_ in range(8):
                                tr(pxt[:, cc_, :], xc8.ap[:, cc_, s * 128:(s + 1) * 128], ident_b.ap,
                                   [xc8.tok(), ident_b.tok()], [pstok(bxt)])
                            cp("act", xs_tok.ap[:, s, (c // 8) * 1024:(c // 8 + 1) * 1024],
                               psb(bxt)[:, 0:1024], [pstok(bxt)], [xs_tok.tok(s, s + 1, NS)])
                elif c < 20:
                    act(BT.ap[:, c - 16, :], accb.ap, AF.Silu, [accb.tok()], [BT.tok()])
                else:
                    act(CT.ap[:, c - 20, :], accb.ap, AF.Silu, [accb.tok()], [CT.tok()])

            pend = None
            for i in range(6):
                wx = w_next()
                for j in range(4):
                    c = 4 * i + j
                    bx = alloc_ps(1)
                    proj_a(bx, hT, hT.tok(), TT, wx, j)
                    xrb, accb = xr[c % 3], acc[c % 3]
                    cp("act", xrb.ap[:, 3:3 + TT], ps[:, bx, 0:TT], [pstok(bx)], [xrb.tok()])
                    act(accb.ap, ps[:, bx, 0:TT], AF.Identity, [pstok(bx), convw.tok(), convb.tok()], [accb.tok()],
                        scale=convw.ap[:, 3, c:c + 1], bias=convb.ap[:, c:c + 1])
                    if pend is not None:
                        conv_tail(*pend)
                    cp("dve", xrb.ap[:, 0:3], hal.ap[:, :, c], [hal.tok(c, c + 1, 24)], [xrb.tok()])
                    for i_t in (2, 1, 0):
                        stt("dve", accb.ap, xrb.ap[:, i_t:i_t + TT], convw.ap[:, i_t, c:c + 1], accb.ap,
                            ALU.mult, ALU.add, [xrb.tok(), accb.tok(), convw.tok()], [accb.tok()])
                    cp("dve", hal.ap[:, :, c], xrb.ap[:, TT:TT + 3], [xrb.tok()], [hal.tok(c, c + 1, 24)])
                    pend = (c, accb)
                w_done(1)
            conv_tail(*pend)
            S.phase = "E2_" + str(t)
            dAs = dA_p
            ecds = [carve([128, 64], F32) for _ in range(NS)]
            Btoks = [carve([128, 4, 128], BF16) for _ in range(NS)]
            cbms = [carve([128, 4, 128], BF16) for _ in range(NS)]
            Rg = Rg_p
            Eg = [carve([128, 8, 128], BF16) for _ in range(2)]
            WTg = [carve([128, 8, 128], BF16) for _ in range(2)]
            dtx = [carve([128, 8, 64], BF16) for _ in range(2)]
            tex = [carve([128, 8, 64], BF16) for _ in range(2)]
            t1 = [carve([128, 512], F32) for _ in range(2)]
            t2 = [carve([128, 512], F32) for _ in range(2)]
            yzs = [carve([128, 2048], F32) for _ in range(NS)]
            ss2gs = [ss2g, ss2g_alt]
            ss2s = [ss2, ss2_alt]
            onb = carve([128, 2048], BF16)
            junk2 = carve([128, 512], F32)
            for s in range(NS):
                sl = slice(s * 128, (s + 1) * 128)
                dA, ecd, Btok, cbm = dAs[s], ecds[s], Btoks[s], cbms[s]
                tt("dve", dA.ap, dtv.ap[:, s, :], A_bc.ap, ALU.mult, [dtv.tok(s, s + 1, NS), A_bc.tok()], [dA.tok()])
                bc_ = alloc_ps(1)
                mm(ps[:, bc_, 0:32], tri_r.ap, dA.ap, True, True, [tri_r.tok(), dA.tok()], [pstok(bc_)])
                mm(ps[:, bc_, 32:64], ones_r.ap, dA.ap, True, True, [ones_r.tok(), dA.tok()], [pstok(bc_)])
                act(ecd.ap, ps[:, bc_, 0:64], AF.Exp, [pstok(bc_)], [ecd.tok()])
                bb = alloc_ps(1)
                pbt = psb(bb).rearrange("p (k t) -> p k t", t=128)
                for g in range(4):
                    tr(pbt[:, g, :], BT.ap[:, g, sl], ident_b.ap, [BT.tok(), ident_b.tok()], [pstok(bb)])
                cp("act", Btok.ap, pbt[:, 0:4, :], [pstok(bb)], [Btok.tok()])
                bcb = alloc_ps(1)
                pcb = ps[:, bcb, :].rearrange("p (g l) -> p g l", l=128)
                for g in range(4):
                    mm(pcb[:, g, :], BT.ap[:, g, sl], CT.ap[:, g, sl], True, True, [BT.tok(), CT.tok()], [pstok(bcb)])
                tt("dve", cbm.ap, pcb, tri_f.ap.unsqueeze(1).to_broadcast([128, 4, 128]), ALU.mult,
                   [pstok(bcb), tri_f.tok()], [cbm.tok()])

            for s in range(NS):
                sl = slice(s * 128, (s + 1) * 128)
                dA, ecd, Btok, cbm = dAs[s], ecds[s], Btoks[s], cbms[s]
                yz, ss2g_c = yzs[s], ss2gs[s % 2]
                def ssd_stage1(g, s=s, sl=sl, dA=dA):
                    Rb, Eb = Rg[g % 2], Eg[g % 2]
                    hs = slice(8 * g, 8 * g + 8)
                    tt("dve", Rb.ap, tri_f.ap.unsqueeze(1).to_broadcast([128, 8, 128]),
                       dA.ap[:, hs].bitcast(F32).unsqueeze(2).to_broadcast([128, 8, 128]), ALU.mult,
                       [tri_f.tok(), dA.tok()], [Rb.tok()])
                    bd_ = alloc_ps(2)
                    for hf in range(2):
                        mm(ps[:, bd_ + hf, :], stri_r.ap, Rb.ap[:, 4 * hf:4 * hf + 4, :].rearrange("p a b -> p (a b)"),
                           True, True, [stri_r.tok(), Rb.tok()], [pstok(bd_, 2)])
                    act(Eb.ap.rearrange("p a b -> p (a b)"), psf(bd_, 2), AF.Exp, [pstok(bd_, 2)], [Eb.tok()])

                def ssd_stage2(g, s=s, sl=sl, ecd=ecd, Btok=Btok, cbm=cbm, yz=yz, ss2g=ss2g_c):
                    Rb, Eb, Wb, dxb, txb, t1b, t2b = Rg[g % 2], Eg[g % 2], WTg[g % 2], dtx[g % 2], tex[g % 2], t1[g % 2], t2[g % 2]
                    hs = slice(8 * g, 8 * g + 8)
                    gs_ = slice(512 * g, 512 * (g + 1))
                    tt("dve", Wb.ap, Eb.ap, cbm.ap[:, g, :].unsqueeze(1).to_broadcast([128, 8, 128]), ALU.mult,
                       [Eb.tok(), cbm.tok()], [Wb.tok()])
                    tt("dve", dxb.ap, xs_tok.ap[:, s, gs_].rearrange("p (h d) -> p h d", d=64),
                       dtv.ap[:, s, hs].unsqueeze(2).to_broadcast([128, 8, 64]), ALU.mult,
                       [xs_tok.tok(s, s + 1, NS), dtv.tok(s, s + 1, NS)], [dxb.tok()])
                    tt("dve", txb.ap, dxb.ap, Eb.ap[:, :, 127:128].to_broadcast([128, 8, 64]), ALU.mult,
                       [dxb.tok(), Eb.tok()], [txb.tok()])
                    by = alloc_ps(1)
                    for hh in range(8):
                        mm(ps[:, by, hh * 64:(hh + 1) * 64], Wb.ap[:, hh, :], dxb.ap[:, hh, :], True, True,
                           [Wb.tok(), dxb.tok()], [pstok(by)])
                    byi = alloc_ps(1)
                    mm(ps[:, byi, :], CT.ap[:, g, sl], hTb.ap[:, gs_], True, True,
                       [CT.tok(), hTb.tok(g, g + 1, 4)], [pstok(byi)])
                    tt("dve", t1b.ap.rearrange("p (h d) -> p h d", d=64),
                       ps[:, byi, :].rearrange("p (h d) -> p h d", d=64),
                       ecd.ap[:, hs].unsqueeze(2).to_broadcast([128, 8, 64]), ALU.mult,
                       [pstok(byi), ecd.tok()], [t1b.tok()])
                    tt("dve", t1b.ap, ps[:, by, :], t1b.ap, ALU.add, [pstok(by), t1b.tok()], [t1b.tok()])
                    tt("dve", t2b.ap.rearrange("p (h d) -> p h d", d=64),
                       xs_tok.ap[:, s, gs_].rearrange("p (h d) -> p h d", d=64),
                       dsk.ap[:, hs].unsqueeze(2).to_broadcast([128, 8, 64]), ALU.mult,
                       [xs_tok.tok(s, s + 1, NS), dsk.tok()], [t2b.tok()])
                    tt("dve", t1b.ap, t1b.ap, t2b.ap, ALU.add, [t1b.tok(), t2b.tok()], [t1b.tok()])
                    tt("dve", yz.ap[:, gs_], t1b.ap, sz.ap[:, s, gs_], ALU.mult,
                       [t1b.tok(), sz.tok(s, s + 1, NS)], [yz.tok(g, g + 1, 4)])
                    act(junk2.ap, yz.ap[:, gs_], AF.Square, [yz.tok(g, g + 1, 4)], [junk2.tok(), ss2g.tok()],
                        accum_out=ss2g.ap[:, g:g + 1])
                    bst = alloc_ps(1)
                    mm(ps[:, bst, :], Btok.ap[:, g, :], txb.ap.rearrange("p a b -> p (a b)"), True, True,
                       [Btok.tok(), txb.tok()], [pstok(bst)])
                    tt("dve", hst.ap[:, gs_].rearrange("p (h d) -> p h d", d=64),
                       hst.ap[:, gs_].rearrange("p (h d) -> p h d", d=64),
                       ecd.ap[:, 32 + 8 * g:32 + 8 * g + 8].unsqueeze(2).to_broadcast([128, 8, 64]), ALU.mult,
                       [hst.tok(g, g + 1, 4), ecd.tok()], [hst.tok(g, g + 1, 4)])
                    tt("dve", hst.ap[:, gs_], hst.ap[:, gs_], ps[:, bst, :], ALU.add,
                       [hst.tok(g, g + 1, 4), pstok(bst)], [hst.tok(g, g + 1, 4)])
                    cp("act", hTb.ap[:, gs_], hst.ap[:, gs_], [hst.tok(g, g + 1, 4)], [hTb.tok(g, g + 1, 4)])

                for g in range(5):
                    if g < 4:
                        ssd_stage1(g)
                    if g >= 1:
                        ssd_stage2(g - 1)
            for s in range(NS):
                gated_norm_tail(s, 128, yzs[s], onb, ss2gs[s % 2], ss2s[s % 2])
            if t == NTILE - 1:
                hout = carve([128, 16, 128], F32)
                for q4 in range(4):
                    bh = alloc_ps(4)
                    for k in range(4):
                        tk = q4 * 4 + k
                        tr(ps[:, bh + k, 0:128], hst.ap[:, tk * 128:(tk + 1) * 128], ident_f.ap,
                           [hst.tok(), ident_f.tok()], [pstok(bh, 4)])
                    cp("dve", hout.ap[:, q4 * 4:(q4 + 1) * 4, :], ps[:, bh:bh + 4, 0:128], [pstok(bh, 4)], [hout.tok()])
                S.add("sp", lambda e: e.dma_start(out=O["hp"].rearrange("(t r) n -> r t n", r=128), in_=hout.ap),
                      reads=[hout.tok()], dma=True)
                cst = carve([128, 128], F32)
                bcs = alloc_ps(1)
                tr(ps[0:72, bcs, 0:128], hal.ap.rearrange("p i c -> p (i c)"), ident_f.ap,
                   [hal.tok(), ident_f.tok()], [pstok(bcs)])
                cp("dve", cst.ap[0:72, :], ps[0:72, bcs, 0:128], [pstok(bcs)], [cst.tok()])
                S.add("sp", lambda e: e.dma_start(out=O["cp"].rearrange("i (c p) -> (i c) p", p=128),
                                                  in_=cst.ap[0:72, :]), reads=[cst.tok()], dma=True)

        def gated_norm_tail(s, R, yz, onb, ss2g=ss2g, ss2=ss2):
            S.add("dve", lambda e: e.tensor_reduce(out=ss2.ap[:R, :], in_=ss2g.ap[:R, :], axis=AX.X, op=ALU.add),
                  [ss2g.tok()], [ss2.tok()])
            rsqrt_mean(ss2, R, 1, 1.0 / 2048)
            act(onb.ap[:R, :], yz.ap[:R, :], AF.Copy, [yz.tok(), ss2.tok()], [onb.tok()], scale=ss2.ap[:R, 0:1])
            bn = alloc_ps(2)
            pn = psb(bn, 2).rearrange("p (k t) -> p k t", t=128)
            for k in range(16):
                tr(pn[:, k, :R], onb.ap[:R, k * 128:(k + 1) * 128], ident_b.ap[:R, :R],
                   [onb.tok(), ident_b.tok()], [pstok(bn, 2)])
            tt("dve", onT.ap[:, :, s * 128:s * 128 + R], pn[:, :, :R],
               gsT.ap.unsqueeze(2).to_broadcast([128, 16, R]), ALU.mult,
               [pstok(bn, 2), gsT.tok()], [onT.tok(s, s + 1, NS)])

        def attention_decode(R, qf, kf, vf, ob, junk, xg, tq):
            Kc = [carve([128, 256], F32) for _ in range(2)]
            Vc = [carve([128, 256], F32) for _ in range(2)]
            sel = [carve([128, 128], F32) for _ in range(2)]
            prod = carve([128, 1024], F32)
            pv = [carve([128, 1024], F32) for _ in range(2)]
            sTb = carve([128, 16], F32)
            pTb = [carve([128, 16], F32) for _ in range(2)]
            snew = carve([128, 16], F32)
            pnew = carve([128, 16], F32)
            den16 = carve([128, 16], F32)
            q4 = qf.ap[:R, :].rearrange("p (g h d) -> p g h d", g=4, h=4)
            kb4 = kf.ap[:R, :].rearrange("p (g d) -> p g d", d=64).unsqueeze(2).to_broadcast([R, 4, 4, 64])
            vb4 = vf.ap[:R, :].rearrange("p (g d) -> p g d", d=64).unsqueeze(2).to_broadcast([R, 4, 4, 64])
            tt("dve", xg.ap[:R, 0:1024].rearrange("p (g h d) -> p g h d", g=4, h=4), q4, kb4, ALU.mult,
               [qf.tok(), kf.tok()], [xg.tok()])
            S.add("dve", lambda e: e.tensor_reduce(out=snew.ap[:R, :],
                                                   in_=xg.ap[:R, 0:1024].rearrange("p (h d) -> p h d", d=64),
                                                   axis=AX.X, op=ALU.add), [xg.tok()], [snew.tok()])
            act(pnew.ap[:R, :], snew.ap[:R, :], AF.Exp, [snew.tok()], [pnew.tok()], scale=0.125)
            boacc = alloc_ps(2)
            ps_state["reserved"].update([boacc, boacc + 1])
            bdacc = alloc_ps(1)
            ps_state["reserved"].add(bdacc)
            S.add("sp", lambda e: e.dma_start(out=O["ks"][:, 0:127, :], in_=I["ck"][:, 1:128, :]), dma=True)
            S.add("sp", lambda e: e.dma_start(out=O["vs"][:, 0:127, :], in_=I["cv"][:, 1:128, :]), dma=True)
            S.add("sp", lambda e: e.dma_start(out=O["ks"][:, 127, :], in_=kf.ap[:R, :]), reads=[kf.tok()], dma=True)
            S.add("sp", lambda e: e.dma_start(out=O["vs"][:, 127, :], in_=vf.ap[:R, :]), reads=[vf.tok()], dma=True)
            for b in range(NSMP):
                Kb, Vb, selb, pvb, pTbb = Kc[b % 2], Vc[b % 2], sel[b % 2], pv[b % 2], pTb[b % 2]
                S.add("sp", lambda e, b=b, Kb=Kb: e.dma_start(out=Kb.ap[0:127, :], in_=I["ck"][b, 1:128, :]),
                      writes=[Kb.tok()], dma=True)
                S.add("sp", lambda e, b=b, Vb=Vb: e.dma_start(out=Vb.ap[0:127, :], in_=I["cv"][b, 1:128, :]),
                      writes=[Vb.tok()], dma=True)
                cp("dve", selb.ap[:R, :], ident_f.ap[:R, b:b + 1].to_broadcast([R, 128]), [ident_f.tok()], [selb.tok()])
                bqb = alloc_ps(2)
                for hf in range(2):
                    mm(ps[:, bqb + hf, :], selb.ap[:R, :], qf.ap[:R, hf * 512:(hf + 1) * 512], True, True,
                       [selb.tok(), qf.tok()], [pstok(bqb, 2)])
                tt("dve", prod.ap[0:127, :].rearrange("p (g h d) -> p g h d", g=4, h=4),
                   psf(bqb, 2)[0:127, :].rearrange("p (g h d) -> p g h d", g=4, h=4),
                   Kb.ap[0:127, :].rearrange("p (g d) -> p g d", d=64).unsqueeze(2).to_broadcast([127, 4, 4, 64]),
                   ALU.mult, [pstok(bqb, 2), Kb.tok()], [prod.tok()])
                S.add("dve", lambda e: e.tensor_reduce(out=sTb.ap[0:127, :],
                                                       in_=prod.ap[0:127, :].rearrange("p (h d) -> p h d", d=64),
                                                       axis=AX.X, op=ALU.add), [prod.tok()], [sTb.tok()])
                act(pTbb.ap[0:127, :], sTb.ap[0:127, :], AF.Exp, [sTb.tok()], [pTbb.tok()], scale=0.125)
                tt("dve", pvb.ap[0:127, :].rearrange("p (g h d) -> p g h d", g=4, h=4),
                   Vb.ap[0:127, :].rearrange("p (g d) -> p g d", d=64).unsqueeze(2).to_broadcast([127, 4, 4, 64]),
                   pTbb.ap[0:127, :].rearrange("p (g h) -> p g h", g=4).unsqueeze(3).to_broadcast([127, 4, 4, 64]),
                   ALU.mult, [Vb.tok(), pTbb.tok()], [pvb.tok()])
                for hf in range(2):
                    mm(ps[:R, boacc + hf, :], selT.ap[0:127, b, :], pvb.ap[0:127, hf * 512:(hf + 1) * 512],
                       b == 0, b == NSMP - 1, [selT.tok(), pvb.tok()], [pstok(boacc, 2)])
                mm(ps[:R, bdacc, 0:16], selT.ap[0:127, b, :], pTbb.ap[0:127, :], b == 0, b == NSMP - 1,
                   [selT.tok(), pTbb.tok()], [pstok(bdacc)])
            tt("dve", xg.ap[:R, 0:1024].rearrange("p (g h d) -> p g h d", g=4, h=4), vb4,
               pnew.ap[:R, :].rearrange("p (g h) -> p g h", g=4).unsqueeze(3).to_broadcast([R, 4, 4, 64]), ALU.mult,
               [vf.tok(), pnew.tok()], [xg.tok()])
            tt("dve", xg.ap[:R, 0:1024], xg.ap[:R, 0:1024], psf(boacc, 2)[:R, :], ALU.add,
               [xg.tok(), pstok(boacc, 2)], [xg.tok()])
            tt("dve", den16.ap[:R, :], ps[:R, bdacc, 0:16], pnew.ap[:R, :], ALU.add, [pstok(bdacc), pnew.tok()],
               [den16.tok()])
            tt("dve", den16.ap[:R, :], den16.ap[:R, :], esink.ap[:R, :], ALU.add, [den16.tok(), esink.tok()],
               [den16.tok()])
            S.add("dve", lambda e: e.reciprocal(out=den16.ap[:R, :], in_=den16.ap[:R, :]), [den16.tok()], [den16.tok()])
            tt("dve", ob.ap[:R, :].rearrange("p (h d) -> p h d", d=64),
               xg.ap[:R, 0:1024].rearrange("p (h d) -> p h d", d=64),
               den16.ap[:R, :].unsqueeze(2).to_broadcast([R, 16, 64]), ALU.mult, [xg.tok(), den16.tok()], [ob.tok()])
            for b_ in (boacc, boacc + 1, bdacc):
                ps_state["reserved"].discard(b_)

        def ssm_sample(R, sz, dtv):
            xbc_s = carve([128, 3072], F32)
            xc_s = carve([128, 3072], F32)
            mark = ar["o"]
            cwt = [carve([128, 4, 512], F32) for _ in range(2)]
            stc = [carve([128, 3, 512], F32) for _ in range(2)]
            cbt = [carve([128, 512], F32) for _ in range(2)]
            accs = [carve([128, 512], F32) for _ in range(2)]
            tmpc = [carve([128, 512], F32) for _ in range(2)]
            S.add("sp", lambda e: e.dma_start(out=O["cs"][:, 0:2, :], in_=I["sconv"][:, 1:3, :]), dma=True)
            for i in range(6):
                wx = w_next()
                cw, sc, cb_, ac, tm = cwt[i % 2], stc[i % 2], cbt[i % 2], accs[i % 2], tmpc[i % 2]
                cs_ = slice(i * 512, (i + 1) * 512)
                S.add("sp", lambda e, cw=cw, cs_=cs_: e.dma_start(
                    out=cw.ap[:R, :, :], in_=I["conv_w"][:, cs_].unsqueeze(0).to_broadcast([R, 4, 512])),
                    writes=[cw.tok()], dma=True)
                S.add("sp", lambda e, sc=sc, cs_=cs_: e.dma_start(out=sc.ap[:R, :, :], in_=I["sconv"][:, :, cs_]),
                      writes=[sc.tok()], dma=True)
                S.add("sp", lambda e, cb_=cb_, cs_=cs_: e.dma_start(
                    out=cb_.ap[:R, :], in_=I["conv_b"][0:1, cs_].to_broadcast([R, 512])), writes=[cb_.tok()], dma=True)
                bx = alloc_ps(1)
                proj_b(bx, hT, hT.tok(0, 1, NS), 0, R, wx, 512)
                w_done(1)
                cp("act", xbc_s.ap[:R, cs_], ps[:R, bx, :], [pstok(bx)], [xbc_s.tok()])
                tt("dve", ac.ap[:R, :], ps[:R, bx, :], cw.ap[:R, 3, :], ALU.mult, [pstok(bx), cw.tok()], [ac.tok()])
                tt("dve", ac.ap[:R, :], ac.ap[:R, :], cb_.ap[:R, :], ALU.add, [ac.tok(), cb_.tok()], [ac.tok()])
                for j in range(3):
                    tt("dve", tm.ap[:R, :], sc.ap[:R, j, :], cw.ap[:R, j, :], ALU.mult, [sc.tok(), cw.tok()], [tm.tok()])
                    tt("dve", ac.ap[:R, :], ac.ap[:R, :], tm.ap[:R, :], ALU.add, [ac.tok(), tm.tok()], [ac.tok()])
                act(xc_s.ap[:R, cs_], ac.ap[:R, :], AF.Silu, [ac.tok()], [xc_s.tok()])
            S.add("sp", lambda e: e.dma_start(out=O["cs"][:, 2, :], in_=xbc_s.ap[:R, :]), reads=[xbc_s.tok()], dma=True)
            ar["o"] = mark
            dtx_s = carve([128, 2048], F32)
            decx_s = carve([128, 2048], F32)
            dtxT = carve([128, 16, 16], F32)
            decT = carve([128, 16, 16], F32)
            yT_all = carve([128, 16, 16], F32)
            dA_s = carve([128, 32], F32)
            sel = [carve([128, 128], F32) for _ in range(2)]
            h0t = [carve([128, 16, 128], F32) for _ in range(2)]
            tmp = carve([128, 16, 128], F32)
            tt("dve", dA_s.ap[:R, :], dtv.ap[:R, 0, :], A_bc.ap[:R, :], ALU.mult, [dtv.tok(), A_bc.tok()], [dA_s.tok()])
            act(dA_s.ap[:R, :], dA_s.ap[:R, :], AF.Exp, [dA_s.tok()], [dA_s.tok()])
            tt("dve", dtx_s.ap[:R, :].rearrange("p (h d) -> p h d", d=64),
               xc_s.ap[:R, 0:2048].rearrange("p (h d) -> p h d", d=64),
               dtv.ap[:R, 0, :].unsqueeze(2).to_broadcast([R, 32, 64]), ALU.mult, [xc_s.tok(), dtv.tok()], [dtx_s.tok()])
            cp("dve", decx_s.ap[:R, :].rearrange("p (h d) -> p h d", d=64),
               dA_s.ap[:R, :].unsqueeze(2).to_broadcast([R, 32, 64]), [dA_s.tok()], [decx_s.tok()])
            for src_, dstT in ((dtx_s, dtxT), (decx_s, decT)):
                bt_ = alloc_ps(1)
                pvw = ps[:, bt_, 0:256].rearrange("p (t b) -> p t b", b=16)
                for tk in range(16):
                    tr(pvw[:, tk, :], src_.ap[:R, tk * 128:(tk + 1) * 128], ident_f.ap[:R, :R],
                       [src_.tok(), ident_f.tok()], [pstok(bt_)])
                cp("dve", dstT.ap, pvw, [pstok(bt_)], [dstT.tok()])
            for b in range(NSMP):
                hb, selb = h0t[b % 2], sel[b % 2]
                S.add("sp", lambda e, b=b, hb=hb: e.dma_start(
                    out=hb.ap, in_=I["sssm"][b].rearrange("(t r) n -> r t n", r=128)), writes=[hb.tok()], dma=True)
                cp("dve", selb.ap[:R, :], ident_f.ap[:R, b:b + 1].to_broadcast([R, 128]), [ident_f.tok()], [selb.tok()])
                bB = alloc_ps(1)
                bC = alloc_ps(1)
                mm(ps[:, bB, :], selb.ap[:R, :], xc_s.ap[:R, 2048:2560], True, True, [selb.tok(), xc_s.tok()], [pstok(bB)])
                mm(ps[:, bC, :], selb.ap[:R, :], xc_s.ap[:R, 2560:3072], True, True, [selb.tok(), xc_s.tok()], [pstok(bC)])
                tmp4 = tmp.ap.rearrange("p (g a) n -> p g a n", g=4)
                h4 = hb.ap.rearrange("p (g a) n -> p g a n", g=4)
                tt("dve", tmp4, ps[:, bB, :].rearrange("p (g n) -> p g n", g=4).unsqueeze(2).to_broadcast([128, 4, 4, 128]),
                   dtxT.ap[:, :, b:b + 1].rearrange("p (g a) o -> p g a o", g=4).to_broadcast([128, 4, 4, 128]),
                   ALU.mult, [pstok(bB), dtxT.tok()], [tmp.tok()])
                tt("dve", hb.ap, hb.ap, decT.ap[:, :, b:b + 1].to_broadcast([128, 16, 128]), ALU.mult,
                   [hb.tok(), decT.tok()], [hb.tok()])
                tt("dve", hb.ap, hb.ap, tmp.ap, ALU.add, [hb.tok(), tmp.tok()], [hb.tok()])
                S.add("sp", lambda e, b=b, hb=hb: e.dma_start(
                    out=O["hs"][b].rearrange("(t r) n -> r t n", r=128), in_=hb.ap), reads=[hb.tok()], dma=True)
                tt("dve", tmp4, h4, ps[:, bC, :].rearrange("p (g n) -> p g n", g=4).unsqueeze(2).to_broadcast([128, 4, 4, 128]),
                   ALU.mult, [hb.tok(), pstok(bC)], [tmp.tok()])
                S.add("dve", lambda e, b=b: e.tensor_reduce(out=yT_all.ap[:, :, b], in_=tmp.ap, axis=AX.X, op=ALU.add),
                      [tmp.tok()], [yT_all.tok()])
            by4 = alloc_ps(4)
            for tk in range(16):
                tr(ps[:R, by4 + tk // 4, (tk % 4) * 128:(tk % 4 + 1) * 128], yT_all.ap[:, tk, :], ident_f.ap,
                   [yT_all.tok(), ident_f.tok()], [pstok(by4, 4)])
            ar["o"] = mark
            yz = carve([128, 2048], F32)
            onb = carve([128, 2048], BF16)
            junk3 = carve([128, 2048], BF16)
            tt("dve", yz.ap[:R, :].rearrange("p (h d) -> p h d", d=64),
               xc_s.ap[:R, 0:2048].rearrange("p (h d) -> p h d", d=64),
               dsk.ap[:R, :].unsqueeze(2).to_broadcast([R, 32, 64]), ALU.mult, [xc_s.tok(), dsk.tok()], [yz.tok()])
            tt("dve", yz.ap[:R, :], yz.ap[:R, :], psf(by4, 4)[:R, :], ALU.add, [yz.tok(), pstok(by4, 4)], [yz.tok()])
            tt("dve", yz.ap[:R, :], yz.ap[:R, :], sz.ap[:R, 0, :], ALU.mult, [yz.tok(), sz.tok()], [yz.tok()])
            S.add("dve", lambda e: e.memset(ss2g.ap[:R, :], 0.0), [], [ss2g.tok()])
            act(junk3.ap[:R, :], yz.ap[:R, :], AF.Square, [yz.tok()], [junk3.tok(), ss2g.tok()],
                accum_out=ss2g.ap[:R, 0:1])
            gated_norm_tail(0, R, yz, onb)

        try:
            for t in range(ntile):
                do_tile(t, False)
            if do_sample:
                do_tile(0, True)
        except StopBuild:
            pass
        S.max_ops = None

        run_sched(nc, S)
    return nc


def make_in_maps(inputs):
    consts = host_consts()
    shared = {}
    for n in ("norm_mix", "q_norm", "k_norm", "attn_sinks", "conv_b", "dt_bias", "a_log", "d_skip", "ssm_norm",
              "norm_mlp"):
        shared[n] = np.ascontiguousarray(np.asarray(inputs[n], np.float32).reshape(IN_SHAPES[n]))
    for n in ("w_in", "conv_w", "w_attn_o", "w_ssm_o", "w_out", "w_up", "w_down"):
        shared[n] = np.ascontiguousarray(np.asarray(inputs[n], np.float32)[0])
    for n, v in consts.items():
        shared["c_" + n] = np.ascontiguousarray(v)
    maps = []
    for c in range(NCORES):
        m = dict(shared)
        sl = slice(c * NSMP, (c + 1) * NSMP)
        m["x"] = np.ascontiguousarray(np.asarray(inputs["x_prompt"], np.float32)[c])
        m["xs"] = np.ascontiguousarray(np.asarray(inputs["x_sample"], np.float32)[sl, 0, :])
        m["ck"] = np.ascontiguousarray(np.asarray(inputs["cache_k"], np.float32)[0, sl].reshape(NSMP, 128, 256))
        m["cv"] = np.ascontiguousarray(np.asarray(inputs["cache_v"], np.float32)[0, sl].reshape(NSMP, 128, 256))
        m["sconv"] = np.ascontiguousarray(np.asarray(inputs["state_conv"], np.float32)[0, sl])
        m["sssm"] = np.ascontiguousarray(np.asarray(inputs["state_ssm"], np.float32)[0, sl].reshape(NSMP, 2048, 128))
        maps.append(m)
    return maps


def assemble(results):
    g = lambda n: [np.asarray(r[n], np.float32) for r in results]
    yp = np.stack(g("yp"), 0)
    ys = np.concatenate(g("ys"), 0).reshape(NCORES * NSMP, 1, D)
    kp = np.stack(g("kp"), 0).reshape(1, NCORES, 128, 4, 64)
    vp = np.stack(g("vp"), 0).reshape(1, NCORES, 128, 4, 64)
    cpo = np.stack(g("cp"), 0).reshape(1, NCORES, 3, 3072)
    hp = np.stack(g("hp"), 0).reshape(1, NCORES, 32, 64, 128)
    ks = np.concatenate(g("ks"), 0).reshape(1, NCORES * NSMP, 128, 4, 64)
    vs = np.concatenate(g("vs"), 0).reshape(1, NCORES * NSMP, 128, 4, 64)
    cs = np.concatenate(g("cs"), 0).reshape(1, NCORES * NSMP, 3, 3072)
    hs = np.concatenate(g("hs"), 0).reshape(1, NCORES * NSMP, 32, 64, 128)
    return (yp, ys, kp, vp, cpo, hp, ks, vs, cs, hs)


_PROG = {}


def kernel(**inputs):
    if "nc" not in _PROG:
        _PROG["nc"] = build_program()
    nc = _PROG["nc"]
    maps = make_in_maps(inputs)
    res = run_bass_kernel_spmd(nc, maps, core_ids=list(range(NCORES)))
    return assemble(res.results)
```

```python
import contextlib
import numpy as np
import concourse.bass as bass
import concourse.mybir as mybir
from concourse.bass_utils import run_bass_kernel_spmd

F32 = mybir.dt.float32
BF16 = mybir.dt.bfloat16
F32R = mybir.dt.float32r
AF = mybir.ActivationFunctionType
ALU = mybir.AluOpType
AX = mybir.AxisListType

NCORES = 8
D = 1024
T = 2048
NSMP = 16
NS = 2
TT = NS * 128
NTILE = T // TT
PAST = 8192
EPS = 1e-6
NEG = -1.0e5
IN_DIM = 8736
C_Q, C_K, C_V, C_Z, C_X, C_DT, C_GA, C_GS = 0, 1024, 1280, 1536, 3584, 6656, 6688, 7712
NSLOT = 4

COMPUTE = ("pe", "act", "dve", "pool")
NDSEM = 8


class StopBuild(Exception):
    pass


class Op:
    __slots__ = ("eng", "fn", "dma", "cdeps", "ddeps", "signal", "dsem", "dval", "prev_same", "cidx", "phase")

    def __init__(self, eng, fn, dma):
        self.eng = eng
        self.fn = fn
        self.dma = dma
        self.cdeps = {}
        self.ddeps = []
        self.signal = False
        self.dsem = None
        self.dval = 0
        self.prev_same = None
        self.cidx = -1


class Rec:
    __slots__ = ("lo", "hi", "Wc", "Rc", "Wd", "Rd")

    def __init__(self, lo, hi):
        self.lo = lo
        self.hi = hi
        self.Wc = {}
        self.Rc = {}
        self.Wd = []
        self.Rd = []


class Sched:
    def __init__(self):
        self.queues = {e: [] for e in ("pe", "act", "dve", "pool", "sp")}
        self.ccount = {e: 0 for e in COMPUTE}
        self.cops = {e: [] for e in COMPUTE}
        self.spaces = {}
        self.dma_n = {q: 0 for q in ("sp", "act", "pool")}
        self.dma_last = {}

    def _recs(self, tok):
        sp, lo, hi = tok
        lst = self.spaces.setdefault(sp, [])
        exact = None
        over = []
        for r in lst:
            if r.lo < hi and lo < r.hi:
                over.append(r)
                if r.lo == lo and r.hi == hi:
                    exact = r
        if exact is None:
            exact = Rec(lo, hi)
            lst.append(exact)
            over.append(exact)
        return exact, over

    max_ops = None
    n_ops = 0
    phase = "setup"

    def add(self, eng, fn, reads=(), writes=(), dma=False):
        if self.max_ops is not None and self.n_ops >= self.max_ops:
            raise StopBuild()
        self.n_ops += 1
        op = Op(eng, fn, dma)
        op.phase = self.phase
        cd = op.cdeps
        sp_excl = True
        dd = []
        rrecs = []
        for tok in reads:
            exact, over = self._recs(tok)
            rrecs.append(exact)
            for r in over:
                for e, i in r.Wc.items():
                    if cd.get(e, -1) < i:
                        cd[e] = i
                dd.extend(r.Wd)
                if sp_excl and tok[0] == "ps":
                    for e, i in r.Rc.items():
                        if e != eng and cd.get(e, -1) < i:
                            cd[e] = i
        wrecs = []
        for tok in writes:
            exact, over = self._recs(tok)
            wrecs.append(exact)
            for r in over:
                for e, i in r.Wc.items():
                    if cd.get(e, -1) < i:
                        cd[e] = i
                for e, i in r.Rc.items():
                    if cd.get(e, -1) < i:
                        cd[e] = i
                dd.extend(r.Wd)
                dd.extend(r.Rd)
        seen = set()
        for d in dd:
            if id(d) not in seen:
                seen.add(id(d))
                op.ddeps.append(d)
        if dma:
            k = self.dma_n[eng] % NDSEM
            self.dma_n[eng] += 1
            op.dsem = (eng, k)
            prev = self.dma_last.get((eng, k))
            op.prev_same = prev
            op.dval = (prev.dval if prev is not None else 0) + 16
            self.dma_last[(eng, k)] = op
            for r in rrecs:
                r.Rd.append(op)
            for r in wrecs:
                r.Wd = [op]
                r.Rd = []
        else:
            op.cidx = self.ccount[eng]
            self.ccount[eng] += 1
            self.cops[eng].append(op)
            for r in rrecs:
                r.Rc[eng] = op.cidx
            for r in wrecs:
                r.Wc[eng] = op.cidx
                r.Wd = []
                r.Rd = []
        self.queues[eng].append(op)
        return op

    def finalize(self):
        self.plan = {}
        for q, ops in self.queues.items():
            waited_c = {e: -1 for e in COMPUTE}
            waited_d = set()
            for op in ops:
                waits_c = []
                waits_d = []
                for e, i in op.cdeps.items():
                    if e == q and not op.dma:
                        if q == "pe":
                            continue
                    if waited_c[e] >= i:
                        continue
                    waited_c[e] = i
                    waits_c.append((e, i))
                for d in op.ddeps:
                    if id(d) in waited_d:
                        continue
                    waited_d.add(id(d))
                    waits_d.append(d)
                if op.dma and op.prev_same is not None and id(op.prev_same) not in waited_d:
                    waited_d.add(id(op.prev_same))
                    waits_d.append(op.prev_same)
                self.plan[id(op)] = (waits_c, waits_d)
                for e, i in waits_c:
                    self.cops[e][i].signal = True
        self.sigcount = {}
        for e in COMPUTE:
            c = 0
            arr = []
            for op in self.cops[e]:
                if op.signal:
                    c += 1
                arr.append(c)
            self.sigcount[e] = arr

    def emit_queue(self, q, e, sems_c, sems_d, final_wait=False):
        for op in self.queues[q]:
            waits_c, waits_d = self.plan[id(op)]
            for (pe_, i) in waits_c:
                e.wait_ge(sems_c[pe_], self.sigcount[pe_][i])
            for d in waits_d:
                e.wait_ge(sems_d[d.dsem], d.dval)
            ins = op.fn(e)
            if op.dma:
                ins.then_inc(sems_d[op.dsem], 16)
            elif op.signal:
                ins.then_inc(sems_c[q], 1)
        if final_wait:
            for key, d in self.dma_last.items():
                e.wait_ge(sems_d[key], d.dval)


def run_sched(nc, S):
    S.finalize()
    with contextlib.ExitStack() as st:
        sems_c = {e: st.enter_context(nc.semaphore("c_" + e)) for e in COMPUTE}
        sems_d = {}
        for q in ("sp", "act", "pool"):
            for k in range(NDSEM):
                sems_d[(q, k)] = st.enter_context(nc.semaphore("d_%s%d" % (q, k)))
        block = st.enter_context(nc.Block())

        @block.sync
        def _(e):
            S.emit_queue("sp", e, sems_c, sems_d, final_wait=True)

        @block.scalar
        def _(e):
            S.emit_queue("act", e, sems_c, sems_d)

        @block.vector
        def _(e):
            S.emit_queue("dve", e, sems_c, sems_d)

        @block.gpsimd
        def _(e):
            S.emit_queue("pool", e, sems_c, sems_d)

        @block.tensor
        def _(e):
            S.emit_queue("pe", e, sems_c, sems_d)


class Buf:
    def __init__(self, ap, space, lo, hi):
        self.ap = ap
        self.space = space
        self.lo = lo
        self.hi = hi

    def tok(self, a=None, b=None, n=None):
        if a is None:
            return (self.space, self.lo, self.hi)
        w = (self.hi - self.lo) // n
        return (self.space, self.lo + a * w, self.lo + b * w)


def host_consts():
    c = {}
    c["ident"] = np.eye(128, dtype=np.float32)
    k = np.arange(128)
    c["tri"] = (k[:, None] <= k[None, :]).astype(np.float32)
    c["stri"] = (k[None, :] < k[:, None]).astype(np.float32)
    cur = np.where(k[None, :] <= k[:, None], 0.0, NEG).astype(np.float32)
    prv = np.where(k[None, :] > k[:, None], 0.0, NEG).astype(np.float32)
    neg = np.full((128, 128), NEG, np.float32)
    c["masks"] = np.stack([np.concatenate([cur, prv], 1), np.concatenate([prv, cur], 1),
                           np.concatenate([cur, neg], 1)], 0).transpose(1, 0, 2).copy()
    half = 32
    inv = (10000.0 ** (-np.arange(half, dtype=np.float32) / half)).astype(np.float32)
    pos = np.concatenate([np.arange(T, dtype=np.float32).reshape(16, 128).T,
                          np.full((128, 1), float(PAST), np.float32)], 1)
    ang = pos[:, :, None] * inv[None, None, :]
    cos = np.cos(ang).astype(np.float32)
    sin = np.sin(ang).astype(np.float32)
    c["cc"] = np.concatenate([cos, cos], -1).astype(np.float32)
    c["sn"] = np.concatenate([-sin, sin], -1).astype(np.float32)
    selT = np.zeros((128, 16, 16), np.float32)
    for b in range(16):
        selT[:, b, b] = 1.0
    c["selT"] = selT
    return c


CONST_SHAPES = {"ident": [128, 128], "tri": [128, 128], "stri": [128, 128], "masks": [128, 3, 256],
                "cc": [128, 17, 64], "sn": [128, 17, 64], "selT": [128, 16, 16]}

IN_SHAPES = {
    "x": [T, D], "xs": [NSMP, D], "ck": [NSMP, 128, 256], "cv": [NSMP, 128, 256],
    "sconv": [NSMP, 3, 3072], "sssm": [NSMP, 2048, 128],
    "norm_mix": [1, D], "w_in": [D, IN_DIM], "q_norm": [1, 64], "k_norm": [1, 64], "attn_sinks": [1, 16],
    "conv_w": [4, 3072], "conv_b": [1, 3072], "dt_bias": [1, 32], "a_log": [1, 32], "d_skip": [1, 32],
    "ssm_norm": [1, 2048], "w_attn_o": [D, D], "w_ssm_o": [2048, D], "w_out": [D, D], "norm_mlp": [1, D],
    "w_up": [D, 4096], "w_down": [4096, D],
}
OUT_SHAPES = {
    "yp": [T, D], "ys": [NSMP, D], "kp": [128, 256], "vp": [128, 256], "cp": [3, 3072], "hp": [2048, 128],
    "ks": [NSMP, 128, 256], "vs": [NSMP, 128, 256], "cs": [NSMP, 3, 3072], "hs": [NSMP, 2048, 128],
}


DEBUG = []


def build_program(do_sample=True, ntile=NTILE, debug=False, stop_after=None, max_ops=None):
    nc = bass.Bass("TRN2", target_bir_lowering=False)
    I = {n: nc.dram_tensor(n, s, F32, kind="ExternalInput").ap() for n, s in IN_SHAPES.items()}
    CI = {n: nc.dram_tensor("c_" + n, s, F32, kind="ExternalInput").ap() for n, s in CONST_SHAPES.items()}
    O = {n: nc.dram_tensor(n, s, F32, kind="ExternalOutput").ap() for n, s in OUT_SHAPES.items()}
    S = Sched()
    S.max_ops = max_ops

    with contextlib.ExitStack() as st:
        def sbt(name, shape, dt):
            return st.enter_context(nc.sbuf_tensor(name, shape, dt))

        def pbuf(name, shape, dt):
            return Buf(sbt(name, shape, dt)[:], name, 0, 4096)

        def dbg(name, buf, ap=None):
            if not debug:
                return
            ap = ap if ap is not None else buf.ap
            d = nc.dram_tensor("dbg_" + name, list(ap.shape), ap.dtype, kind="ExternalOutput").ap()
            DEBUG.append("dbg_" + name)
            S.add("sp", lambda e: e.dma_start(out=d, in_=ap), reads=[buf.tok()], dma=True)

        ps = st.enter_context(nc.psum_tensor("ps", [128, 8, 512], F32))
        ps_state = {"next": 0, "reserved": set()}

        def alloc_ps(n=1):
            while True:
                b = ps_state["next"]
                if b % n != 0:
                    b += n - (b % n)
                if b + n > 8:
                    b = 0
                ps_state["next"] = (b + n) % 8
                if all((b + i) not in ps_state["reserved"] for i in range(n)):
                    return b

        def pstok(b, n=1):
            return ("ps", b, b + n)

        def psf(b, n=1):
            if n == 1:
                return ps[:, b, :]
            return ps[:, b:b + n, :].rearrange("p b c -> p (b c)")

        def psb(b, n=1):
            return psf(b, n).bitcast(BF16)

        ident_f = pbuf("ident_f", [128, 128], F32)
        ident_b = pbuf("ident_b", [128, 128], BF16)
        tri_f = pbuf("tri_f", [128, 128], F32)
        tri_r = pbuf("tri_r", [128, 128], F32R)
        stri_f = pbuf("stri_f", [128, 128], F32)
        stri_r = pbuf("stri_r", [128, 128], F32R)
        ones_r = pbuf("ones_r", [128, 128], F32R)
        masks = pbuf("masks", [128, 3, 256], F32)
        masks_b = pbuf("masks_b", [128, 3, 256], BF16)
        cc = pbuf("cc", [128, 17, 64], F32)
        sn = pbuf("sn", [128, 17, 64], F32)
        selT = pbuf("selT", [128, 16, 16], F32)
        cst_all = pbuf("cst_all", [128, 152], F32)
        stgA = pbuf("stgA", [128, 128], F32)
        stgB = pbuf("stgB", [128, 128], F32)

        def cview(lo, hi, shape3=None):
            ap = cst_all.ap[:, lo:hi]
            if shape3 is not None:
                ap = ap.rearrange("p (a b) -> p a b", b=shape3)
            return Buf(ap, "cst_all", 0, 4096)

        gT1 = cview(0, 8)
        gT2 = cview(8, 16)
        gsT = cview(16, 32)
        convb = cview(32, 56)
        convw = cview(56, 152, 24)
        gq = pbuf("gq", [128, 64], F32)
        gk = pbuf("gk", [128, 64], F32)
        esink = pbuf("esink", [128, 16], F32)
        dtb = pbuf("dtb", [128, 32], F32)
        A_bc = pbuf("A_bc", [128, 32], F32)
        dsk = pbuf("dsk", [128, 32], F32)

        def dma_in(buf, src, q="sp"):
            S.add(q, lambda e: e.dma_start(out=buf.ap, in_=src), writes=[buf.tok()], dma=True)

        def dma_in_slow(buf, src):
            S.add("sp", lambda e: e.dma_start(out=buf.ap, in_=src, allow_slow_non_contiguous=True),
                  writes=[buf.tok()], dma=True)

        dma_in(ident_f, CI["ident"])
        dma_in(tri_f, CI["tri"])
        dma_in(stri_f, CI["stri"])
        dma_in(masks, CI["masks"])
        dma_in(cc, CI["cc"])
        dma_in(sn, CI["sn"])
        dma_in(selT, CI["selT"])
        S.add("sp", lambda e: e.dma_start(out=stgA.ap[0:8, :], in_=I["norm_mix"].rearrange("o (k p) -> (o k) p", p=128)),
              writes=[stgA.tok()], dma=True)
        S.add("sp", lambda e: e.dma_start(out=stgA.ap[8:16, :], in_=I["norm_mlp"].rearrange("o (k p) -> (o k) p", p=128)),
              writes=[stgA.tok()], dma=True)
        S.add("sp", lambda e: e.dma_start(out=stgA.ap[16:32, :], in_=I["ssm_norm"].rearrange("o (k p) -> (o k) p", p=128)),
              writes=[stgA.tok()], dma=True)
        S.add("sp", lambda e: e.dma_start(out=stgA.ap[32:56, :], in_=I["conv_b"].rearrange("o (k p) -> (o k) p", p=128)),
              writes=[stgA.tok()], dma=True)
        S.add("sp", lambda e: e.dma_start(out=stgB.ap[0:96, :], in_=I["conv_w"].rearrange("i (c p) -> (i c) p", p=128)),
              writes=[stgB.tok()], dma=True)
        S.add("pe", lambda e: e.transpose(out=ps[:, 0, 0:56], in_=stgA.ap[0:56, :], identity=ident_f.ap[0:56, 0:56]),
              [stgA.tok(), ident_f.tok()], [("ps", 0, 1)])
        S.add("pe", lambda e: e.transpose(out=ps[:, 0, 56:152], in_=stgB.ap[0:96, :], identity=ident_f.ap[0:96, 0:96]),
              [stgB.tok(), ident_f.tok()], [("ps", 0, 1)])
        S.add("dve", lambda e: e.tensor_copy(out=cst_all.ap, in_=ps[:, 0, 0:152]), [("ps", 0, 1)], [cst_all.tok()])
        dma_in(gq, I["q_norm"][0:1, :].to_broadcast([128, 64]))
        dma_in(gk, I["k_norm"][0:1, :].to_broadcast([128, 64]))
        dma_in(esink, I["attn_sinks"][0:1, :].to_broadcast([128, 16]))
        dma_in(dtb, I["dt_bias"][0:1, :].to_broadcast([128, 32]))
        dma_in(A_bc, I["a_log"][0:1, :].to_broadcast([128, 32]))
        dma_in(dsk, I["d_skip"][0:1, :].to_broadcast([128, 32]))

        S.add("dve", lambda e: e.tensor_copy(out=ident_b.ap, in_=ident_f.ap), [ident_f.tok()], [ident_b.tok()])
        S.add("dve", lambda e: e.tensor_copy(out=masks_b.ap, in_=masks.ap), [masks.tok()], [masks_b.tok()])
        S.add("dve", lambda e: e.tensor_copy(out=tri_r.ap, in_=tri_f.ap), [tri_f.tok()], [tri_r.tok()])
        S.add("dve", lambda e: e.tensor_copy(out=stri_r.ap, in_=stri_f.ap), [stri_f.tok()], [stri_r.tok()])
        S.add("dve", lambda e: e.tensor_scalar(out=ones_r.ap, in0=tri_f.ap, scalar1=0.0, scalar2=1.0,
                                               op0=ALU.mult, op1=ALU.add), [tri_f.tok()], [ones_r.tok()])
        S.add("act", lambda e: e.activation(out=esink.ap, in_=esink.ap, func=AF.Exp), [esink.tok()], [esink.tok()])
        S.add("act", lambda e: e.activation(out=A_bc.ap, in_=A_bc.ap, func=AF.Exp), [A_bc.tok()], [A_bc.tok()])
        S.add("dve", lambda e: e.tensor_scalar(out=A_bc.ap, in0=A_bc.ap, scalar1=-1.0, scalar2=None, op0=ALU.mult),
              [A_bc.tok()], [A_bc.tok()])

        xt = pbuf("xt", [128, NS, D], F32)
        hT = pbuf("hT", [128, 8, TT], BF16)
        mixT = pbuf("mixT", [128, 8, TT], BF16)
        onT = pbuf("onT", [128, 16, TT], BF16)
        oT = pbuf("oT", [128, 8, TT], BF16)
        hst = pbuf("hst", [128, 2048], F32)
        hTb = pbuf("hTb", [128, 2048], BF16)
        kT = pbuf("kT", [128, 4, 2, 2, 128], BF16)
        vaug = pbuf("vaug", [128, 2, 4, 80], BF16)
        hal = Buf(sbt("hal", [128, 3, 24], F32)[:], "hal", 0, 24 * 64)
        small = pbuf("small", [128, 256], F32)
        dA_p = [pbuf("dA%d" % i, [128, 32], F32R) for i in range(NS)]
        Rg_p = [pbuf("Rg%d" % i, [128, 8, 128], F32R) for i in range(2)]

        S.add("dve", lambda e: e.memset(hst.ap, 0.0), [], [hst.tok()])
        S.add("dve", lambda e: e.memset(hTb.ap, 0.0), [], [hTb.tok()])
        S.add("dve", lambda e: e.memset(kT.ap, 0.0), [], [kT.tok()])
        S.add("dve", lambda e: e.memset(vaug.ap, 1.0), [], [vaug.tok()])
        S.add("dve", lambda e: e.memset(hal.ap, 0.0), [], [hal.tok()])

        _sm = {"o": 0}

        def smalloc(n):
            o = _sm["o"]
            _sm["o"] += n
            assert _sm["o"] <= 256
            return Buf(small.ap[:, o:o + n], "small", o * 16, (o + n) * 16)

        wslots = [pbuf("w%d" % i, [128, 8, 512], BF16) for i in range(NSLOT)]
        wstate = {"issued": 0, "rel": 0, "pos": 0}

        def wsrc(name, r0, c0, ncol):
            return (I[name][r0:r0 + 1024, c0:c0 + ncol].rearrange("(k p) c -> p k c", p=128), ncol)

        tile_seq = []
        tile_seq += [wsrc("w_in", 0, C_Q, 512), wsrc("w_in", 0, C_Q + 512, 512), wsrc("w_in", 0, C_K, 512)]
        tile_seq += [wsrc("w_attn_o", 0, 0, 512), wsrc("w_in", 0, C_GA, 512),
                     wsrc("w_attn_o", 0, 512, 512), wsrc("w_in", 0, C_GA + 512, 512)]
        tile_seq += [wsrc("w_in", 0, C_Z + 512 * i, 512) for i in range(4)]
        tile_seq += [wsrc("w_in", 0, C_DT, 32)]
        tile_seq += [wsrc("w_in", 0, C_X + 512 * i, 512) for i in range(6)]
        for ct in range(2):
            tile_seq += [wsrc("w_ssm_o", 0, 512 * ct, 512), wsrc("w_ssm_o", 1024, 512 * ct, 512),
                         wsrc("w_in", 0, C_GS + 512 * ct, 512)]
        tile_seq += [wsrc("w_out", 0, 0, 512), wsrc("w_out", 0, 512, 512)]
        tile_seq += [wsrc("w_up", 0, 512 * i, 512) for i in range(8)]
        for ct in range(2):
            tile_seq += [wsrc("w_down", 1024 * kg, 512 * ct, 512) for kg in range(4)]
        NWT = len(tile_seq)
        n_pass = ntile + (1 if do_sample else 0)
        total_w = NWT * n_pass

        wscr = nc.dram_tensor("wscr", [NWT, 128, 4096], BF16, kind="Internal").ap()

        def w_issue():
            i = wstate["issued"]
            if i >= total_w:
                return
            j = i % NWT
            src, ncol = tile_seq[j]
            slot = wslots[i % NSLOT]
            scr = wscr[j].rearrange("p (k c) -> p k c", c=512)[:, :, 0:ncol]
            full = (ncol == 512)
            scr2 = wscr[j].bitcast(F32)
            slot2 = slot.ap.rearrange("p k c -> p (k c)").bitcast(F32)
            if i < NWT:
                S.add("pool", lambda e: e.dma_start(out=slot.ap[:, :, 0:ncol], in_=src), writes=[slot.tok()], dma=True)
                if n_pass > 1:
                    if full:
                        S.add("sp", lambda e: e.dma_start(out=scr2, in_=slot2), reads=[slot.tok()],
                              writes=[("wscr", j, j + 1)], dma=True)
                    else:
                        S.add("sp", lambda e: e.dma_start(out=scr, in_=slot.ap[:, :, 0:ncol]), reads=[slot.tok()],
                              writes=[("wscr", j, j + 1)], dma=True)
            else:
                if full:
                    S.add("pool", lambda e: e.dma_start(out=slot2, in_=scr2), reads=[("wscr", j, j + 1)],
                          writes=[slot.tok()], dma=True)
                else:
                    S.add("pool", lambda e: e.dma_start(out=slot.ap[:, :, 0:ncol], in_=scr), reads=[("wscr", j, j + 1)],
                          writes=[slot.tok()], dma=True)
            wstate["issued"] += 1

        def w_next():
            i = wstate["pos"]
            wstate["pos"] += 1
            while wstate["issued"] < min(total_w, wstate["rel"] + NSLOT):
                w_issue()
            assert wstate["issued"] > i or i >= total_w, "weight ring: holding too many tiles"
            return wslots[i % NSLOT]

        def w_done(n=1):
            wstate["rel"] += n
            while wstate["issued"] < min(total_w, wstate["rel"] + NSLOT):
                w_issue()

        ARW = 24576
        arena_t = sbt("arena", [128, ARW], F32)
        ar = {"o": 0}

        def ar_reset():
            ar["o"] = 0

        def carve(shape, dt, parts=128):
            n = int(np.prod(shape[1:]))
            words = n if dt in (F32, F32R) else (n + 1) // 2
            words = (words + 15) // 16 * 16
            lo = ar["o"]
            ar["o"] += words
            assert ar["o"] <= ARW, ("arena overflow", ar["o"])
            ap = arena_t[:, lo:lo + words]
            if dt == BF16:
                ap = ap.bitcast(BF16)[:, 0:n]
            elif dt == F32R:
                ap = ap.bitcast(F32R)[:, 0:n]
            else:
                ap = ap[:, 0:n]
            if len(shape) == 3:
                ap = ap.rearrange("p (a b) -> p a b", b=shape[2])
            elif len(shape) == 4:
                ap = ap.rearrange("p (a b c) -> p a b c", b=shape[2], c=shape[3])
            return Buf(ap, "arena", lo, lo + words)

        def mm(out, lhsT, rhs, start, stop, r, w):
            S.add("pe", lambda e: e.matmul(out, lhsT=lhsT, rhs=rhs, start=start, stop=stop), r, w)

        def tr(out, in_, ident, r, w):
            S.add("pe", lambda e: e.transpose(out=out, in_=in_, identity=ident), r, w)

        def tt(eng, out, in0, in1, op, r, w):
            S.add(eng, lambda e: e.tensor_tensor(out=out, in0=in0, in1=in1, op=op), r, w)

        def ts(eng, out, in0, s1, s2, op0, op1, r, w):
            if op1 is None:
                S.add(eng, lambda e: e.tensor_scalar(out=out, in0=in0, scalar1=s1, scalar2=None, op0=op0), r, w)
            else:
                S.add(eng, lambda e: e.tensor_scalar(out=out, in0=in0, scalar1=s1, scalar2=s2, op0=op0, op1=op1), r, w)

        def stt(eng, out, in0, scalar, in1, op0, op1, r, w):
            S.add(eng, lambda e: e.scalar_tensor_tensor(out=out, in0=in0, scalar=scalar, in1=in1, op0=op0, op1=op1),
                  r, w)

        def act(out, in_, func, r, w, bias=None, scale=None, accum_out=None):
            kw = {}
            if bias is not None:
                kw["bias"] = bias
            if scale is not None:
                kw["scale"] = scale
            if accum_out is not None:
                kw["accum_out"] = accum_out
            S.add("act", lambda e: e.activation(out=out, in_=in_, func=func, **kw), r, w)

        def cp(eng, out, in_, r, w):
            if eng == "act":
                S.add("act", lambda e: e.activation(out=out, in_=in_, func=AF.Copy), r, w)
            else:
                S.add(eng, lambda e: e.tensor_copy(out=out, in_=in_), r, w)

        def rsqrt_mean(stat, R, n, inv_n):
            act(stat.ap[:R, 0:n], stat.ap[:R, 0:n], AF.Sqrt, [stat.tok()], [stat.tok()], bias=EPS, scale=inv_n)
            S.add("dve", lambda e: e.reciprocal(out=stat.ap[:R, 0:n], in_=stat.ap[:R, 0:n]), [stat.tok()], [stat.tok()])

        ss1 = smalloc(NS)
        ss20 = smalloc(20)
        ss2g = smalloc(4)
        ss2 = smalloc(1)
        ss2g_alt = smalloc(4)
        ss2_alt = smalloc(1)
        den4 = smalloc(4)

        def norm_to_hT(s, R, gT, xn, junk):
            xs_ = xt.ap[:R, s, :]
            act(junk.ap[:R, 0:D], xs_, AF.Square, [xt.tok(s, s + 1, NS)], [junk.tok(), ss1.tok()],
                accum_out=ss1.ap[:R, s:s + 1])
            act(ss1.ap[:R, s:s + 1], ss1.ap[:R, s:s + 1], AF.Sqrt, [ss1.tok()], [ss1.tok()], bias=EPS, scale=1.0 / D)
            S.add("dve", lambda e: e.reciprocal(out=ss1.ap[:R, s:s + 1], in_=ss1.ap[:R, s:s + 1]),
                  [ss1.tok()], [ss1.tok()])
            ts("dve", xn.ap[:R, :], xs_, ss1.ap[:R, s:s + 1], None, ALU.mult, None,
               [xt.tok(s, s + 1, NS), ss1.tok()], [xn.tok()])
            b = alloc_ps(1)
            pv = psb(b).rearrange("p (k t) -> p k t", t=128)
            for k in range(8):
                tr(pv[:, k, :R], xn.ap[:R, k * 128:(k + 1) * 128], ident_b.ap[:R, :R],
                   [xn.tok(), ident_b.tok()], [pstok(b)])
            tt("dve", hT.ap[:, :, s * 128:s * 128 + R], pv[:, :, :R],
               gT.ap.unsqueeze(2).to_broadcast([128, 8, R]), ALU.mult,
               [pstok(b), gT.tok()], [hT.tok(s, s + 1, NS)])

        def proj_b(bank, actT, tokr, s, R, wb, ncol, kc=8, k0=0, ktot=None, acc_first=True, acc_last=True, wk0=0):
            for k in range(kc):
                mm(ps[:R, bank, 0:ncol], actT.ap[:, k0 + k, s * 128:s * 128 + R], wb.ap[:, wk0 + k, 0:ncol],
                   acc_first and k == 0, acc_last and k == kc - 1,
                   [tokr, wb.tok()], [pstok(bank)])

        def proj_a(bank, actT, tokr, NTOK, wb, j, kc=8, k0=0, first=True, last=True):
            for k in range(kc):
                mm(ps[:, bank, 0:NTOK], wb.ap[:, k, j * 128:(j + 1) * 128], actT.ap[:, k0 + k, 0:NTOK],
                   first and k == 0, last and k == kc - 1,
                   [tokr, wb.tok()], [pstok(bank)])

        def do_tile(t, is_sample):
            R = NSMP if is_sample else 128
            nsb = 1 if is_sample else NS
            NTOK = R * nsb
            S.phase = "A" + str(t) + ("s" if is_sample else "")
            ar_reset()
            xn = carve([128, D], BF16)
            junk = carve([128, 1280], F32)
            for s in range(nsb):
                if is_sample:
                    src = I["xs"]
                else:
                    src = I["x"][t * TT + s * 128: t * TT + (s + 1) * 128, :]
                S.add("sp", lambda e, s=s, src=src: e.dma_start(out=xt.ap[:R, s, :], in_=src),
                      writes=[xt.tok(s, s + 1, NS)], dma=True)
                norm_to_hT(s, R, gT1, xn, junk)

            if stop_after == "A":
                raise StopBuild()
            S.phase = "B" + str(t) + ("s" if is_sample else "")
            xg = carve([128, 1280], F32)
            tq = carve([128, 1280], F32)
            uq = carve([128, 1280], F32)
            q_rs = [carve([128, D], BF16) for _ in range(nsb)]
            qf = carve([128, D], F32) if is_sample else None
            kfs = [carve([128, 256], F32) for _ in range(nsb)]
            vfs = [carve([128, 256], F32) for _ in range(nsb)]
            kdup = carve([128, 4, 2, 64], BF16)
            qT = carve([128, 8, 128], BF16)
            sm = [carve([128, 4, 256], F32) for _ in range(2)]
            pp = [carve([128, 4, 256], BF16) for _ in range(2)]
            pT = [carve([128, 8, 128], BF16) for _ in range(2)]
            obs = [carve([128, D], BF16) for _ in range(nsb)]
            rden = carve([128, 16], F32)

            wq0 = w_next()
            wq1 = w_next()
            wkv = w_next()
            for s in range(nsb):
                blk = 16 if is_sample else t * NS + s
                q_r, kf, vf = q_rs[s], kfs[s], vfs[s]
                bq = alloc_ps(2)
                bkv = alloc_ps(1)
                htok = hT.tok(s, s + 1, NS)
                proj_b(bq, hT, htok, s, R, wq0, 512)
                proj_b(bq + 1, hT, htok, s, R, wq1, 512)
                proj_b(bkv, hT, htok, s, R, wkv, 512)
                qps = psf(bq, 2)[:R, :]
                kps = ps[:R, bkv, 0:256]
                vps = ps[:R, bkv, 256:512]
                act(junk.ap[:R, 0:1024], qps, AF.Square, [pstok(bq, 2)], [junk.tok()])
                act(junk.ap[:R, 1024:1280], kps, AF.Square, [pstok(bkv)], [junk.tok()])
                S.add("dve", lambda e: e.tensor_reduce(out=ss20.ap[:R, :],
                                                       in_=junk.ap[:R, :].rearrange("p (h d) -> p h d", d=64),
                                                       axis=AX.X, op=ALU.add), [junk.tok()], [ss20.tok()])
                rsqrt_mean(ss20, R, 20, 1.0 / 64)
                tt("dve", xg.ap[:R, 0:1024].rearrange("p (h d) -> p h d", d=64),
                   qps.rearrange("p (h d) -> p h d", d=64),
                   gq.ap[:R, :].unsqueeze(1).to_broadcast([R, 16, 64]), ALU.mult,
                   [pstok(bq, 2), gq.tok()], [xg.tok()])
                tt("dve", xg.ap[:R, 1024:1280].rearrange("p (h d) -> p h d", d=64),
                   kps.rearrange("p (h d) -> p h d", d=64),
                   gk.ap[:R, :].unsqueeze(1).to_broadcast([R, 4, 64]), ALU.mult,
                   [pstok(bkv), gk.tok()], [xg.tok()])
                cp("act", vf.ap[:R, :], vps, [pstok(bkv)], [vf.tok()])
                xg3 = xg.ap[:R, :].rearrange("p (h d) -> p h d", d=64)
                tq3 = tq.ap[:R, :].rearrange("p (h d) -> p h d", d=64)
                uq3 = uq.ap[:R, :].rearrange("p (h d) -> p h d", d=64)
                tt("dve", tq3, xg3, cc.ap[:R, blk, :].unsqueeze(1).to_broadcast([R, 20, 64]), ALU.mult,
                   [xg.tok(), cc.tok()], [tq.tok()])
                tt("dve", uq3[:, :, 0:32], xg3[:, :, 32:64],
                   sn.ap[:R, blk, 0:32].unsqueeze(1).to_broadcast([R, 20, 32]), ALU.mult,
                   [xg.tok(), sn.tok()], [uq.tok()])
                tt("dve", uq3[:, :, 32:64], xg3[:, :, 0:32],
                   sn.ap[:R, blk, 32:64].unsqueeze(1).to_broadcast([R, 20, 32]), ALU.mult,
                   [xg.tok(), sn.tok()], [uq.tok()])
                tt("dve", tq.ap[:R, :], tq.ap[:R, :], uq.ap[:R, :], ALU.add, [tq.tok(), uq.tok()], [tq.tok()])
                qdst = qf if is_sample else q_r
                tt("dve", qdst.ap[:R, :].rearrange("p (h d) -> p h d", d=64), tq3[:, 0:16, :],
                   ss20.ap[:R, 0:16].unsqueeze(2).to_broadcast([R, 16, 64]), ALU.mult,
                   [tq.tok(), ss20.tok()], [qdst.tok()])
                tt("dve", kf.ap[:R, :].rearrange("p (h d) -> p h d", d=64), tq3[:, 16:20, :],
                   ss20.ap[:R, 16:20].unsqueeze(2).to_broadcast([R, 4, 64]), ALU.mult,
                   [tq.tok(), ss20.tok()], [kf.tok()])

            for s in range(nsb):
                blk = 16 if is_sample else t * NS + s
                q_r, kf, vf = q_rs[s], kfs[s], vfs[s]
                ob = obs[s]
                if not is_sample and t == 0:
                    dbg("q%d" % s, q_r)
                    dbg("k%d" % s, kf)
                    dbg("v%d" % s, vf)
                if not is_sample:
                    slot = blk % 2
                    mi = 2 if blk == 0 else (0 if slot == 0 else 1)
                    cp("dve", vaug.ap[:, slot, :, 0:64], vf.ap[:, :].rearrange("p (g d) -> p g d", d=64),
                       [vf.tok()], [vaug.tok(slot, slot + 1, 2)])
                    cp("dve", kdup.ap, kf.ap[:, :].rearrange("p (g d) -> p g d", d=64).unsqueeze(2)
                       .to_broadcast([128, 4, 2, 64]), [kf.tok()], [kdup.tok()])
                    bk = alloc_ps(1)
                    pk = psb(bk).rearrange("p (k t) -> p k t", t=128)
                    for g in range(4):
                        tr(pk[:, g, :], kdup.ap[:, g, :, :].rearrange("p a d -> p (a d)"), ident_b.ap,
                           [kdup.tok(), ident_b.tok()], [pstok(bk)])
                    cp("act", kT.ap[0:64, :, 0, slot, :], pk[0:64, 0:4, :], [pstok(bk)], [kT.tok()])
                    cp("act", kT.ap[64:128, :, 1, slot, :], pk[64:128, 0:4, :], [pstok(bk)], [kT.tok()])
                    bqt = alloc_ps(1)
                    pq = psb(bqt).rearrange("p (k t) -> p k t", t=128)
                    for k in range(8):
                        tr(pq[:, k, :], q_r.ap[:, k * 128:(k + 1) * 128], ident_b.ap,
                           [q_r.tok(), ident_b.tok()], [pstok(bqt)])
                    cp("act", qT.ap, pq, [pstok(bqt)], [qT.tok()])
                    if blk == T // 128 - 1:
                        S.add("sp", lambda e: e.dma_start(out=O["kp"], in_=kf.ap), reads=[kf.tok()], dma=True)
                        S.add("sp", lambda e: e.dma_start(out=O["vp"], in_=vf.ap), reads=[vf.tok()], dma=True)
                    def att_stage1(g):
                        smg, ppg = sm[g % 2], pp[g % 2]
                        bs = alloc_ps(2)
                        for hh in range(4):
                            h = 4 * g + hh
                            qt_ = h // 2
                            mm(ps[:, bs + hh // 2, (hh % 2) * 256:(hh % 2) * 256 + 256],
                               qT.ap[:, qt_, :],
                               kT.ap[:, g, h % 2, :, :].rearrange("p a b -> p (a b)"), True, False,
                               [qT.tok(), kT.tok()], [pstok(bs, 2)])
                            mm(ps[:, bs + hh // 2, (hh % 2) * 256:(hh % 2) * 256 + 256],
                               ident_b.ap, masks_b.ap[:, mi, :], False, True,
                               [ident_b.tok(), masks_b.tok()], [pstok(bs, 2)])
                        sv = psf(bs, 2).rearrange("p (h k) -> p h k", k=256)
                        act(ppg.ap, sv, AF.Exp, [pstok(bs, 2)], [ppg.tok()], scale=0.125)

                    def att_stage2(g):
                        ppg, pTg = pp[g % 2], pT[g % 2]
                        bt_ = alloc_ps(1)
                        pt = psb(bt_).rearrange("p (k t) -> p k t", t=128)
                        for hh in range(4):
                            for c in range(2):
                                tr(pt[:, hh * 2 + c, :], ppg.ap[:, hh, c * 128:(c + 1) * 128], ident_b.ap,
                                   [ppg.tok(), ident_b.tok()], [pstok(bt_)])
                        cp("act", pTg.ap, pt, [pstok(bt_)], [pTg.tok()])
                        bo = alloc_ps(1)
                        ov = ps[:, bo, :].rearrange("p (h e) -> p h e", e=128)
                        for hh in range(4):
                            for c in range(2):
                                mm(ov[:, hh, 0:65], pTg.ap[:, hh * 2 + c, :], vaug.ap[:, c, g, 0:65], c == 0, c == 1,
                                   [pTg.tok(), vaug.tok()], [pstok(bo)])
                        tt("dve", den4.ap, ov[:, :, 64], esink.ap[:, 4 * g:4 * g + 4], ALU.add,
                           [pstok(bo), esink.tok()], [den4.tok()])
                        S.add("dve", lambda e, g=g: e.reciprocal(out=rden.ap[:, 4 * g:4 * g + 4], in_=den4.ap),
                              [den4.tok()], [rden.tok()])
                        tt("dve", ob.ap[:, 256 * g:256 * (g + 1)].rearrange("p (h d) -> p h d", d=64),
                           ov[:, :, 0:64], rden.ap[:, 4 * g:4 * g + 4].unsqueeze(2).to_broadcast([128, 4, 64]),
                           ALU.mult, [pstok(bo), rden.tok()], [ob.tok()])

                    for g in range(5):
                        if g < 4:
                            att_stage1(g)
                        if g >= 1:
                            att_stage2(g - 1)
                else:
                    attention_decode(R, qf, kf, vf, ob, junk, xg, tq)
            for s in range(nsb):
                ob = obs[s]
                if not is_sample and t == 0:
                    dbg("ob%d" % s, ob)
                bot = alloc_ps(1)
                po = psb(bot).rearrange("p (k t) -> p k t", t=128)
                for k in range(8):
                    tr(po[:, k, :R], ob.ap[:R, k * 128:(k + 1) * 128], ident_b.ap[:R, :R],
                       [ob.tok(), ident_b.tok()], [pstok(bot)])
                cp("act", oT.ap[:, :, s * 128:s * 128 + R], po[:, :, :R], [pstok(bot)], [oT.tok(s, s + 1, NS)])

            if stop_after == "B":
                raise StopBuild()
            w_done(3)
            S.phase = "D" + str(t) + ("s" if is_sample else "")
            sg = [carve([128, TT], F32) for _ in range(2)]
            for ct in range(2):
                wao = w_next()
                wga = w_next()
                for j in range(4):
                    cb = ct * 4 + j
                    boa = alloc_ps(1)
                    bga = alloc_ps(1)
                    proj_a(boa, oT, oT.tok(), NTOK, wao, j)
                    proj_a(bga, hT, hT.tok(), NTOK, wga, j)
                    sgb = sg[cb % 2]
                    act(sgb.ap[:, :NTOK], ps[:, bga, 0:NTOK], AF.Sigmoid, [pstok(bga)], [sgb.tok()])
                    tt("dve", mixT.ap[:, cb, :NTOK], ps[:, boa, 0:NTOK], sgb.ap[:, :NTOK], ALU.mult,
                       [pstok(boa), sgb.tok()], [mixT.tok(cb, cb + 1, 8)])
                w_done(2)

            if stop_after == "D":
                raise StopBuild()
            S.phase = "E" + str(t) + ("s" if is_sample else "")
            ar_reset()
            ssm_branch(t, is_sample, R, nsb, NTOK)

            if stop_after == "E":
                raise StopBuild()
            S.phase = "F" + str(t) + ("s" if is_sample else "")
            ar_reset()
            sgf = [carve([128, TT], F32) for _ in range(2)]
            tg = [carve([128, TT], F32) for _ in range(2)]
            for ct in range(2):
                wso0 = w_next()
                wso1 = w_next()
                wgs = w_next()
                for j in range(4):
                    cb = ct * 4 + j
                    bos = alloc_ps(1)
                    bgs = alloc_ps(1)
                    proj_a(bos, onT, onT.tok(), NTOK, wso0, j, k0=0, first=True, last=False)
                    proj_a(bos, onT, onT.tok(), NTOK, wso1, j, k0=8, first=False, last=True)
                    proj_a(bgs, hT, hT.tok(), NTOK, wgs, j)
                    sgb, tgb = sgf[cb % 2], tg[cb % 2]
                    act(sgb.ap[:, :NTOK], ps[:, bgs, 0:NTOK], AF.Sigmoid, [pstok(bgs)], [sgb.tok()])
                    tt("dve", tgb.ap[:, :NTOK], ps[:, bos, 0:NTOK], sgb.ap[:, :NTOK], ALU.mult,
                       [pstok(bos), sgb.tok()], [tgb.tok()])
                    tt("dve", mixT.ap[:, cb, :NTOK], mixT.ap[:, cb, :NTOK], tgb.ap[:, :NTOK], ALU.add,
                       [mixT.tok(cb, cb + 1, 8), tgb.tok()], [mixT.tok(cb, cb + 1, 8)])
                w_done(3)

            S.phase = "G" + str(t) + ("s" if is_sample else "")
            if stop_after == "F":
                raise StopBuild()
            xn2 = carve([128, D], BF16)
            junkg = carve([128, D], F32)
            wo0 = w_next()
            wo1 = w_next()
            bo2s = []
            for s in range(nsb):
                bo2 = alloc_ps(2)
                ps_state["reserved"].update([bo2, bo2 + 1])
                bo2s.append(bo2)
                proj_b(bo2, mixT, mixT.tok(), s, R, wo0, 512)
                proj_b(bo2 + 1, mixT, mixT.tok(), s, R, wo1, 512)
            for s in range(nsb):
                bo2 = bo2s[s]
                ps_state["reserved"].discard(bo2)
                ps_state["reserved"].discard(bo2 + 1)
                tt("dve", xt.ap[:R, s, :], xt.ap[:R, s, :], psf(bo2, 2)[:R, :], ALU.add,
                   [xt.tok(s, s + 1, NS), pstok(bo2, 2)], [xt.tok(s, s + 1, NS)])
                norm_to_hT(s, R, gT2, xn2, junkg)
            w_done(2)

            if stop_after == "G":
                raise StopBuild()
            S.phase = "H" + str(t) + ("s" if is_sample else "")
            aT = carve([128, 32, TT], BF16)
            rl = [carve([128, TT], F32) for _ in range(2)]
            yo = carve([128, NS, D], F32)
            for i in range(8):
                wu = w_next()
                for j in range(4):
                    cb = i * 4 + j
                    bu = alloc_ps(1)
                    proj_a(bu, hT, hT.tok(), NTOK, wu, j)
                    rlb = rl[cb % 2]
                    act(rlb.ap[:, :NTOK], ps[:, bu, 0:NTOK], AF.Relu, [pstok(bu)], [rlb.tok()])
                    tt("dve", aT.ap[:, cb, :NTOK], rlb.ap[:, :NTOK], rlb.ap[:, :NTOK], ALU.mult,
                       [rlb.tok()], [aT.tok(cb, cb + 1, 32)])
                w_done(1)

            if stop_after == "H":
                raise StopBuild()
            S.phase = "I" + str(t) + ("s" if is_sample else "")
            for ct in range(2):
                banks = []
                for s in range(nsb):
                    b = alloc_ps(1)
                    ps_state["reserved"].add(b)
                    banks.append(b)
                for kg in range(4):
                    wd = w_next()
                    for s in range(nsb):
                        proj_b(banks[s], aT, aT.tok(), s, R, wd, 512, kc=8, k0=kg * 8,
                               acc_first=(kg == 0), acc_last=(kg == 3))
                    w_done(1)
                for s in range(nsb):
                    tt("dve", yo.ap[:R, s, ct * 512:(ct + 1) * 512], xt.ap[:R, s, ct * 512:(ct + 1) * 512],
                       ps[:R, banks[s], :], ALU.add,
                       [xt.tok(s, s + 1, NS), pstok(banks[s])], [yo.tok(s, s + 1, NS)])
                    ps_state["reserved"].discard(banks[s])
            for s in range(nsb):
                if is_sample:
                    dst = O["ys"]
                else:
                    dst = O["yp"][t * TT + s * 128: t * TT + (s + 1) * 128, :]
                S.add("sp", lambda e, s=s, dst=dst: e.dma_start(out=dst, in_=yo.ap[:R, s, :]),
                      reads=[yo.tok(s, s + 1, NS)], dma=True)

        def ssm_branch(t, is_sample, R, nsb, NTOK):
            sz = carve([128, NS, 2048], BF16)
            dtv = carve([128, NS, 32], F32)
            xb = carve([128, 32], F32)
            ab = carve([128, 32], F32)
            for i in range(4):
                wz = w_next()
                for s in range(nsb):
                    bz = alloc_ps(1)
                    proj_b(bz, hT, hT.tok(s, s + 1, NS), s, R, wz, 512)
                    act(sz.ap[:R, s, i * 512:(i + 1) * 512], ps[:R, bz, :], AF.Silu, [pstok(bz)],
                        [sz.tok(s, s + 1, NS)])
                w_done(1)
            wdt = w_next()
            for s in range(nsb):
                bd = alloc_ps(1)
                proj_b(bd, hT, hT.tok(s, s + 1, NS), s, R, wdt, 32)
                tt("dve", xb.ap[:R, :], ps[:R, bd, 0:32], dtb.ap[:R, :], ALU.add, [pstok(bd), dtb.tok()], [xb.tok()])
                act(ab.ap[:R, :], xb.ap[:R, :], AF.Abs, [xb.tok()], [ab.tok()])
                act(ab.ap[:R, :], ab.ap[:R, :], AF.Exp, [ab.tok()], [ab.tok()], scale=-1.0)
                act(ab.ap[:R, :], ab.ap[:R, :], AF.Ln, [ab.tok()], [ab.tok()], bias=1.0)
                stt("dve", dtv.ap[:R, s, :], xb.ap[:R, :], 0.0, ab.ap[:R, :], ALU.max, ALU.add,
                    [xb.tok(), ab.tok()], [dtv.tok(s, s + 1, NS)])
            w_done(1)
            if is_sample:
                ssm_sample(R, sz, dtv)
                return
            xs_tok = carve([128, NS, 2048], BF16)
            BT = carve([128, 4, TT], BF16)
            CT = carve([128, 4, TT], BF16)
            xr = [carve([128, TT + 16], F32) for _ in range(3)]
            acc = [carve([128, TT], F32) for _ in range(3)]
            xc8 = carve([128, 8, TT], BF16)
            def conv_tail(c, accb):
                if c < 16:
                    act(xc8.ap[:, c % 8, :], accb.ap, AF.Silu, [accb.tok()], [xc8.tok(c % 8, c % 8 + 1, 8)])
                    if c % 8 == 7:
                        for s in range(NS):
                            bxt = alloc_ps(1)
                            pxt = psb(bxt).rearrange("p (k t) -> p k t", t=128)
                            for cc_ in range(8):
                                tr(pxt[:, cc_, :], xc8.ap[:, cc_, s * 128:(s + 1) * 128], ident_b.ap,
                                   [xc8.tok(), ident_b.tok()], [pstok(bxt)])
                            cp("act", xs_tok.ap[:, s, (c // 8) * 1024:(c // 8 + 1) * 1024],
                               psb(bxt)[:, 0:1024], [pstok(bxt)], [xs_tok.tok(s, s + 1, NS)])
                elif c < 20:
                    act(BT.ap[:, c - 16, :], accb.ap, AF.Silu, [accb.tok()], [BT.tok()])
                else:
                    act(CT.ap[:, c - 20, :], accb.ap, AF.Silu, [accb.tok()], [CT.tok()])

            pend = None
            for i in range(6):
                wx = w_next()
                for j in range(4):
                    c = 4 * i + j
                    bx = alloc_ps(1)
                    proj_a(bx, hT, hT.tok(), TT, wx, j)
                    xrb, accb = xr[c % 3], acc[c % 3]
                    cp("act", xrb.ap[:, 3:3 + TT], ps[:, bx, 0:TT], [pstok(bx)], [xrb.tok()])
                    act(accb.ap, ps[:, bx, 0:TT], AF.Identity, [pstok(bx), convw.tok(), convb.tok()], [accb.tok()],
                        scale=convw.ap[:, 3, c:c + 1], bias=convb.ap[:, c:c + 1])
                    if pend is not None:
                        conv_tail(*pend)
                    cp("dve", xrb.ap[:, 0:3], hal.ap[:, :, c], [hal.tok(c, c + 1, 24)], [xrb.tok()])
                    for i_t in (2, 1, 0):
                        stt("dve", accb.ap, xrb.ap[:, i_t:i_t + TT], convw.ap[:, i_t, c:c + 1], accb.ap,
                            ALU.mult, ALU.add, [xrb.tok(), accb.tok(), convw.tok()], [accb.tok()])
                    cp("dve", hal.ap[:, :, c], xrb.ap[:, TT:TT + 3], [xrb.tok()], [hal.tok(c, c + 1, 24)])
                    pend = (c, accb)
                w_done(1)
            conv_tail(*pend)
            S.phase = "E2_" + str(t)
            dAs = dA_p
            ecds = [carve([128, 64], F32) for _ in range(NS)]
            Btoks = [carve([128, 4, 128], BF16) for _ in range(NS)]
            cbms = [carve([128, 4, 128], BF16) for _ in range(NS)]
            Rg = Rg_p
            Eg = [carve([128, 8, 128], BF16) for _ in range(2)]
            WTg = [carve([128, 8, 128], BF16) for _ in range(2)]
            dtx = [carve([128, 8, 64], BF16) for _ in range(2)]
            tex = [carve([128, 8, 64], BF16) for _ in range(2)]
            t1 = [carve([128, 512], F32) for _ in range(2)]
            t2 = [carve([128, 512], F32) for _ in range(2)]
            yzs = [carve([128, 2048], F32) for _ in range(NS)]
            ss2gs = [ss2g, ss2g_alt]
            ss2s = [ss2, ss2_alt]
            onb = carve([128, 2048], BF16)
            junk2 = carve([128, 512], F32)
            for s in range(NS):
                sl = slice(s * 128, (s + 1) * 128)
                dA, ecd, Btok, cbm = dAs[s], ecds[s], Btoks[s], cbms[s]
                tt("dve", dA.ap, dtv.ap[:, s, :], A_bc.ap, ALU.mult, [dtv.tok(s, s + 1, NS), A_bc.tok()], [dA.tok()])
                bc_ = alloc_ps(1)
                mm(ps[:, bc_, 0:32], tri_r.ap, dA.ap, True, True, [tri_r.tok(), dA.tok()], [pstok(bc_)])
                mm(ps[:, bc_, 32:64], ones_r.ap, dA.ap, True, True, [ones_r.tok(), dA.tok()], [pstok(bc_)])
                act(ecd.ap, ps[:, bc_, 0:64], AF.Exp, [pstok(bc_)], [ecd.tok()])
                bb = alloc_ps(1)
                pbt = psb(bb).rearrange("p (k t) -> p k t", t=128)
                for g in range(4):
                    tr(pbt[:, g, :], BT.ap[:, g, sl], ident_b.ap, [BT.tok(), ident_b.tok()], [pstok(bb)])
                cp("act", Btok.ap, pbt[:, 0:4, :], [pstok(bb)], [Btok.tok()])
                bcb = alloc_ps(1)
                pcb = ps[:, bcb, :].rearrange("p (g l) -> p g l", l=128)
                for g in range(4):
                    mm(pcb[:, g, :], BT.ap[:, g, sl], CT.ap[:, g, sl], True, True, [BT.tok(), CT.tok()], [pstok(bcb)])
                tt("dve", cbm.ap, pcb, tri_f.ap.unsqueeze(1).to_broadcast([128, 4, 128]), ALU.mult,
                   [pstok(bcb), tri_f.tok()], [cbm.tok()])

            for s in range(NS):
                sl = slice(s * 128, (s + 1) * 128)
                dA, ecd, Btok, cbm = dAs[s], ecds[s], Btoks[s], cbms[s]
                yz, ss2g_c = yzs[s], ss2gs[s % 2]
                def ssd_stage1(g, s=s, sl=sl, dA=dA):
                    Rb, Eb = Rg[g % 2], Eg[g % 2]
                    hs = slice(8 * g, 8 * g + 8)
                    tt("dve", Rb.ap, tri_f.ap.unsqueeze(1).to_broadcast([128, 8, 128]),
                       dA.ap[:, hs].bitcast(F32).unsqueeze(2).to_broadcast([128, 8, 128]), ALU.mult,
                       [tri_f.tok(), dA.tok()], [Rb.tok()])
                    bd_ = alloc_ps(2)
                    for hf in range(2):
                        mm(ps[:, bd_ + hf, :], stri_r.ap, Rb.ap[:, 4 * hf:4 * hf + 4, :].rearrange("p a b -> p (a b)"),
                           True, True, [stri_r.tok(), Rb.tok()], [pstok(bd_, 2)])
                    act(Eb.ap.rearrange("p a b -> p (a b)"), psf(bd_, 2), AF.Exp, [pstok(bd_, 2)], [Eb.tok()])

                def ssd_stage2(g, s=s, sl=sl, ecd=ecd, Btok=Btok, cbm=cbm, yz=yz, ss2g=ss2g_c):
                    Rb, Eb, Wb, dxb, txb, t1b, t2b = Rg[g % 2], Eg[g % 2], WTg[g % 2], dtx[g % 2], tex[g % 2], t1[g % 2], t2[g % 2]
                    hs = slice(8 * g, 8 * g + 8)
                    gs_ = slice(512 * g, 512 * (g + 1))
                    tt("dve", Wb.ap, Eb.ap, cbm.ap[:, g, :].unsqueeze(1).to_broadcast([128, 8, 128]), ALU.mult,
                       [Eb.tok(), cbm.tok()], [Wb.tok()])
                    tt("dve", dxb.ap, xs_tok.ap[:, s, gs_].rearrange("p (h d) -> p h d", d=64),
                       dtv.ap[:, s, hs].unsqueeze(2).to_broadcast([128, 8, 64]), ALU.mult,
                       [xs_tok.tok(s, s + 1, NS), dtv.tok(s, s + 1, NS)], [dxb.tok()])
                    tt("dve", txb.ap, dxb.ap, Eb.ap[:, :, 127:128].to_broadcast([128, 8, 64]), ALU.mult,
                       [dxb.tok(), Eb.tok()], [txb.tok()])
                    by = alloc_ps(1)
                    for hh in range(8):
                        mm(ps[:, by, hh * 64:(hh + 1) * 64], Wb.ap[:, hh, :], dxb.ap[:, hh, :], True, True,
                           [Wb.tok(), dxb.tok()], [pstok(by)])
                    byi = alloc_ps(1)
                    mm(ps[:, byi, :], CT.ap[:, g, sl], hTb.ap[:, gs_], True, True,
                       [CT.tok(), hTb.tok(g, g + 1, 4)], [pstok(byi)])
                    tt("dve", t1b.ap.rearrange("p (h d) -> p h d", d=64),
                       ps[:, byi, :].rearrange("p (h d) -> p h d", d=64),
                       ecd.ap[:, hs].unsqueeze(2).to_broadcast([128, 8, 64]), ALU.mult,
                       [pstok(byi), ecd.tok()], [t1b.tok()])
                    tt("dve", t1b.ap, ps[:, by, :], t1b.ap, ALU.add, [pstok(by), t1b.tok()], [t1b.tok()])
                    tt("dve", t2b.ap.rearrange("p (h d) -> p h d", d=64),
                       xs_tok.ap[:, s, gs_].rearrange("p (h d) -> p h d", d=64),
                       dsk.ap[:, hs].unsqueeze(2).to_broadcast([128, 8, 64]), ALU.mult,
                       [xs_tok.tok(s, s + 1, NS), dsk.tok()], [t2b.tok()])
                    tt("dve", t1b.ap, t1b.ap, t2b.ap, ALU.add, [t1b.tok(), t2b.tok()], [t1b.tok()])
                    tt("dve", yz.ap[:, gs_], t1b.ap, sz.ap[:, s, gs_], ALU.mult,
                       [t1b.tok(), sz.tok(s, s + 1, NS)], [yz.tok(g, g + 1, 4)])
                    act(junk2.ap, yz.ap[:, gs_], AF.Square, [yz.tok(g, g + 1, 4)], [junk2.tok(), ss2g.tok()],
                        accum_out=ss2g.ap[:, g:g + 1])
                    bst = alloc_ps(1)
                    mm(ps[:, bst, :], Btok.ap[:, g, :], txb.ap.rearrange("p a b -> p (a b)"), True, True,
                       [Btok.tok(), txb.tok()], [pstok(bst)])
                    tt("dve", hst.ap[:, gs_].rearrange("p (h d) -> p h d", d=64),
                       hst.ap[:, gs_].rearrange("p (h d) -> p h d", d=64),
                       ecd.ap[:, 32 + 8 * g:32 + 8 * g + 8].unsqueeze(2).to_broadcast([128, 8, 64]), ALU.mult,
                       [hst.tok(g, g + 1, 4), ecd.tok()], [hst.tok(g, g + 1, 4)])
                    tt("dve", hst.ap[:, gs_], hst.ap[:, gs_], ps[:, bst, :], ALU.add,
                       [hst.tok(g, g + 1, 4), pstok(bst)], [hst.tok(g, g + 1, 4)])
                    cp("act", hTb.ap[:, gs_], hst.ap[:, gs_], [hst.tok(g, g + 1, 4)], [hTb.tok(g, g + 1, 4)])

                for g in range(5):
                    if g < 4:
                        ssd_stage1(g)
                    if g >= 1:
                        ssd_stage2(g - 1)
            for s in range(NS):
                gated_norm_tail(s, 128, yzs[s], onb, ss2gs[s % 2], ss2s[s % 2])
            if t == NTILE - 1:
                hout = carve([128, 16, 128], F32)
                for q4 in range(4):
                    bh = alloc_ps(4)
                    for k in range(4):
                        tk = q4 * 4 + k
                        tr(ps[:, bh + k, 0:128], hst.ap[:, tk * 128:(tk + 1) * 128], ident_f.ap,
                           [hst.tok(), ident_f.tok()], [pstok(bh, 4)])
                    cp("dve", hout.ap[:, q4 * 4:(q4 + 1) * 4, :], ps[:, bh:bh + 4, 0:128], [pstok(bh, 4)], [hout.tok()])
                S.add("sp", lambda e: e.dma_start(out=O["hp"].rearrange("(t r) n -> r t n", r=128), in_=hout.ap),
                      reads=[hout.tok()], dma=True)
                cst = carve([128, 128], F32)
                bcs = alloc_ps(1)
                tr(ps[0:72, bcs, 0:128], hal.ap.rearrange("p i c -> p (i c)"), ident_f.ap,
                   [hal.tok(), ident_f.tok()], [pstok(bcs)])
                cp("dve", cst.ap[0:72, :], ps[0:72, bcs, 0:128], [pstok(bcs)], [cst.tok()])
                S.add("sp", lambda e: e.dma_start(out=O["cp"].rearrange("i (c p) -> (i c) p", p=128),
                                                  in_=cst.ap[0:72, :]), reads=[cst.tok()], dma=True)

        def gated_norm_tail(s, R, yz, onb, ss2g=ss2g, ss2=ss2):
            S.add("dve", lambda e: e.tensor_reduce(out=ss2.ap[:R, :], in_=ss2g.ap[:R, :], axis=AX.X, op=ALU.add),
                  [ss2g.tok()], [ss2.tok()])
            rsqrt_mean(ss2, R, 1, 1.0 / 2048)
            act(onb.ap[:R, :], yz.ap[:R, :], AF.Copy, [yz.tok(), ss2.tok()], [onb.tok()], scale=ss2.ap[:R, 0:1])
            bn = alloc_ps(2)
            pn = psb(bn, 2).rearrange("p (k t) -> p k t", t=128)
            for k in range(16):
                tr(pn[:, k, :R], onb.ap[:R, k * 128:(k + 1) * 128], ident_b.ap[:R, :R],
                   [onb.tok(), ident_b.tok()], [pstok(bn, 2)])
            tt("dve", onT.ap[:, :, s * 128:s * 128 + R], pn[:, :, :R],
               gsT.ap.unsqueeze(2).to_broadcast([128, 16, R]), ALU.mult,
               [pstok(bn, 2), gsT.tok()], [onT.tok(s, s + 1, NS)])

        def attention_decode(R, qf, kf, vf, ob, junk, xg, tq):
            Kc = [carve([128, 256], F32) for _ in range(2)]
            Vc = [carve([128, 256], F32) for _ in range(2)]
            sel = [carve([128, 128], F32) for _ in range(2)]
            prod = carve([128, 1024], F32)
            pv = [carve([128, 1024], F32) for _ in range(2)]
            sTb = carve([128, 16], F32)
            pTb = [carve([128, 16], F32) for _ in range(2)]
            snew = carve([128, 16], F32)
            pnew = carve([128, 16], F32)
            den16 = carve([128, 16], F32)
            q4 = qf.ap[:R, :].rearrange("p (g h d) -> p g h d", g=4, h=4)
            kb4 = kf.ap[:R, :].rearrange("p (g d) -> p g d", d=64).unsqueeze(2).to_broadcast([R, 4, 4, 64])
            vb4 = vf.ap[:R, :].rearrange("p (g d) -> p g d", d=64).unsqueeze(2).to_broadcast([R, 4, 4, 64])
            tt("dve", xg.ap[:R, 0:1024].rearrange("p (g h d) -> p g h d", g=4, h=4), q4, kb4, ALU.mult,
               [qf.tok(), kf.tok()], [xg.tok()])
            S.add("dve", lambda e: e.tensor_reduce(out=snew.ap[:R, :],
                                                   in_=xg.ap[:R, 0:1024].rearrange("p (h d) -> p h d", d=64),
                                                   axis=AX.X, op=ALU.add), [xg.tok()], [snew.tok()])
            act(pnew.ap[:R, :], snew.ap[:R, :], AF.Exp, [snew.tok()], [pnew.tok()], scale=0.125)
            boacc = alloc_ps(2)
            ps_state["reserved"].update([boacc, boacc + 1])
            bdacc = alloc_ps(1)
            ps_state["reserved"].add(bdacc)
            S.add("sp", lambda e: e.dma_start(out=O["ks"][:, 0:127, :], in_=I["ck"][:, 1:128, :]), dma=True)
            S.add("sp", lambda e: e.dma_start(out=O["vs"][:, 0:127, :], in_=I["cv"][:, 1:128, :]), dma=True)
            S.add("sp", lambda e: e.dma_start(out=O["ks"][:, 127, :], in_=kf.ap[:R, :]), reads=[kf.tok()], dma=True)
            S.add("sp", lambda e: e.dma_start(out=O["vs"][:, 127, :], in_=vf.ap[:R, :]), reads=[vf.tok()], dma=True)
            for b in range(NSMP):
                Kb, Vb, selb, pvb, pTbb = Kc[b % 2], Vc[b % 2], sel[b % 2], pv[b % 2], pTb[b % 2]
                S.add("sp", lambda e, b=b, Kb=Kb: e.dma_start(out=Kb.ap[0:127, :], in_=I["ck"][b, 1:128, :]),
                      writes=[Kb.tok()], dma=True)
                S.add("sp", lambda e, b=b, Vb=Vb: e.dma_start(out=Vb.ap[0:127, :], in_=I["cv"][b, 1:128, :]),
                      writes=[Vb.tok()], dma=True)
                cp("dve", selb.ap[:R, :], ident_f.ap[:R, b:b + 1].to_broadcast([R, 128]), [ident_f.tok()], [selb.tok()])
                bqb = alloc_ps(2)
                for hf in range(2):
                    mm(ps[:, bqb + hf, :], selb.ap[:R, :], qf.ap[:R, hf * 512:(hf + 1) * 512], True, True,
                       [selb.tok(), qf.tok()], [pstok(bqb, 2)])
                tt("dve", prod.ap[0:127, :].rearrange("p (g h d) -> p g h d", g=4, h=4),
                   psf(bqb, 2)[0:127, :].rearrange("p (g h d) -> p g h d", g=4, h=4),
                   Kb.ap[0:127, :].rearrange("p (g d) -> p g d", d=64).unsqueeze(2).to_broadcast([127, 4, 4, 64]),
                   ALU.mult, [pstok(bqb, 2), Kb.tok()], [prod.tok()])
                S.add("dve", lambda e: e.tensor_reduce(out=sTb.ap[0:127, :],
                                                       in_=prod.ap[0:127, :].rearrange("p (h d) -> p h d", d=64),
                                                       axis=AX.X, op=ALU.add), [prod.tok()], [sTb.tok()])
                act(pTbb.ap[0:127, :], sTb.ap[0:127, :], AF.Exp, [sTb.tok()], [pTbb.tok()], scale=0.125)
                tt("dve", pvb.ap[0:127, :].rearrange("p (g h d) -> p g h d", g=4, h=4),
                   Vb.ap[0:127, :].rearrange("p (g d) -> p g d", d=64).unsqueeze(2).to_broadcast([127, 4, 4, 64]),
                   pTbb.ap[0:127, :].rearrange("p (g h) -> p g h", g=4).unsqueeze(3).to_broadcast([127, 4, 4, 64]),
                   ALU.mult, [Vb.tok(), pTbb.tok()], [pvb.tok()])
                for hf in range(2):
                    mm(ps[:R, boacc + hf, :], selT.ap[0:127, b, :], pvb.ap[0:127, hf * 512:(hf + 1) * 512],
                       b == 0, b == NSMP - 1, [selT.tok(), pvb.tok()], [pstok(boacc, 2)])
                mm(ps[:R, bdacc, 0:16], selT.ap[0:127, b, :], pTbb.ap[0:127, :], b == 0, b == NSMP - 1,
                   [selT.tok(), pTbb.tok()], [pstok(bdacc)])
            tt("dve", xg.ap[:R, 0:1024].rearrange("p (g h d) -> p g h d", g=4, h=4), vb4,
               pnew.ap[:R, :].rearrange("p (g h) -> p g h", g=4).unsqueeze(3).to_broadcast([R, 4, 4, 64]), ALU.mult,
               [vf.tok(), pnew.tok()], [xg.tok()])
            tt("dve", xg.ap[:R, 0:1024], xg.ap[:R, 0:1024], psf(boacc, 2)[:R, :], ALU.add,
               [xg.tok(), pstok(boacc, 2)], [xg.tok()])
            tt("dve", den16.ap[:R, :], ps[:R, bdacc, 0:16], pnew.ap[:R, :], ALU.add, [pstok(bdacc), pnew.tok()],
               [den16.tok()])
            tt("dve", den16.ap[:R, :], den16.ap[:R, :], esink.ap[:R, :], ALU.add, [den16.tok(), esink.tok()],
               [den16.tok()])
            S.add("dve", lambda e: e.reciprocal(out=den16.ap[:R, :], in_=den16.ap[:R, :]), [den16.tok()], [den16.tok()])
            tt("dve", ob.ap[:R, :].rearrange("p (h d) -> p h d", d=64),
               xg.ap[:R, 0:1024].rearrange("p (h d) -> p h d", d=64),
               den16.ap[:R, :].unsqueeze(2).to_broadcast([R, 16, 64]), ALU.mult, [xg.tok(), den16.tok()], [ob.tok()])
            for b_ in (boacc, boacc + 1, bdacc):
                ps_state["reserved"].discard(b_)

        def ssm_sample(R, sz, dtv):
            xbc_s = carve([128, 3072], F32)
            xc_s = carve([128, 3072], F32)
            mark = ar["o"]
            cwt = [carve([128, 4, 512], F32) for _ in range(2)]
            stc = [carve([128, 3, 512], F32) for _ in range(2)]
            cbt = [carve([128, 512], F32) for _ in range(2)]
            accs = [carve([128, 512], F32) for _ in range(2)]
            tmpc = [carve([128, 512], F32) for _ in range(2)]
            S.add("sp", lambda e: e.dma_start(out=O["cs"][:, 0:2, :], in_=I["sconv"][:, 1:3, :]), dma=True)
            for i in range(6):
                wx = w_next()
                cw, sc, cb_, ac, tm = cwt[i % 2], stc[i % 2], cbt[i % 2], accs[i % 2], tmpc[i % 2]
                cs_ = slice(i * 512, (i + 1) * 512)
                S.add("sp", lambda e, cw=cw, cs_=cs_: e.dma_start(
                    out=cw.ap[:R, :, :], in_=I["conv_w"][:, cs_].unsqueeze(0).to_broadcast([R, 4, 512])),
                    writes=[cw.tok()], dma=True)
                S.add("sp", lambda e, sc=sc, cs_=cs_: e.dma_start(out=sc.ap[:R, :, :], in_=I["sconv"][:, :, cs_]),
                      writes=[sc.tok()], dma=True)
                S.add("sp", lambda e, cb_=cb_, cs_=cs_: e.dma_start(
                    out=cb_.ap[:R, :], in_=I["conv_b"][0:1, cs_].to_broadcast([R, 512])), writes=[cb_.tok()], dma=True)
                bx = alloc_ps(1)
                proj_b(bx, hT, hT.tok(0, 1, NS), 0, R, wx, 512)
                w_done(1)
                cp("act", xbc_s.ap[:R, cs_], ps[:R, bx, :], [pstok(bx)], [xbc_s.tok()])
                tt("dve", ac.ap[:R, :], ps[:R, bx, :], cw.ap[:R, 3, :], ALU.mult, [pstok(bx), cw.tok()], [ac.tok()])
                tt("dve", ac.ap[:R, :], ac.ap[:R, :], cb_.ap[:R, :], ALU.add, [ac.tok(), cb_.tok()], [ac.tok()])
                for j in range(3):
                    tt("dve", tm.ap[:R, :], sc.ap[:R, j, :], cw.ap[:R, j, :], ALU.mult, [sc.tok(), cw.tok()], [tm.tok()])
                    tt("dve", ac.ap[:R, :], ac.ap[:R, :], tm.ap[:R, :], ALU.add, [ac.tok(), tm.tok()], [ac.tok()])
                act(xc_s.ap[:R, cs_], ac.ap[:R, :], AF.Silu, [ac.tok()], [xc_s.tok()])
            S.add("sp", lambda e: e.dma_start(out=O["cs"][:, 2, :], in_=xbc_s.ap[:R, :]), reads=[xbc_s.tok()], dma=True)
            ar["o"] = mark
            dtx_s = carve([128, 2048], F32)
            decx_s = carve([128, 2048], F32)
            dtxT = carve([128, 16, 16], F32)
            decT = carve([128, 16, 16], F32)
            yT_all = carve([128, 16, 16], F32)
            dA_s = carve([128, 32], F32)
            sel = [carve([128, 128], F32) for _ in range(2)]
            h0t = [carve([128, 16, 128], F32) for _ in range(2)]
            tmp = carve([128, 16, 128], F32)
            tt("dve", dA_s.ap[:R, :], dtv.ap[:R, 0, :], A_bc.ap[:R, :], ALU.mult, [dtv.tok(), A_bc.tok()], [dA_s.tok()])
            act(dA_s.ap[:R, :], dA_s.ap[:R, :], AF.Exp, [dA_s.tok()], [dA_s.tok()])
            tt("dve", dtx_s.ap[:R, :].rearrange("p (h d) -> p h d", d=64),
               xc_s.ap[:R, 0:2048].rearrange("p (h d) -> p h d", d=64),
               dtv.ap[:R, 0, :].unsqueeze(2).to_broadcast([R, 32, 64]), ALU.mult, [xc_s.tok(), dtv.tok()], [dtx_s.tok()])
            cp("dve", decx_s.ap[:R, :].rearrange("p (h d) -> p h d", d=64),
               dA_s.ap[:R, :].unsqueeze(2).to_broadcast([R, 32, 64]), [dA_s.tok()], [decx_s.tok()])
            for src_, dstT in ((dtx_s, dtxT), (decx_s, decT)):
                bt_ = alloc_ps(1)
                pvw = ps[:, bt_, 0:256].rearrange("p (t b) -> p t b", b=16)
                for tk in range(16):
                    tr(pvw[:, tk, :], src_.ap[:R, tk * 128:(tk + 1) * 128], ident_f.ap[:R, :R],
                       [src_.tok(), ident_f.tok()], [pstok(bt_)])
                cp("dve", dstT.ap, pvw, [pstok(bt_)], [dstT.tok()])
            for b in range(NSMP):
                hb, selb = h0t[b % 2], sel[b % 2]
                S.add("sp", lambda e, b=b, hb=hb: e.dma_start(
                    out=hb.ap, in_=I["sssm"][b].rearrange("(t r) n -> r t n", r=128)), writes=[hb.tok()], dma=True)
                cp("dve", selb.ap[:R, :], ident_f.ap[:R, b:b + 1].to_broadcast([R, 128]), [ident_f.tok()], [selb.tok()])
                bB = alloc_ps(1)
                bC = alloc_ps(1)
                mm(ps[:, bB, :], selb.ap[:R, :], xc_s.ap[:R, 2048:2560], True, True, [selb.tok(), xc_s.tok()], [pstok(bB)])
                mm(ps[:, bC, :], selb.ap[:R, :], xc_s.ap[:R, 2560:3072], True, True, [selb.tok(), xc_s.tok()], [pstok(bC)])
                tmp4 = tmp.ap.rearrange("p (g a) n -> p g a n", g=4)
                h4 = hb.ap.rearrange("p (g a) n -> p g a n", g=4)
                tt("dve", tmp4, ps[:, bB, :].rearrange("p (g n) -> p g n", g=4).unsqueeze(2).to_broadcast([128, 4, 4, 128]),
                   dtxT.ap[:, :, b:b + 1].rearrange("p (g a) o -> p g a o", g=4).to_broadcast([128, 4, 4, 128]),
                   ALU.mult, [pstok(bB), dtxT.tok()], [tmp.tok()])
                tt("dve", hb.ap, hb.ap, decT.ap[:, :, b:b + 1].to_broadcast([128, 16, 128]), ALU.mult,
                   [hb.tok(), decT.tok()], [hb.tok()])
                tt("dve", hb.ap, hb.ap, tmp.ap, ALU.add, [hb.tok(), tmp.tok()], [hb.tok()])
                S.add("sp", lambda e, b=b, hb=hb: e.dma_start(
                    out=O["hs"][b].rearrange("(t r) n -> r t n", r=128), in_=hb.ap), reads=[hb.tok()], dma=True)
                tt("dve", tmp4, h4, ps[:, bC, :].rearrange("p (g n) -> p g n", g=4).unsqueeze(2).to_broadcast([128, 4, 4, 128]),
                   ALU.mult, [hb.tok(), pstok(bC)], [tmp.tok()])
                S.add("dve", lambda e, b=b: e.tensor_reduce(out=yT_all.ap[:, :, b], in_=tmp.ap, axis=AX.X, op=ALU.add),
                      [tmp.tok()], [yT_all.tok()])
            by4 = alloc_ps(4)
            for tk in range(16):
                tr(ps[:R, by4 + tk // 4, (tk % 4) * 128:(tk % 4 + 1) * 128], yT_all.ap[:, tk, :], ident_f.ap,
                   [yT_all.tok(), ident_f.tok()], [pstok(by4, 4)])
            ar["o"] = mark
            yz = carve([128, 2048], F32)
            onb = carve([128, 2048], BF16)
            junk3 = carve([128, 2048], BF16)
            tt("dve", yz.ap[:R, :].rearrange("p (h d) -> p h d", d=64),
               xc_s.ap[:R, 0:2048].rearrange("p (h d) -> p h d", d=64),
               dsk.ap[:R, :].unsqueeze(2).to_broadcast([R, 32, 64]), ALU.mult, [xc_s.tok(), dsk.tok()], [yz.tok()])
            tt("dve", yz.ap[:R, :], yz.ap[:R, :], psf(by4, 4)[:R, :], ALU.add, [yz.tok(), pstok(by4, 4)], [yz.tok()])
            tt("dve", yz.ap[:R, :], yz.ap[:R, :], sz.ap[:R, 0, :], ALU.mult, [yz.tok(), sz.tok()], [yz.tok()])
            S.add("dve", lambda e: e.memset(ss2g.ap[:R, :], 0.0), [], [ss2g.tok()])
            act(junk3.ap[:R, :], yz.ap[:R, :], AF.Square, [yz.tok()], [junk3.tok(), ss2g.tok()],
                accum_out=ss2g.ap[:R, 0:1])
            gated_norm_tail(0, R, yz, onb)

        try:
            for t in range(ntile):
                do_tile(t, False)
            if do_sample:
                do_tile(0, True)
        except StopBuild:
            pass
        S.max_ops = None

        run_sched(nc, S)
    return nc


def make_in_maps(inputs):
    consts = host_consts()
    shared = {}
    for n in ("norm_mix", "q_norm", "k_norm", "attn_sinks", "conv_b", "dt_bias", "a_log", "d_skip", "ssm_norm",
              "norm_mlp"):
        shared[n] = np.ascontiguousarray(np.asarray(inputs[n], np.float32).reshape(IN_SHAPES[n]))
    for n in ("w_in", "conv_w", "w_attn_o", "w_ssm_o", "w_out", "w_up", "w_down"):
        shared[n] = np.ascontiguousarray(np.asarray(inputs[n], np.float32)[0])
    for n, v in consts.items():
        shared["c_" + n] = np.ascontiguousarray(v)
    maps = []
    for c in range(NCORES):
        m = dict(shared)
        sl = slice(c * NSMP, (c + 1) * NSMP)
        m["x"] = np.ascontiguousarray(np.asarray(inputs["x_prompt"], np.float32)[c])
        m["xs"] = np.ascontiguousarray(np.asarray(inputs["x_sample"], np.float32)[sl, 0, :])
        m["ck"] = np.ascontiguousarray(np.asarray(inputs["cache_k"], np.float32)[0, sl].reshape(NSMP, 128, 256))
        m["cv"] = np.ascontiguousarray(np.asarray(inputs["cache_v"], np.float32)[0, sl].reshape(NSMP, 128, 256))
        m["sconv"] = np.ascontiguousarray(np.asarray(inputs["state_conv"], np.float32)[0, sl])
        m["sssm"] = np.ascontiguousarray(np.asarray(inputs["state_ssm"], np.float32)[0, sl].reshape(NSMP, 2048, 128))
        maps.append(m)
    return maps


def assemble(results):
    g = lambda n: [np.asarray(r[n], np.float32) for r in results]
    yp = np.stack(g("yp"), 0)
    ys = np.concatenate(g("ys"), 0).reshape(NCORES * NSMP, 1, D)
    kp = np.stack(g("kp"), 0).reshape(1, NCORES, 128, 4, 64)
    vp = np.stack(g("vp"), 0).reshape(1, NCORES, 128, 4, 64)
    cpo = np.stack(g("cp"), 0).reshape(1, NCORES, 3, 3072)
    hp = np.stack(g("hp"), 0).reshape(1, NCORES, 32, 64, 128)
    ks = np.concatenate(g("ks"), 0).reshape(1, NCORES * NSMP, 128, 4, 64)
    vs = np.concatenate(g("vs"), 0).reshape(1, NCORES * NSMP, 128, 4, 64)
    cs = np.concatenate(g("cs"), 0).reshape(1, NCORES * NSMP, 3, 3072)
    hs = np.concatenate(g("hs"), 0).reshape(1, NCORES * NSMP, 32, 64, 128)
    return (yp, ys, kp, vp, cpo, hp, ks, vs, cs, hs)


_PROG = {}


def kernel(**inputs):
    if "nc" not in _PROG:
        _PROG["nc"] = build_program()
    nc = _PROG["nc"]
    maps = make_in_maps(inputs)
    res = run_bass_kernel_spmd(nc, maps, core_ids=list(range(NCORES)))
    return assemble(res.results)
```
